# Optimizing a Trainium2 kernel written in Bass

```python
import math
import jax, jax.numpy as jnp
from jax import lax
import numpy as np

D_MODEL = 1024
BATCH = 2
SEQ = 8192
DEPTH = 2

RET_WIDTH = D_MODEL // 2
RET_HEAD_DIM = 64
RET_HEADS = RET_WIDTH // RET_HEAD_DIM
DIFF_WIDTH = D_MODEL - RET_WIDTH
DIFF_HEAD_DIM = 64
DIFF_V_DIM = 2 * DIFF_HEAD_DIM
DIFF_HEADS = DIFF_WIDTH // DIFF_V_DIM
MIX_WIDTH = RET_WIDTH + DIFF_WIDTH
D_IN_PROJ = 4 * RET_WIDTH + 3 * DIFF_WIDTH
D_FF = -(-8 * D_MODEL // (3 * 256)) * 256
CHUNK = 128
Q_BLOCK = 128
EPS = 1e-6

kernel_name = "hybrid_retention_diffattn_encoder"


def rms_norm(x, g):
    xf = x.astype(jnp.float32)
    y = xf * lax.rsqrt(jnp.mean(xf * xf, axis=-1, keepdims=True) + EPS)
    return (y * g.astype(jnp.float32)).astype(x.dtype)


def alibi_slopes(n):
    return 2.0 ** (-8.0 * jnp.arange(1, n + 1, dtype=jnp.float32) / n)


def retention_bidir(q, k, v, lg_f, lg_b):
    B, S, H, D = q.shape
    N = S // CHUNK
    qc = q.astype(jnp.float32).reshape(B, N, CHUNK, H, D)
    kc = k.astype(jnp.float32).reshape(B, N, CHUNK, H, D)
    vc = v.astype(jnp.float32).reshape(B, N, CHUNK, H, D)
    pos = jnp.arange(CHUNK, dtype=jnp.float32)
    dist = pos[:, None] - pos[None, :]
    mask_f = jnp.where(dist >= 0, jnp.exp(lg_f[:, None, None] * jnp.maximum(dist, 0.0)), 0.0)
    mask_b = jnp.where(dist < 0, jnp.exp(lg_b[:, None, None] * jnp.maximum(-dist, 0.0)), 0.0)
    dmask = mask_f + mask_b
    scores = jnp.einsum('bnthd,bnshd->bnhts', qc, kc) * dmask[None, None]
    intra = jnp.einsum('bnhts,bnshd->bnthd', scores, vc)

    wk_f = jnp.exp(lg_f[None, :] * (CHUNK - 1 - pos)[:, None])
    wq_f = jnp.exp(lg_f[None, :] * (pos + 1)[:, None])
    kv_f = jnp.einsum('bnshd,bnshe->nbhde', kc * wk_f[None, None, :, :, None], vc)
    dc_f = jnp.exp(lg_f * CHUNK)[None, :, None, None]
    wk_b = jnp.exp(lg_b[None, :] * pos[:, None])
    wq_b = jnp.exp(lg_b[None, :] * (CHUNK - pos)[:, None])
    kv_b = jnp.einsum('bnshd,bnshe->nbhde', kc * wk_b[None, None, :, :, None], vc)
    dc_b = jnp.exp(lg_b * CHUNK)[None, :, None, None]

    zeros = jnp.zeros((B, H, D, D), jnp.float32)

    def fwd_step(R, kv):
        return dc_f * R + kv, R

    def bwd_step(R, kv):
        return dc_b * R + kv, R

    _, R_f = lax.scan(fwd_step, zeros, kv_f)
    _, R_b = lax.scan(bwd_step, zeros, kv_b, reverse=True)
    cross_f = jnp.einsum('bnthd,nbhde->bnthe', qc * wq_f[None, None, :, :, None], R_f)
    cross_b = jnp.einsum('bnthd,nbhde->bnthe', qc * wq_b[None, None, :, :, None], R_b)
    out = (intra + cross_f + cross_b).reshape(B, S, H, D)
    return out.astype(q.dtype)


def diff_attention(q, k, v, lam, slopes):
    B, S, H, _, Dh = q.shape
    NB = S // Q_BLOCK
    q_blocks = q.reshape(B, NB, Q_BLOCK, H, 2, Dh).transpose(1, 0, 2, 3, 4, 5)
    starts = jnp.arange(NB, dtype=jnp.int32) * Q_BLOCK
    kpos = jnp.arange(S, dtype=jnp.int32)
    vf = v.astype(jnp.float32)

    def block(args):
        qb, start = args
        s = jnp.einsum('bqhmd,bkhmd->bhmqk', qb, k).astype(jnp.float32)
        qpos = start + jnp.arange(Q_BLOCK, dtype=jnp.int32)
        dist = jnp.abs(qpos[:, None] - kpos[None, :]).astype(jnp.float32)
        s = s - slopes[None, :, None, None, None] * dist[None, None, None]
        p = jax.nn.softmax(s, axis=-1)
        a = p[:, :, 0] - lam * p[:, :, 1]
        return jnp.einsum('bhqk,bkhe->bqhe', a, vf)

    o = lax.map(block, (q_blocks, starts))
    return o.transpose(1, 0, 2, 3, 4).reshape(B, S, H, DIFF_V_DIM).astype(q.dtype)


def setup_inputs(seed: int = 0) -> dict:
    key = jax.random.key(seed)
    ks = jax.random.split(key, 20)
    f32 = jnp.float32
    nrm = lambda k, shape, scale: jax.random.normal(k, shape, f32) * scale
    heads = jnp.arange(RET_HEADS, dtype=f32)
    base = jnp.log(-jnp.log1p(-(2.0 ** (-5.0 - heads))))
    return {
        "x": jax.random.normal(ks[0], (BATCH, SEQ, D_MODEL), f32),
        "attn_norm_g": 1.0 + nrm(ks[1], (DEPTH, D_MODEL), 0.02),
        "w_in": nrm(ks[2], (DEPTH, D_MODEL, D_IN_PROJ), D_MODEL ** -0.5),
        "ret_decay_fwd": base[None, :] + nrm(ks[3], (DEPTH, RET_HEADS), 0.1),
        "ret_decay_bwd": base[None, :] + nrm(ks[4], (DEPTH, RET_HEADS), 0.1),
        "ret_norm_g": 1.0 + nrm(ks[5], (DEPTH, RET_HEAD_DIM), 0.02),
        "dq_norm_g": 1.0 + nrm(ks[6], (DEPTH, DIFF_HEAD_DIM), 0.02),
        "dk_norm_g": 1.0 + nrm(ks[7], (DEPTH, DIFF_HEAD_DIM), 0.02),
        "lambda_q1": nrm(ks[8], (DEPTH, DIFF_HEAD_DIM), 0.1),
        "lambda_k1": nrm(ks[9], (DEPTH, DIFF_HEAD_DIM), 0.1),
        "lambda_q2": nrm(ks[10], (DEPTH, DIFF_HEAD_DIM), 0.1),
        "lambda_k2": nrm(ks[11], (DEPTH, DIFF_HEAD_DIM), 0.1),
        "diff_norm_g": 1.0 + nrm(ks[12], (DEPTH, DIFF_V_DIM), 0.02),
        "w_out": nrm(ks[13], (DEPTH, MIX_WIDTH, D_MODEL), MIX_WIDTH ** -0.5),
        "ffn_norm_g": 1.0 + nrm(ks[14], (DEPTH, D_MODEL), 0.02),
        "w_gate": nrm(ks[15], (DEPTH, D_MODEL, D_FF), D_MODEL ** -0.5),
        "w_up": nrm(ks[16], (DEPTH, D_MODEL, D_FF), D_MODEL ** -0.5),
        "w_down": nrm(ks[17], (DEPTH, D_FF, D_MODEL), D_FF ** -0.5),
    }


def reference(x, attn_norm_g, w_in, ret_decay_fwd, ret_decay_bwd, ret_norm_g,
              dq_norm_g, dk_norm_g, lambda_q1, lambda_k1, lambda_q2, lambda_k2,
              diff_norm_g, w_out, ffn_norm_g, w_gate, w_up, w_down):
    B, S, _ = x.shape
    slopes = alibi_slopes(DIFF_HEADS)
    split_at = [RET_WIDTH, 2 * RET_WIDTH, 3 * RET_WIDTH, 4 * RET_WIDTH,
                4 * RET_WIDTH + DIFF_WIDTH, 4 * RET_WIDTH + 2 * DIFF_WIDTH]
    for l in range(DEPTH):
        lam_init = 0.8 - 0.6 * math.exp(-0.3 * l)
        h = rms_norm(x, attn_norm_g[l])
        proj = h @ w_in[l]
        rq, rk, rv, rg, dq, dk, dv = jnp.split(proj, split_at, axis=-1)

        shp = (B, S, RET_HEADS, RET_HEAD_DIM)
        lg_f = -jnp.exp(ret_decay_fwd[l].astype(jnp.float32))
        lg_b = -jnp.exp(ret_decay_bwd[l].astype(jnp.float32))
        ret = retention_bidir(rq.reshape(shp), rk.reshape(shp) * RET_HEAD_DIM ** -0.5,
                              rv.reshape(shp), lg_f, lg_b)
        ret = rms_norm(ret, ret_norm_g[l]) * jax.nn.silu(rg.reshape(shp))
        ret = ret.reshape(B, S, RET_WIDTH)

        qk_shp = (B, S, DIFF_HEADS, 2, DIFF_HEAD_DIM)
        dqn = rms_norm(dq.reshape(qk_shp), dq_norm_g[l]) * DIFF_HEAD_DIM ** -0.5
        dkn = rms_norm(dk.reshape(qk_shp), dk_norm_g[l])
        lam = (jnp.exp(jnp.sum(lambda_q1[l].astype(jnp.float32) * lambda_k1[l].astype(jnp.float32)))
               - jnp.exp(jnp.sum(lambda_q2[l].astype(jnp.float32) * lambda_k2[l].astype(jnp.float32)))
               + lam_init)
        da = diff_attention(dqn, dkn, dv.reshape(B, S, DIFF_HEADS, DIFF_V_DIM), lam, slopes)
        da = (rms_norm(da, diff_norm_g[l]) * (1.0 - lam_init)).reshape(B, S, DIFF_WIDTH)

        x = x + jnp.concatenate([ret, da], axis=-1) @ w_out[l]

        h = rms_norm(x, ffn_norm_g[l])
        x = x + (jax.nn.silu(h @ w_gate[l]) * (h @ w_up[l])) @ w_down[l]
    return x
```

```python
import math
from contextlib import ExitStack

import numpy as np
import ml_dtypes

import concourse.bass as bass
import concourse.mybir as mybir
from concourse.bass_utils import run_bass_kernel_spmd

F32 = mybir.dt.float32
BF16 = mybir.dt.bfloat16
AF = mybir.ActivationFunctionType
ALU = mybir.AluOpType
AX = mybir.AxisListType
NPBF = ml_dtypes.bfloat16

D = 1024
S = 8192
NCORE = 8
TL = 2048
NT = 16
DIN = 3584
DFF = 2816
NF = 22
EPS = 1e-6
SLOPES = [2.0 ** (-8.0 * (i + 1) / 4) for i in range(4)]
MASKV = -30000.0


class Res:
    __slots__ = ("w", "rd")

    def __init__(self):
        self.w = None
        self.rd = {}


class Sched:
    def __init__(self, nc, es, ndma=12):
        self.nc = nc
        self.eng = {"pe": nc.tensor, "act": nc.scalar, "dve": nc.vector, "pool": nc.gpsimd, "sp": nc.sync}
        self.sem = {k: es.enter_context(nc.semaphore("s_" + k)) for k in ("pe", "act", "dve", "pool")}
        self.cnt = {k: 0 for k in self.sem}
        self.seen = {e: {} for e in self.eng}
        self.dsem = {q: [es.enter_context(nc.semaphore(f"d_{q}{i}")) for i in range(ndma)] for q in ("sp", "pool")}
        self.dcnt = {q: [0] * ndma for q in self.dsem}
        self.drr = {q: 0 for q in self.dsem}
        self.ccsem = [es.enter_context(nc.semaphore(f"cc{i}")) for i in range(10)]
        self.ncc = 0

    def _need(self, eng, toks):
        best = {}
        for (s, v) in toks:
            if best.get(s, 0) < v:
                best[s] = v
        for s, v in best.items():
            if self.seen[eng].get(s, 0) >= v:
                continue
            self.eng[eng].wait_ge(s, v)
            self.seen[eng][s] = v

    def _deps(self, eng, reads, writes):
        toks = []
        for r in reads:
            if r.w is not None:
                toks.append(r.w)
        for w in writes:
            if w.w is not None:
                toks.append(w.w)
            toks.extend(w.rd.items())
        if eng == "pe":
            toks = [t for t in toks if t[0] is not self.sem["pe"]]
        return toks

    def _mark(self, tok, reads, writes):
        for r in reads:
            if r.rd.get(tok[0], 0) < tok[1]:
                r.rd[tok[0]] = tok[1]
        for w in writes:
            w.w = tok
            w.rd = {}

    def op(self, eng, reads, writes, fn):
        self._need(eng, self._deps(eng, reads, writes))
        ins = fn(self.eng[eng])
        self.cnt[eng] += 1
        ins.then_inc(self.sem[eng], 1)
        tok = (self.sem[eng], self.cnt[eng])
        self._mark(tok, reads, writes)
        return tok

    def dma(self, q, out, in_, reads, writes):
        toks = self._deps(q, reads, writes)
        k = self.drr[q]
        self.drr[q] = (k + 1) % len(self.dsem[q])
        if self.dcnt[q][k] > 0:
            toks.append((self.dsem[q][k], self.dcnt[q][k]))
        self._need(q, toks)
        ins = self.eng[q].dma_start(out=out, in_=in_)
        self.dcnt[q][k] += 16
        ins.then_inc(self.dsem[q][k], 16)
        tok = (self.dsem[q][k], self.dcnt[q][k])
        self._mark(tok, reads, writes)
        return tok

    def allgather(self, in_ap, out_ap, reads, writes):
        self._need("pool", self._deps("pool", reads, writes))
        s = self.ccsem[self.ncc]
        self.ncc += 1
        ins = self.nc.gpsimd.collective_compute(
            "AllGather", ALU.bypass, replica_groups=[[0, 1, 2, 3], [4, 5, 6, 7]], ins=[in_ap], outs=[out_ap])
        ins.then_inc(s, 1)
        tok = (s, 1)
        self._mark(tok, reads, writes)
        return tok

    def all_tokens(self):
        toks = [(self.sem[k], self.cnt[k]) for k in self.sem if self.cnt[k] > 0]
        for q in self.dsem:
            toks += [(s, c) for s, c in zip(self.dsem[q], self.dcnt[q]) if c > 0]
        return toks

    def barrier(self, engines=None):
        toks = self.all_tokens()
        for e in (engines or self.eng):
            self._need(e, toks)


def bc(ap, shape):
    return ap.unsqueeze(len(ap.shape)).to_broadcast(list(shape))


class _Stop(Exception):
    pass


class Builder:
    stopped = False

    def chk(self, tag):
        import os
        if os.environ.get("KSTOP", "") == tag:
            self.stopped = True
        return self.stopped

    def __init__(self, mode, dbg=False):
        self.mode = mode
        self.dbg = dbg
        self.nc = bass.Bass("TRN2", target_bir_lowering=False)
        self.T = {}

    def din(self, name, shape, dt=F32):
        t = self.nc.dram_tensor(name, list(shape), dt, kind="ExternalInput").ap()
        self.T[name] = t
        return t

    def dout(self, name, shape, dt=F32):
        t = self.nc.dram_tensor(name, list(shape), dt, kind="ExternalOutput").ap()
        self.T[name] = t
        return t

    def dscr(self, name, shape, dt=BF16):
        t = self.nc.dram_tensor(name, list(shape), dt).ap()
        self.T[name] = t
        return t

    def build(self):
        nc = self.nc
        mode = self.mode
        T = self.T
        self.din("x", [TL, D])
        self.din("w_in", [2, D, DIN]); self.din("w_out", [2, D, D])
        self.din("w_gate", [2, D, DFF]); self.din("w_up", [2, D, DFF]); self.din("w_down", [2, DFF, D])
        self.din("gcol", [128, 2, 2, 8])
        self.din("pcol", [128, 2, 4])
        self.din("decp", [128, 2, 2, 4])
        self.din("dech", [128, 2, 2, 8])
        self.din("lamv", [128, 2, 4, 64])
        self.din("c_ident", [128, 128], BF16)
        self.din("c_ones", [128, 128], BF16)
        self.din("c_blk", [128, 128], BF16)
        self.din("c_dm", [128, 4, 128])
        self.din("c_pos", [128, 2, 128])
        self.din("c_ppos", [128, 2])
        self.din("c_ex", [128, 8])
        self.din("c_qaug", [4, 32, TL], BF16)
        self.din("c_kaugc", [4, 32, S], BF16)
        self.din("c_bt", [128, 4, 4, 64])
        self.din("c_dt", [128, 2, 4, 512], BF16)
        self.din("c_mi", [128, 4, 128], BF16)
        layers = {"fused": [0, 1], "L1": [0], "L2": [0, 1], "L3": [1]}[mode]
        for l in layers:
            import os
            for nm in ("rqT", "rqfT", "rqbT", "rkT", "rgT", "dqT"):
                self.dscr(f"{nm}{l}", [512, TL])
            self.dscr(f"wgu{l}", [NF // 2, 2, 128, 8, 256])
            for nm in ("retT", "daT"):
                if os.environ.get("KDBG"):
                    self.dout(f"{nm}{l}", [512, TL], BF16)
                else:
                    self.dscr(f"{nm}{l}", [512, TL])
        def xdecl(l, loc_kind, all_kind):
            for hp in (0, 1):
                for (nm, shp, dt_) in ((f"kt_loc{l}_{hp}", [256, TL], BF16), (f"v_loc{l}_{hp}", [TL, 256], BF16)):
                    (self.dout if loc_kind == "out" else self.dscr)(nm, shp, dt_)
                if all_kind is not None:
                    for (nm, shp, dt_) in ((f"kt_all{l}_{hp}", [1024, TL], BF16), (f"v_all{l}_{hp}", [S, 256], BF16)):
                        (self.din if all_kind == "in" else self.dscr)(nm, shp, dt_)
            (self.dout if loc_kind == "out" else self.dscr)(f"st_loc{l}", [128, 512], F32)
            if all_kind is not None:
                (self.din if all_kind == "in" else self.dscr)(f"st_all{l}", [512, 512], F32)
        if mode == "fused":
            for l in (0, 1):
                for hp in (0, 1):
                    self.dscr(f"kt_loc{l}_{hp}", [256, TL], BF16); self.dscr(f"v_loc{l}_{hp}", [TL, 256], BF16)
                self.dscr(f"st_loc{l}", [128, 512], F32); self.dscr(f"st_all{l}", [512, 512], F32)
                for hp in (0, 1):
                    self.dscr(f"kt_all{l}_{hp}", [1024, TL], BF16); self.dscr(f"v_all{l}_{hp}", [S, 256], BF16)
                    self.dscr(f"ktbig{l}_{hp}", [2048, TL], BF16); self.dscr(f"vbig{l}_{hp}", [2 * S, 256], BF16)
                self.dscr(f"rel_kt{l}", [2, 768, TL], BF16); self.dscr(f"rel_v{l}", [2, 3 * TL, 256], BF16)
        elif mode == "L1":
            xdecl(0, "out", None)
        elif mode == "L2":
            xdecl(0, "scr", "in"); xdecl(1, "out", None)
        elif mode == "L3":
            xdecl(1, "scr", "in")
        if mode in ("fused", "L2", "L3"):
            self.dout("y", [TL, D])

        with ExitStack() as es:
            self.es = es
            sc = self.sc = Sched(nc, es)
            self.R = {}
            self.stA = ExitStack()

            def sb(name, shape, dt, stack=es):
                t = stack.enter_context(nc.sbuf_tensor(name, list(shape), dt))
                return t
            self.sb = sb
            xs = self.xs = sb("xs", [128, NT, D], F32)
            self.xs_r = [Res() for _ in range(NT)]
            cst = self.cst = {}
            for nm, shape, dt in (("gcol", [128, 2, 2, 8], F32), ("pcol", [128, 2, 4], F32), ("decp", [128, 2, 2, 4], F32),
                                  ("dech", [128, 2, 2, 8], F32), ("lamv", [128, 2, 4, 64], F32),
                                  ("c_ident", [128, 128], BF16), ("c_ones", [128, 128], BF16), ("c_blk", [128, 128], BF16),
                                  ("c_dm", [128, 4, 128], F32), ("c_pos", [128, 2, 128], F32), ("c_ppos", [128, 2], F32),
                                  ("c_ex", [128, 8], F32)):
                t = sb("k_" + nm, shape, dt)
                r = Res()
                sc.dma("sp", t[:], T[nm], [], [r])
                cst[nm] = (t, r)
            xr = T["x"].rearrange("(n p) d -> p n d", p=128)
            for i in range(NT):
                sc.dma("sp", xs[:, i, :], xr[:, i, :], [], [self.xs_r[i]])
            self.dmask = sb("dmask", [128, 8, 128], F32)
            self.wq2 = sb("wq2", [128, 2, 4, 128], F32)
            self.wk8 = sb("wk8", [128, 2, 8], F32)
            self.dccat = sb("dccat", [128, 8], F32)
            self.lgh = sb("lgh", [128, 2, 8], F32)
            self.lgp = sb("lgp", [128, 2, 4], F32)
            self.lgcat = sb("lgcat", [128, 8], F32)
            self.lam = sb("lam", [128, 4], F32)
            self.gk8 = sb("gk8", [128, 1], F32)
            self.gdf = sb("gdf", [128, 1], F32)
            self.r_der = Res()

            self.chk("consts")
            for l in layers:
                if self.stopped:
                    break
                do_attn = not (mode == "L1") and not (mode == "L2" and l == 1)
                self.derived(l)
                if self.chk("derived"):
                    break
                self.phase_A(l)
                if self.chk("A") or not do_attn:
                    break
                self.phase_R(l)
                if self.chk("R"):
                    break
                self.phase_B(l)
                if self.chk("B"):
                    break
                self.phase_C(l)
                if self.chk("C"):
                    break
            self.stA.close()
            if "y" in T:
                yr = T["y"].rearrange("(n p) d -> p n d", p=128)
                for i in range(NT):
                    sc.dma("sp", yr[:, i, :], xs[:, i, :], [self.xs_r[i]], [])
            sc.barrier()
        return nc

    def derived(self, l):
        sc, cst = self.sc, self.cst
        rd = self.r_der
        dech, r1 = cst["dech"]; decp, r2 = cst["decp"]; lamv, r3 = cst["lamv"]
        cdm, r4 = cst["c_dm"]; cpos, r5 = cst["c_pos"]; cpp, r6 = cst["c_ppos"]; pcol, r7 = cst["pcol"]
        lgh, lgp, lgcat = self.lgh, self.lgp, self.lgcat
        with ExitStack() as st:
            tmp = self.sb(f"dtmp{l}", [128, 2, 128], F32, st)
            tl = self.sb(f"dtl{l}", [128, 4, 64], F32, st)
            rt = Res()
            sc.op("act", [r1, r2], [rd], lambda e: (e.activation(out=lgh[:], in_=dech[:, l], func=AF.Exp),
                                                     e.activation(out=lgp[:], in_=decp[:, l], func=AF.Exp))[1])
            sc.op("dve", [rd], [rd], lambda e: (e.tensor_scalar(out=lgh[:], in0=lgh[:], scalar1=-1.0, scalar2=None, op0=ALU.mult),
                                                 e.tensor_scalar(out=lgp[:], in0=lgp[:], scalar1=-1.0, scalar2=None, op0=ALU.mult))[1])
            sc.op("dve", [rd], [rd], lambda e: (e.tensor_copy(out=lgcat[0:64, :], in_=lgh[0:64, 0, :]),
                                                 e.tensor_copy(out=lgcat[64:128, :], in_=lgh[64:128, 1, :]))[1])
            sc.op("act", [rd], [rd], lambda e: e.activation(out=self.dccat[:], in_=lgcat[:], func=AF.Exp, scale=128.0))
            def f_wk(e):
                e.tensor_scalar(out=self.wk8[:, 0, :], in0=lgh[:, 0, :], scalar1=cpp[:, 0:1], scalar2=None, op0=ALU.mult)
                return e.tensor_scalar(out=self.wk8[:, 1, :], in0=lgh[:, 1, :], scalar1=cpp[:, 1:2], scalar2=None, op0=ALU.mult)
            sc.op("dve", [rd, r6], [rd], f_wk)
            sc.op("act", [rd], [rd], lambda e: e.activation(out=self.wk8[:], in_=self.wk8[:], func=AF.Exp))
            sc.op("dve", [rd], [rd], lambda e: e.tensor_scalar(out=self.wk8[:], in0=self.wk8[:], scalar1=0.125, scalar2=None, op0=ALU.mult))
            def f_wq(e):
                ins = None
                for d in range(2):
                    for cc in range(4):
                        ins = e.activation(out=self.wq2[:, d, cc, :], in_=cpos[:, d, :], func=AF.Exp, scale=lgp[:, d, cc:cc + 1])
                return ins
            sc.op("act", [rd, r5], [rd], f_wq)
            for h in range(8):
                sc.op("act", [rd, r4], [rt], lambda e, h=h: (e.activation(out=tmp[:, 0, :], in_=cdm[:, 0, :], func=AF.Exp, scale=lgh[:, 0, h:h + 1]),
                                                              e.activation(out=tmp[:, 1, :], in_=cdm[:, 2, :], func=AF.Exp, scale=lgh[:, 1, h:h + 1]))[1])
                sc.op("dve", [rt, r4], [rt], lambda e: e.tensor_tensor(out=tmp[:, 0, :], in0=tmp[:, 0, :], in1=cdm[:, 1, :], op=ALU.mult))
                sc.op("dve", [rt, r4], [rt], lambda e: e.tensor_tensor(out=tmp[:, 1, :], in0=tmp[:, 1, :], in1=cdm[:, 3, :], op=ALU.mult))
                sc.op("dve", [rt], [rt, rd], lambda e, h=h: e.tensor_tensor(out=self.dmask[:, h, :], in0=tmp[:, 0, :], in1=tmp[:, 1, :], op=ALU.add))
            lam_init = 0.8 - 0.6 * math.exp(-0.3 * l)
            sc.op("dve", [r3], [rt], lambda e: e.tensor_tensor(out=tl[:, 0, :], in0=lamv[:, l, 0, :], in1=lamv[:, l, 1, :], op=ALU.mult))
            sc.op("dve", [r3, rt], [rt], lambda e: e.tensor_tensor(out=tl[:, 1, :], in0=lamv[:, l, 2, :], in1=lamv[:, l, 3, :], op=ALU.mult))
            sc.op("dve", [rt], [rd], lambda e: e.reduce_sum(out=self.lam[:, 2:4], in_=tl[:, 0:2, :], axis=AX.X))
            sc.op("act", [rd], [rd], lambda e: e.activation(out=self.lam[:, 2:4], in_=self.lam[:, 2:4], func=AF.Exp))
            sc.op("dve", [rd], [rd], lambda e: e.tensor_tensor(out=self.lam[:, 0:1], in0=self.lam[:, 2:3], in1=self.lam[:, 3:4], op=ALU.subtract))
            sc.op("dve", [rd], [rd], lambda e: e.tensor_scalar(out=self.lam[:, 0:1], in0=self.lam[:, 0:1], scalar1=lam_init, scalar2=None, op0=ALU.add))
            sc.op("dve", [rd], [rd], lambda e: e.tensor_scalar(out=self.lam[:, 1:2], in0=self.lam[:, 0:1], scalar1=-1.0, scalar2=None, op0=ALU.mult))
            sc.op("dve", [rd, r7], [rd], lambda e: e.tensor_scalar(out=self.gk8[:], in0=pcol[:, l, 2:3], scalar1=8.0, scalar2=None, op0=ALU.mult))
            sc.op("dve", [rd, r7], [rd], lambda e: e.tensor_scalar(out=self.gdf[:], in0=pcol[:, l, 3:4], scalar1=1.0 - lam_init, scalar2=None, op0=ALU.mult))
            sc.barrier(["act", "dve"])

    def norm_scratch(self, l, kind, tag, n, st):
        ss = self.sb(f"ss{l}{kind}{tag}", [128, n], F32, st)
        junk = self.sb(f"junk{l}{kind}{tag}", [128, D], BF16, st)
        xn = [self.sb(f"xn{l}{kind}{tag}_{k}", [128, D], BF16, st) for k in range(2)]
        return (ss, Res(), junk, Res(), xn, [Res(), Res()])

    def norm_T(self, l, kind, tiles, hT, hT_r, st, ps_t, ps_r, scratch=None):
        sc, cst, xs = self.sc, self.cst, self.xs
        gcol, rg = cst["gcol"]; ident, ri = cst["c_ident"]
        n = len(tiles)
        if scratch is None:
            scratch = self.norm_scratch(l, kind, tiles[0], n, st)
        ss, rs, junk, rj, xn, rxn = scratch
        for j, i in enumerate(tiles):
            sc.op("act", [self.xs_r[i]], [rj, rs], lambda e, i=i, j=j: e.activation(
                out=junk[:], in_=xs[:, i, :], func=AF.Square, accum_out=ss[:, j:j + 1]))
        sc.op("act", [rs], [rs], lambda e: e.activation(out=ss[:], in_=ss[:], func=AF.Ln, bias=EPS, scale=1.0 / D))
        sc.op("act", [rs], [rs], lambda e: e.activation(out=ss[:], in_=ss[:], func=AF.Exp, scale=-0.5))
        for j, i in enumerate(tiles):
            k = j % 2
            sc.op("dve", [self.xs_r[i], rs], [rxn[k]], lambda e, i=i, j=j, k=k: e.tensor_scalar(
                out=xn[k][:], in0=xs[:, i, :], scalar1=ss[:, j:j + 1], scalar2=None, op0=ALU.mult))
            def f_t(e, k=k):
                ins = None
                for c in range(8):
                    ins = e.transpose(out=ps_t[:, c, :], in_=xn[k][:, c * 128:(c + 1) * 128], identity=ident[:])
                return ins
            sc.op("pe", [rxn[k], ri], [ps_r], f_t)
            sc.op("dve", [ps_r, rg], [hT_r], lambda e, j=j: e.tensor_tensor(
                out=hT[:, :, j * 128:(j + 1) * 128], in0=ps_t[:], in1=bc(gcol[:, l, kind, :], [128, 8, 128]), op=ALU.mult))

    def load_w(self, wname, l, c0, ncols, dst, dst_r, krows=8):
        w = self.T[wname]
        src = w[l].rearrange("(kc p) n -> p kc n", p=128)[:, 0:krows, c0:c0 + ncols]
        self.sc.dma("pool", dst[:, 0:krows, 0:ncols], src, [], [dst_r])

    def phase_A(self, l):
        nc, sc, cst, T = self.nc, self.sc, self.cst, self.T
        sb = self.sb
        pcol, rpc = cst["pcol"]; blk, rblk = cst["c_blk"]
        sc.barrier()
        self.stA = ExitStack()
        stA = self.stA
        self.V_all = sb(f"V_all{l}", [128, NT, 512], BF16, stA)
        self.KV = sb(f"KV{l}", [128, NT, 512], F32, stA)
        self.rV = Res(); self.rKV = Res()
        V_all, KV = self.V_all, self.KV
        with ExitStack() as st:
            hT = sb(f"hT{l}", [128, 8, TL], BF16, st)
            rh = Res()
            wb = [sb(f"wb{l}_{k}", [128, 8, 512], BF16, st) for k in range(3)]
            rwb = [Res(), Res(), Res()]
            ps_t = st.enter_context(nc.psum_tensor(f"pst{l}", [128, 8, 128], BF16)); rpt = Res()
            pp = [st.enter_context(nc.psum_tensor(f"pp{l}_{k}", [128, 512], F32)) for k in range(3)]
            rpp = [Res() for _ in range(3)]
            pss = st.enter_context(nc.psum_tensor(f"pss{l}", [128, 512], F32)); rss = Res()
            pkv = st.enter_context(nc.psum_tensor(f"pkv{l}", [128, 512], F32)); rkv = Res()
            stg = [sb(f"stg{l}_{k}", [128, 512], BF16, st) for k in range(4)]
            rstg = [Res() for _ in range(4)]
            sqt = sb(f"sqt{l}", [128, 512], BF16, st); rsq = Res()
            rr = sb(f"rr{l}", [128, 512], F32, st); rrr = Res()
            kc_t = [sb(f"kcat{l}_{k}", [128, 8, 128], BF16, st) for k in range(2)]
            rkc = [Res(), Res()]
            self.norm_T(l, 0, list(range(NT)), hT, rh, st, ps_t, rpt)
            import os
            order = [2, 1, 5, 6, 0, 3, 4][:int(os.environ.get("KNB", "7"))]
            if self.chk("norm"):
                order = []
            ppi = [0]; sgi = [0]

            def next_pp():
                k = ppi[0]; ppi[0] = (k + 1) % 3
                return pp[k], rpp[k]

            def next_stg():
                k = sgi[0]; sgi[0] = (k + 1) % 4
                return stg[k], rstg[k]

            def feat_mm(w, rw, cc, tg, p, rp):
                def f(e):
                    ins = None
                    for kc in range(8):
                        ins = e.matmul(p[:], lhsT=w[:, kc, cc * 128:(cc + 1) * 128], rhs=hT[:, kc, tg * 512:(tg + 1) * 512],
                                       start=(kc == 0), stop=(kc == 7))
                    return ins
                sc.op("pe", [rw, rh], [rp], f)

            def tok_mm(w, rw, i, p, rp):
                def f(e):
                    ins = None
                    for kc in range(8):
                        ins = e.matmul(p[:], lhsT=hT[:, kc, i * 128:(i + 1) * 128], rhs=w[:, kc, :], start=(kc == 0), stop=(kc == 7))
                    return ins
                sc.op("pe", [rw, rh], [rp], f)

            def store(dname, rows, cols, s, rs_):
                r = Res()
                self.R.setdefault(dname, []).append(r)
                sc.dma("sp", T[dname][rows[0]:rows[1], cols[0]:cols[1]], s, [rs_], [r])

            for bi0 in range(min(2, len(order))):
                self.load_w("w_in", l, order[bi0] * 512, 512, wb[bi0], rwb[bi0])
            for bi, b in enumerate(order):
                w, rw = wb[bi % 3], rwb[bi % 3]
                if bi + 2 < len(order) and not (self.mode == "fused" and bi + 2 == 6):
                    self.load_w("w_in", l, order[bi + 2] * 512, 512, wb[(bi + 2) % 3], rwb[(bi + 2) % 3])
                if b == 2:
                    for i in range(NT):
                        p, rp = next_pp()
                        tok_mm(w, rw, i, p, rp)
                        sc.op("act", [rp], [self.rV], lambda e, i=i, p=p: e.copy(out=V_all[:, i, :], in_=p[:]))
                elif b == 6:
                    for i in range(NT):
                        p, rp = next_pp()
                        tok_mm(w, rw, i, p, rp)
                        s, rs_ = next_stg()
                        sc.op("act", [rp], [rs_], lambda e, p=p, s=s: e.copy(out=s[:], in_=p[:]))
                        store(f"v_loc{l}_0", (i * 128, (i + 1) * 128), (0, 256), s[:, 0:256], rs_)
                        store(f"v_loc{l}_1", (i * 128, (i + 1) * 128), (0, 256), s[:, 256:512], rs_)
                    if self.mode == "fused":
                        self.load_w("w_in", l, order[6] * 512, 512, wb[6 % 3], rwb[6 % 3])
                        self.gather_kv(l, 0)
                        self.dup_kv(l, 0)
                elif b == 1:
                    wk8 = self.wk8
                    for i in range(NT):
                        p, rp = next_pp()
                        tok_mm(w, rw, i, p, rp)
                        k = i % 2
                        def f_kc(e, p=p, k=k):
                            pv = p[:].rearrange("p (h d) -> p h d", h=8)
                            e.tensor_tensor(out=kc_t[k][:, :, 0:64], in0=pv, in1=bc(wk8[:, 0, :], [128, 8, 64]), op=ALU.mult)
                            return e.tensor_tensor(out=kc_t[k][:, :, 64:128], in0=pv, in1=bc(wk8[:, 1, :], [128, 8, 64]), op=ALU.mult)
                        sc.op("dve", [rp, self.r_der], [rkc[k]], f_kc)
                        def f_kv(e, i=i, k=k):
                            ins = None
                            for h in range(8):
                                ins = e.matmul(pkv[:, h * 64:(h + 1) * 64], lhsT=kc_t[k][:, h, :], rhs=V_all[:, i, h * 64:(h + 1) * 64],
                                               start=True, stop=True)
                            return ins
                        sc.op("pe", [rkc[k], self.rV], [rkv], f_kv)
                        sc.op("act", [rkv], [self.rKV], lambda e, i=i: e.copy(out=KV[:, i, :], in_=pkv[:]))
                    for cc in range(4):
                        for tg in range(4):
                            p, rp = next_pp()
                            feat_mm(w, rw, cc, tg, p, rp)
                            s, rs_ = next_stg()
                            sc.op("act", [rp], [rs_], lambda e, p=p, s=s: e.mul(out=s[:], in_=p[:], mul=0.125))
                            store(f"rkT{l}", (cc * 128, (cc + 1) * 128), (tg * 512, (tg + 1) * 512), s[:], rs_)
                    self.scan_states(l)
                elif b == 0:
                    for cc in range(4):
                        for tg in range(4):
                            p, rp = next_pp()
                            feat_mm(w, rw, cc, tg, p, rp)
                            s, rs_ = next_stg()
                            sc.op("act", [rp], [rs_], lambda e, p=p, s=s: e.copy(out=s[:], in_=p[:]))
                            store(f"rqT{l}", (cc * 128, (cc + 1) * 128), (tg * 512, (tg + 1) * 512), s[:], rs_)
                            rs_raw = rs_
                            for d, nm in ((0, "rqfT"), (1, "rqbT")):
                                s, rs_ = next_stg()
                                sc.op("dve", [rp, rs_raw, self.r_der], [rs_], lambda e, p=p, s=s, d=d, cc=cc: e.tensor_tensor(
                                    out=s[:].rearrange("p (n t) -> p n t", n=4), in0=p[:].rearrange("p (n t) -> p n t", n=4),
                                    in1=self.wq2[:, d, cc, :].unsqueeze(1).to_broadcast([128, 4, 128]), op=ALU.mult))
                                store(f"{nm}{l}", (cc * 128, (cc + 1) * 128), (tg * 512, (tg + 1) * 512), s[:], rs_)
                elif b == 3:
                    for cc in range(4):
                        for tg in range(4):
                            p, rp = next_pp()
                            feat_mm(w, rw, cc, tg, p, rp)
                            s, rs_ = next_stg()
                            sc.op("act", [rp], [rs_], lambda e, p=p, s=s: e.activation(out=s[:], in_=p[:], func=AF.Silu))
                            store(f"rgT{l}", (cc * 128, (cc + 1) * 128), (tg * 512, (tg + 1) * 512), s[:], rs_)
                else:
                    dname = f"dqT{l}" if b == 4 else None
                    gsc = pcol[:, l, 1:2] if b == 4 else self.gk8[:, 0:1]
                    for cc in range(4):
                        for tg in range(4):
                            p, rp = next_pp()
                            feat_mm(w, rw, cc, tg, p, rp)
                            sc.op("act", [rp], [rsq], lambda e, p=p: e.activation(out=sqt[:], in_=p[:], func=AF.Square))
                            sc.op("pe", [rsq, rblk], [rss], lambda e: e.matmul(pss[:], lhsT=blk[:], rhs=sqt[:], start=True, stop=True))
                            sc.op("act", [rss], [rrr], lambda e: e.activation(out=rr[:], in_=pss[:], func=AF.Ln, bias=64.0 * EPS, scale=1.0))
                            sc.op("act", [rrr], [rrr], lambda e: e.activation(out=rr[:], in_=rr[:], func=AF.Exp, scale=-0.5))
                            s, rs_ = next_stg()
                            sc.op("dve", [rp, rrr, rpc, self.r_der], [rs_], lambda e, p=p, s=s, gsc=gsc: e.scalar_tensor_tensor(
                                out=s[:], in0=p[:], scalar=gsc, in1=rr[:], op0=ALU.mult, op1=ALU.mult))
                            if b == 4:
                                store(dname, (cc * 128, (cc + 1) * 128), (tg * 512, (tg + 1) * 512), s[:], rs_)
                            else:
                                store(f"kt_loc{l}_{cc // 2}", ((cc % 2) * 128, (cc % 2) * 128 + 128), (tg * 512, (tg + 1) * 512), s[:], rs_)
            sc.barrier()
        if self.stopped:
            return
        if self.mode == "fused":
            pass
        else:
            for nm in ("kt_all", "v_all"):
                for hp in (0, 1):
                    self.R.setdefault(f"{nm}{l}_{hp}", [])
            self.R.setdefault(f"st_all{l}", [])

    def gather_kv(self, l, hp):
        T, sc = self.T, self.sc
        for nm in ("kt", "v"):
            ra = Res()
            self.R[f"{nm}_all{l}_{hp}"] = [ra]
            sc.allgather(T[f"{nm}_loc{l}_{hp}"], T[f"{nm}_all{l}_{hp}"], self.R[f"{nm}_loc{l}_{hp}"], [ra])

    def dup_kv(self, l, hp):
        T, sc = self.T, self.sc
        if not hasattr(self, "_dup"):
            self._dup = {}
        for nm, rows in (("kt", 256), ("v", TL)):
            big = T[f"{nm}big{l}_{hp}"]
            al = T[f"{nm}_all{l}_{hp}"]
            rd = [Res(), Res()]
            sc.dma("pool", big[0:4 * rows, :], al, self.R[f"{nm}_all{l}_{hp}"], [rd[0]])
            sc.dma("pool", big[4 * rows:8 * rows, :], al, self.R[f"{nm}_all{l}_{hp}"], [rd[1]])
            self._dup[(l, nm, hp)] = rd

    def relayout_kv(self, l, hp):
        T, sc, nc = self.T, self.sc, self.nc
        pid = nc.sync.partition_id()
        nxt = (pid % 4) + 1
        for nm, rows in (("kt", 256), ("v", TL)):
            big = T[f"{nm}big{l}_{hp}"]
            r1 = Res()
            sc.dma("sp", T[f"rel_{nm}{l}"][hp], big[bass.ds(nxt * rows, 3 * rows), :], self._dup[(l, nm, hp)], [r1])
            self.R[f"rel_{nm}{l}_{hp}"] = [r1]

    def scan_states(self, l):
        sc, T = self.sc, self.T
        KV, rKV = self.KV, self.rKV
        dcb0 = bc(self.dccat[0:64, :], [64, 8, 64])
        dcb1 = bc(self.dccat[64:128, :], [64, 8, 64])
        with ExitStack() as st:
            tmp = self.sb(f"sctmp{l}", [128, 512], F32, st)
            rt = Res()

            def v3(ap):
                return ap.rearrange("p (h e) -> p h e", h=8)

            def f(e):
                ins = None
                for n in range(1, NT):
                    e.tensor_tensor(out=v3(tmp[0:64, :]), in0=v3(KV[0:64, n - 1, :]), in1=dcb0, op=ALU.mult)
                    e.tensor_tensor(out=KV[0:64, n, :], in0=KV[0:64, n, :], in1=tmp[0:64, :], op=ALU.add)
                    m = NT - 1 - n
                    e.tensor_tensor(out=v3(tmp[64:128, :]), in0=v3(KV[64:128, m + 1, :]), in1=dcb1, op=ALU.mult)
                    ins = e.tensor_tensor(out=KV[64:128, m, :], in0=KV[64:128, m, :], in1=tmp[64:128, :], op=ALU.add)
                return ins
            sc.op("dve", [rKV, self.r_der], [rKV, rt], f)
            rs = [Res(), Res()]
            self.R[f"st_loc{l}"] = rs
            sc.dma("sp", T[f"st_loc{l}"][0:64, :], KV[0:64, NT - 1, :], [rKV], [rs[0]])
            sc.dma("sp", T[f"st_loc{l}"][64:128, :], KV[64:128, 0, :], [rKV], [rs[1]])
            if self.mode == "fused":
                ra = Res()
                self.R[f"st_all{l}"] = [ra]
                sc.allgather(T[f"st_loc{l}"], T[f"st_all{l}"], rs, [ra])
            sc.barrier(["dve"])

    def phase_R(self, l):
        nc, sc, cst, T, sb, R = self.nc, self.sc, self.cst, self.T, self.sb, self.R
        KV, V_all = self.KV, self.V_all
        cex, rex = cst["c_ex"]; pcol, rpc = cst["pcol"]; blk, rblk = cst["c_blk"]
        with ExitStack() as st:
            Rbf = sb(f"Rbf{l}", [128, NT, 512], BF16, st); rRb = Res()
            stin = sb(f"stin{l}", [128, 4, 512], F32, st); rsi = Res()
            cf = sb(f"cf{l}", [128, 4, 8], F32, st); rcf = Res()
            C = sb(f"Cc{l}", [128, 512], F32, st); rC = Res()
            tmp = sb(f"Rtmp{l}", [128, 512], F32, st)
            sc.dma("sp", stin[:], T[f"st_all{l}"].rearrange("(r p) e -> p r e", p=128), R[f"st_all{l}"], [rsi])
            def f_c(e):
                ins = None
                for p in range(4):
                    ins = e.activation(out=cf[:, p, :], in_=self.lgcat[:], func=AF.Exp, scale=cex[:, p:p + 1])
                return ins
            sc.op("act", [self.r_der, rex], [rcf], f_c)

            def v3(ap):
                return ap.rearrange("p (h e) -> p h e", h=8)

            def f_in(e):
                for p in range(4):
                    e.tensor_scalar(out=cf[:, p, :], in0=cf[:, p, :], scalar1=cex[:, 4 + p:5 + p], scalar2=None, op0=ALU.mult)
                e.tensor_tensor(out=v3(C[:]), in0=v3(stin[:, 0, :]), in1=bc(cf[:, 0, :], [128, 8, 64]), op=ALU.mult)
                ins = None
                for p in range(1, 4):
                    e.tensor_tensor(out=v3(tmp[:]), in0=v3(stin[:, p, :]), in1=bc(cf[:, p, :], [128, 8, 64]), op=ALU.mult)
                    ins = e.tensor_tensor(out=C[:], in0=C[:], in1=tmp[:], op=ALU.add)
                return ins
            sc.op("dve", [rcf, rsi, rex], [rC, rcf], f_in)
            dcb0 = bc(self.dccat[0:64, :], [64, 8, 64])
            dcb1 = bc(self.dccat[64:128, :], [64, 8, 64])

            def f_rb(e):
                e.tensor_copy(out=Rbf[0:64, 0, :], in_=C[0:64, :])
                for n in range(1, NT):
                    e.tensor_tensor(out=v3(C[0:64, :]), in0=v3(C[0:64, :]), in1=dcb0, op=ALU.mult)
                    e.tensor_tensor(out=Rbf[0:64, n, :], in0=KV[0:64, n - 1, :], in1=C[0:64, :], op=ALU.add)
                ins = e.tensor_copy(out=Rbf[64:128, NT - 1, :], in_=C[64:128, :])
                for n in range(NT - 2, -1, -1):
                    e.tensor_tensor(out=v3(C[64:128, :]), in0=v3(C[64:128, :]), in1=dcb1, op=ALU.mult)
                    ins = e.tensor_tensor(out=Rbf[64:128, n, :], in0=KV[64:128, n + 1, :], in1=C[64:128, :], op=ALU.add)
                return ins
            sc.op("dve", [rC, self.rKV, self.r_der], [rC, rRb], f_rb)
            NB = 2
            QR = [sb(f"QR{l}_{k}", [64, TL], BF16, st) for k in range(NB)]
            KT = [sb(f"KTr{l}_{k}", [64, TL], BF16, st) for k in range(NB)]
            QC = [sb(f"QC{l}_{k}", [128, TL], BF16, st) for k in range(NB)]
            GG = [sb(f"GG{l}_{k}", [64, TL], BF16, st) for k in range(NB)]
            rin = [Res() for _ in range(NB)]
            ps_s = [st.enter_context(nc.psum_tensor(f"pss_r{l}_{k}", [128, 4, 128], F32)) for k in range(2)]
            rps = [Res(), Res()]
            ps_o = [st.enter_context(nc.psum_tensor(f"pso_r{l}_{k}", [64, 512], F32)) for k in range(2)]
            rpo = [Res(), Res()]
            ps_n = [st.enter_context(nc.psum_tensor(f"psn_r{l}_{k}", [64, 512], F32)) for k in range(2)]
            rpn = [Res(), Res()]
            sm = [sb(f"sm{l}_{k}", [128, 4, 128], BF16, st) for k in range(2)]
            rsm = [Res(), Res()]
            osb = [sb(f"osb{l}_{k}", [64, 512], F32, st) for k in range(2)]
            ros = [Res(), Res()]
            sq = [sb(f"sqr{l}_{k}", [64, 512], BF16, st) for k in range(2)]
            rsq = [Res(), Res()]
            rr = [sb(f"rrr{l}_{k}", [64, 512], F32, st) for k in range(2)]
            rrr = [Res(), Res()]
            ofin = [sb(f"ofin{l}_{k}", [64, 512], BF16, st) for k in range(2)]
            rof = [Res(), Res()]
            items = [(h, tg) for h in range(8) for tg in range(4)]

            def stage1(it):
                h, tg = items[it]
                k = h % NB
                j = it % 2
                rows = (h * 64, (h + 1) * 64)
                def loads(hh):
                    kk = hh % NB
                    rw = (hh * 64, (hh + 1) * 64)
                    sc.dma("sp", QR[kk][:], T[f"rqT{l}"][rw[0]:rw[1], :], R[f"rqT{l}"], [rin[kk]])
                    sc.dma("sp", KT[kk][:], T[f"rkT{l}"][rw[0]:rw[1], :], R[f"rkT{l}"], [rin[kk]])
                    sc.dma("sp", QC[kk][0:64, :], T[f"rqfT{l}"][rw[0]:rw[1], :], R[f"rqfT{l}"], [rin[kk]])
                    sc.dma("sp", QC[kk][64:128, :], T[f"rqbT{l}"][rw[0]:rw[1], :], R[f"rqbT{l}"], [rin[kk]])
                    sc.dma("sp", GG[kk][:], T[f"rgT{l}"][rw[0]:rw[1], :], R[f"rgT{l}"], [rin[kk]])
                if it == 0:
                    loads(0)
                if tg == 1 and h + 1 < 8:
                    loads(h + 1)

                def f_s(e):
                    ins = None
                    for c in range(4):
                        n = tg * 4 + c
                        ins = e.matmul(ps_s[j][:, c, :], lhsT=KT[k][:, n * 128:(n + 1) * 128], rhs=QR[k][:, n * 128:(n + 1) * 128],
                                       start=True, stop=True)
                    return ins
                sc.op("pe", [rin[k]], [rps[j]], f_s)
                sc.op("dve", [rps[j], self.r_der], [rsm[j]], lambda e: e.tensor_tensor(
                    out=sm[j][:], in0=ps_s[j][:], in1=self.dmask[:, h, :].unsqueeze(1).to_broadcast([128, 4, 128]), op=ALU.mult))

                def f_o(e):
                    ins = None
                    for c in range(4):
                        n = tg * 4 + c
                        e.matmul(ps_o[j][:, c * 128:(c + 1) * 128], lhsT=V_all[:, n, h * 64:(h + 1) * 64], rhs=sm[j][:, c, :],
                                 start=True, stop=False)
                        ins = e.matmul(ps_o[j][:, c * 128:(c + 1) * 128], lhsT=Rbf[:, n, h * 64:(h + 1) * 64],
                                       rhs=QC[k][:, n * 128:(n + 1) * 128], start=False, stop=True)
                    return ins
                sc.op("pe", [rsm[j], self.rV, rRb, rin[k]], [rpo[j]], f_o)
                sc.op("act", [rpo[j]], [rsq[j]], lambda e: e.activation(out=sq[j][:], in_=ps_o[j][:], func=AF.Square))
                sc.op("pe", [rsq[j], rblk], [rpn[j]], lambda e: e.matmul(ps_n[j][:], lhsT=blk[0:64, 0:64], rhs=sq[j][:], start=True, stop=True))

            def stage2(it):
                h, tg = items[it]
                k = h % NB
                j = it % 2
                rows = (h * 64, (h + 1) * 64)
                sc.op("act", [rpn[j]], [rrr[j]], lambda e: e.activation(out=rr[j][:], in_=ps_n[j][:], func=AF.Ln, bias=EPS, scale=1.0 / 64))
                sc.op("act", [rrr[j]], [rrr[j]], lambda e: e.activation(out=rr[j][:], in_=rr[j][:], func=AF.Exp, scale=-0.5))
                sc.op("dve", [rpo[j], rrr[j], rpc], [ros[j]], lambda e: e.scalar_tensor_tensor(
                    out=osb[j][:], in0=ps_o[j][:], scalar=pcol[0:64, l, 0:1], in1=rr[j][:], op0=ALU.mult, op1=ALU.mult))
                sc.op("dve", [ros[j], rin[k]], [rof[j]], lambda e: e.tensor_tensor(
                    out=ofin[j][:], in0=osb[j][:], in1=GG[k][:, tg * 512:(tg + 1) * 512], op=ALU.mult))
                ro = Res()
                R.setdefault(f"retT{l}", []).append(ro)
                sc.dma("sp", T[f"retT{l}"][rows[0]:rows[1], tg * 512:(tg + 1) * 512], ofin[j][:], [rof[j]], [ro])

            stage1(0)
            for it in range(len(items)):
                if it + 1 < len(items):
                    stage1(it + 1)
                stage2(it)
            sc.barrier()
        self.stA.close()

    def phase_B(self, l):
        nc, sc, cst, T, sb, R = self.nc, self.sc, self.cst, self.T, self.sb, self.R
        ones, rones = cst["c_ones"]
        self.relayout_kv(l, 0)
        self.gather_kv(l, 1)
        self.dup_kv(l, 1)
        with ExitStack() as st:
            loc = {}
            for nm, shape, dt in (("c_bt", [128, 4, 4, 64], F32), ("c_dt", [128, 2, 4, 512], BF16), ("c_mi", [128, 4, 128], BF16)):
                t_ = sb(f"k_{nm}{l}", shape, dt, st)
                r_ = Res()
                sc.dma("sp", t_[:], T[nm], [], [r_])
                loc[nm] = (t_, r_)
            bt, rbt = loc["c_bt"]; cdt, rdt = loc["c_dt"]; cmi, rmi = loc["c_mi"]
            KA = sb(f"KA{l}", [96, 2, S], BF16, st); rKA = [Res() for _ in range(4)]
            VA = sb(f"VA{l}", [128, 64, 128], BF16, st); rVA = [Res() for _ in range(4)]
            QB = sb(f"QB{l}", [96, 2, TL], BF16, st); rQ = Res()
            ps_s = [st.enter_context(nc.psum_tensor(f"bs{l}_{k}", [128, 2, 512], F32)) for k in range(2)]
            rps = [Res(), Res()]
            ps_a = st.enter_context(nc.psum_tensor(f"ba{l}", [128, 2, 512], F32)); rpa = Res()
            ps_m = st.enter_context(nc.psum_tensor(f"bm{l}", [128, 512], F32)); rpm = Res()
            ps_e = st.enter_context(nc.psum_tensor(f"be{l}", [128, 512], F32)); rpe = Res()
            oraw = sb(f"oraw{l}", [128, 2, 512], F32, st); roraw = Res()
            pending = []
            ssb = sb(f"ssb{l}", [64, 512], F32, st); rssb = Res()
            shi = sb(f"shi{l}", [64, 512], BF16, st)
            slo = sb(f"slo{l}", [64, 512], BF16, st); rhl = Res()
            P = [sb(f"P{l}_{k}", [128, 2, 512], BF16, st) for k in range(3)]
            rP = [Res() for _ in range(3)]
            rsum = sb(f"rsum{l}", [128, 2, 512], F32, st); rrs = Res()
            o12 = sb(f"o12{l}", [128, 2, 512], F32, st); ro12 = Res()
            ocb = sb(f"ocb{l}", [128, 512], F32, st); rocb = Res()
            sq = sb(f"sqb{l}", [128, 512], BF16, st); rsq = Res()
            rr = sb(f"rrb{l}", [128, 512], F32, st); rrr = Res()
            ofin = [sb(f"ofb{l}_{k}", [128, 512], BF16, st) for k in range(2)]
            rof = [Res(), Res()]
            pcnt = [0]
            for h in range(4):
                hp = h // 2
                hc = slice((h % 2) * 128, (h % 2) * 128 + 128)
                if h == 2:
                    self.relayout_kv(l, 1)
                m_h = SLOPES[h]

                def need(g_, kbr_):
                    c_, kb_ = kbr_ // 16, kbr_ % 16
                    q0, q1 = 512 * g_, 512 * g_ + 511
                    if c_ == 0:
                        d_ = max(0, q0 - (128 * kb_ + 127), 128 * kb_ - q1)
                    elif c_ == 1:
                        d_ = (TL + 128 * kb_) - q1
                    elif c_ == 3:
                        d_ = q0 - (-TL + 128 * kb_ + 127)
                    else:
                        d_ = min((2 * TL + 128 * kb_) - q1, q0 - (-2 * TL + 128 * kb_ + 127))
                    return m_h * d_ < 130.0
                blocks_g = []
                for g in range(4):
                    bl = [("L", 4 * g + jb) for jb in range(4)]
                    for c_ in (0, 1, 3, 2):
                        for kb_ in range(16):
                            kbr_ = c_ * 16 + kb_
                            if c_ == 0 and 4 * g <= kb_ <= 4 * g + 3:
                                continue
                            if need(g, kbr_):
                                bl.append(("R", kbr_))
                    blocks_g.append(bl)
                used_chunks = sorted({kbr_ // 16 for bl in blocks_g for (t_, kbr_) in bl}, key=lambda c_: (0, 1, 3, 2).index(c_))
                for m in range(2):
                    r0 = h * 128 + m * 64
                    sc.dma("sp", QB[0:64, m, :], T[f"dqT{l}"][r0:r0 + 64, :], R[f"dqT{l}"], [rQ])
                    sc.dma("sp", QB[64:96, m, :], T["c_qaug"][h], [], [rQ])
                for c_ in used_chunks:
                    cs = slice(c_ * TL, (c_ + 1) * TL)
                    for m in range(2):
                        rh = (h % 2) * 128 + m * 64
                        if c_ == 0:
                            sc.dma("sp", KA[0:64, m, cs], T[f"kt_loc{l}_{hp}"][rh:rh + 64, :], R[f"kt_loc{l}_{hp}"], [rKA[c_]])
                        else:
                            rb = (c_ - 1) * 256 + rh
                            sc.dma("sp", KA[0:64, m, cs], T[f"rel_kt{l}"][hp, rb:rb + 64, :], R[f"rel_kt{l}_{hp}"], [rKA[c_]])
                        sc.dma("sp", KA[64:96, m, cs], T["c_kaugc"][h, :, cs], [], [rKA[c_]])
                    if c_ == 0:
                        sc.dma("sp", VA[:, 0:16, :], T[f"v_loc{l}_{hp}"][:, hc].rearrange("(kb p) e -> p kb e", p=128),
                               R[f"v_loc{l}_{hp}"], [rVA[c_]])
                    else:
                        sc.dma("sp", VA[:, c_ * 16:(c_ + 1) * 16, :],
                               T[f"rel_v{l}"][hp, (c_ - 1) * TL:c_ * TL, hc].rearrange("(kb p) e -> p kb e", p=128),
                               R[f"rel_v{l}_{hp}"], [rVA[c_]])
                for g in range(4):
                    gc = slice(g * 512, (g + 1) * 512)
                    blocks = blocks_g[g]
                    nb = len(blocks)

                    def emit_qk(bi):
                        typ, kb = blocks[bi]
                        j = bi % 2
                        kc = slice(kb * 128, (kb + 1) * 128)
                        if typ == "R":
                            def f_qk(e):
                                e.matmul(ps_s[j][:, 0, :], lhsT=KA[0:96, 0, kc], rhs=QB[0:96, 0, gc], start=True, stop=True)
                                return e.matmul(ps_s[j][:, 1, :], lhsT=KA[0:96, 1, kc], rhs=QB[0:96, 1, gc], start=True, stop=True)
                            sc.op("pe", [rKA[kb // 16], rQ], [rps[j]], f_qk)
                        else:
                            jb = kb - 4 * g
                            def f_qk(e):
                                ins = None
                                for m in range(2):
                                    e.matmul(ps_s[j][:, m, :], lhsT=KA[0:64, m, kc], rhs=QB[0:64, m, gc], start=True, stop=False)
                                    e.matmul(ps_s[j][:, m, :], lhsT=cmi[:, h, :], rhs=cdt[:, 0, jb, :], start=False, stop=False)
                                    ins = e.matmul(ps_s[j][:, m, :], lhsT=cmi[:, h, :], rhs=cdt[:, 1, jb, :], start=False, stop=True)
                                return ins
                            sc.op("pe", [rKA[0], rQ, rmi, rdt], [rps[j]], f_qk)

                    def emit_exp(bi):
                        typ, kb = blocks[bi]
                        j = bi % 2
                        pj = pcnt[0] % 3
                        bias = bt[:, h, g, kb:kb + 1] if typ == "R" else 0.0
                        sc.op("act", [rps[j], rbt], [rP[pj]], lambda e: e.activation(
                            out=P[pj][:], in_=ps_s[j][:], func=AF.Exp, bias=bias, scale=1.0))

                    def emit_av(bi):
                        typ, kb = blocks[bi]
                        pj = pcnt[0] % 3
                        pcnt[0] += 1
                        vt, rvt = VA, rVA[kb // 16]
                        def f_av(e):
                            for m in range(2):
                                e.matmul(ps_a[:, m, :], lhsT=vt[:, kb, :], rhs=P[pj][:, m, :], start=(bi == 0), stop=(bi == nb - 1))
                            e.matmul(ps_m[0:32, :], lhsT=ones[:, 0:32], rhs=P[pj][:, 0, :], start=(bi == 0), stop=(bi == nb - 1),
                                     tile_position=(0, 0))
                            return e.matmul(ps_m[32:64, :], lhsT=ones[:, 32:64], rhs=P[pj][:, 1, :], start=(bi == 0), stop=(bi == nb - 1),
                                            tile_position=(0, 32))
                        sc.op("pe", [rP[pj], rvt, rones], [rpa, rpm], f_av)

                    emit_qk(0)
                    for bi in range(nb):
                        if bi + 1 < nb:
                            emit_qk(bi + 1)
                        emit_exp(bi)
                        emit_av(bi)
                        if pending and bi >= 1:
                            pending.pop(0)()
                    sc.op("dve", [rpa], [roraw], lambda e: e.tensor_copy(out=oraw[:], in_=ps_a[:]))
                    sc.op("dve", [rpm], [rssb], lambda e: e.tensor_copy(out=ssb[:], in_=ps_m[0:64, :]))
                    sc.op("dve", [rssb], [rhl], lambda e: e.tensor_copy(out=shi[:], in_=ssb[:]))
                    sc.op("dve", [rssb, rhl], [rhl], lambda e: e.tensor_tensor(out=slo[:], in0=ssb[:], in1=shi[:], op=ALU.subtract))

                    def mk_rep(m):
                        def stage():
                            def f_rep(e):
                                e.matmul(ps_e[:], lhsT=ones[32 * m:32 * m + 32, :], rhs=shi[32 * m:32 * m + 32, :], start=True, stop=False)
                                return e.matmul(ps_e[:], lhsT=ones[32 * m:32 * m + 32, :], rhs=slo[32 * m:32 * m + 32, :], start=False, stop=True)
                            sc.op("pe", [rhl, rones], [rpe], f_rep)
                            sc.op("dve", [rpe], [rrs], lambda e: e.reciprocal(out=rsum[:, m, :], in_=ps_e[:]))
                        return stage

                    def stage3():
                        sc.op("dve", [roraw, rrs], [ro12], lambda e: e.scalar_tensor_tensor(
                            out=o12[:], in0=oraw[:], scalar=32.0, in1=rsum[:], op0=ALU.mult, op1=ALU.mult))
                        sc.op("dve", [ro12, self.r_der], [rocb], lambda e: e.scalar_tensor_tensor(
                            out=ocb[:], in0=o12[:, 1, :], scalar=self.lam[:, 1:2], in1=o12[:, 0, :], op0=ALU.mult, op1=ALU.add))
                        sc.op("act", [rocb], [rsq], lambda e: e.activation(out=sq[:], in_=ocb[:], func=AF.Square))

                    def stage4():
                        sc.op("pe", [rsq, rones], [rpe], lambda e: e.matmul(ps_e[:], lhsT=ones[:], rhs=sq[:], start=True, stop=True))
                        sc.op("act", [rpe], [rrr], lambda e: e.activation(out=rr[:], in_=ps_e[:], func=AF.Ln, bias=EPS, scale=1.0 / 128))
                        sc.op("act", [rrr], [rrr], lambda e: e.activation(out=rr[:], in_=rr[:], func=AF.Exp, scale=-0.5))

                    def mk_fin(h=h, g=g, gc=gc):
                        def stage():
                            fo = (h * 4 + g) % 2
                            sc.op("dve", [rocb, rrr, self.r_der], [rof[fo]], lambda e: e.scalar_tensor_tensor(
                                out=ofin[fo][:], in0=ocb[:], scalar=self.gdf[:, 0:1], in1=rr[:], op0=ALU.mult, op1=ALU.mult))
                            ro = Res()
                            R.setdefault(f"daT{l}", []).append(ro)
                            sc.dma("sp", T[f"daT{l}"][h * 128:(h + 1) * 128, gc], ofin[fo][:], [rof[fo]], [ro])
                        return stage
                    while pending:
                        pending.pop(0)()
                    pending.extend([mk_rep(0), mk_rep(1), stage3, stage4, mk_fin()])
            while pending:
                pending.pop(0)()
            sc.barrier()

    def phase_C(self, l):
        nc, sc, cst, T, sb, R = self.nc, self.sc, self.cst, self.T, self.sb, self.R
        xs = self.xs
        stF = ExitStack()
        wd = sb(f"wd{l}", [128, NF, D], BF16, stF); rwd = Res()
        wdr = T["w_down"][l].rearrange("(f p) n -> p f n", p=128)
        with ExitStack() as st:
            OT = sb(f"OT{l}", [128, 8, TL], BF16, st); rOT = Res()
            wo = sb(f"wo{l}", [128, 8, D], BF16, st); rwo = Res()
            pp = [st.enter_context(nc.psum_tensor(f"cp{l}_{k}", [128, 2, 512], F32)) for k in range(2)]
            rpp = [Res(), Res()]
            for c in range(4):
                sc.dma("sp", OT[:, c, :], T[f"retT{l}"][c * 128:(c + 1) * 128, :], R[f"retT{l}"], [rOT])
                sc.dma("sp", OT[:, 4 + c, :], T[f"daT{l}"][c * 128:(c + 1) * 128, :], R[f"daT{l}"], [rOT])
            self.load_w("w_out", l, 0, 512, wo, rwo)
            src = T["w_out"][l].rearrange("(kc p) n -> p kc n", p=128)[:, :, 512:1024]
            sc.dma("pool", wo[:, :, 512:1024], src, [], [rwo])
            for f0 in range(0, NF, 2):
                sc.dma("pool", wd[:, f0:f0 + 2, :], wdr[:, f0:f0 + 2, :], [], [rwd])
            for i in range(NT):
                j = i % 2
                def f(e, i=i, j=j):
                    ins = None
                    for nb_ in range(2):
                        for c in range(8):
                            ins = e.matmul(pp[j][:, nb_, :], lhsT=OT[:, c, i * 128:(i + 1) * 128], rhs=wo[:, c, nb_ * 512:(nb_ + 1) * 512],
                                           start=(c == 0), stop=(c == 7))
                    return ins
                sc.op("pe", [rOT, rwo], [rpp[j]], f)
                sc.op("dve", [rpp[j], self.xs_r[i]], [self.xs_r[i]], lambda e, i=i, j=j: e.tensor_tensor(
                    out=xs[:, i, :], in0=xs[:, i, :], in1=pp[j][:].rearrange("p a b -> p (a b)"), op=ALU.add))
            sc.barrier()
        with stF as st:
            wg = [sb(f"wg{l}_{k}", [128, 8, 256], BF16, st) for k in range(2)]
            wu = [sb(f"wu{l}_{k}", [128, 8, 256], BF16, st) for k in range(2)]
            rwg = [Res(), Res()]
            h2 = sb(f"h2{l}", [128, 8, 512], BF16, st); rh2 = Res()
            aT = sb(f"aT{l}", [128, NF, 512], BF16, st); raT = Res()
            ps_t = st.enter_context(nc.psum_tensor(f"fpt{l}", [128, 8, 128], BF16)); rpt = Res()
            pg = [st.enter_context(nc.psum_tensor(f"fpg{l}_{k}", [128, 2, 512], F32)) for k in range(2)]
            rpg = [Res(), Res()]
            pd = [st.enter_context(nc.psum_tensor(f"fpd{l}_{k}", [128, 512], F32)) for k in range(2)]
            rpd = [Res(), Res()]
            sg = [sb(f"fsg{l}_{k}", [128, 512], F32, st) for k in range(2)]
            rsg = [Res(), Res()]
            NP = NF // 2
            wsc = self.T.get(f"wgu{l}")
            rws = [[Res(), Res()] for _ in range(NP)]
            seq = [(tg, fp) for tg in range(4) for fp in range(NP)]

            def issue_load(idx):
                tg_, fp_ = seq[idx]
                k_ = idx % 2
                if tg_ == 0:
                    self.load_w("w_gate", l, fp_ * 256, 256, wg[k_], rwg[k_])
                    self.load_w("w_up", l, fp_ * 256, 256, wu[k_], rwg[k_])
                    sc.dma("sp", wsc[fp_, 0], wg[k_][:], [rwg[k_]], [rws[fp_][0]])
                    sc.dma("sp", wsc[fp_, 1], wu[k_][:], [rwg[k_]], [rws[fp_][1]])
                else:
                    sc.dma("sp", wg[k_][:], wsc[fp_, 0], [rws[fp_][0]], [rwg[k_]])
                    sc.dma("sp", wu[k_][:], wsc[fp_, 1], [rws[fp_][1]], [rwg[k_]])
            issue_load(0)
            nscr = self.norm_scratch(l, 1, "f", 4, st)
            for tg in range(4):
                with ExitStack() as st2:
                    self.norm_T(l, 1, list(range(tg * 4, tg * 4 + 4)), h2, rh2, st2, ps_t, rpt, scratch=nscr)
                    for fp in range(NP):
                        idx = tg * NP + fp
                        k = idx % 2
                        if idx + 1 < len(seq):
                            issue_load(idx + 1)
                        for s_ in range(2):
                            f_ = fp * 2 + s_
                            j = f_ % 2
                            def fm(e, k=k, s_=s_, j=j):
                                ins = None
                                for wi_, wt in enumerate((wg[k], wu[k])):
                                    for kc in range(8):
                                        ins = e.matmul(pg[j][:, wi_, :], lhsT=wt[:, kc, s_ * 128:(s_ + 1) * 128], rhs=h2[:, kc, :],
                                                       start=(kc == 0), stop=(kc == 7))
                                return ins
                            sc.op("pe", [rwg[k], rh2], [rpg[j]], fm)
                            sc.op("act", [rpg[j]], [rsg[j]], lambda e, j=j: e.activation(out=sg[j][:], in_=pg[j][:, 0, :], func=AF.Silu))
                            sc.op("dve", [rsg[j], rpg[j]], [raT], lambda e, j=j, f_=f_: e.tensor_tensor(
                                out=aT[:, f_, :], in0=sg[j][:], in1=pg[j][:, 1, :], op=ALU.mult))
                    for ti in range(4):
                        i = tg * 4 + ti
                        for hf in range(2):
                            j = (ti * 2 + hf) % 2
                            def fd(e, ti=ti, hf=hf, j=j):
                                ins = None
                                for f_ in range(NF):
                                    ins = e.matmul(pd[j][:], lhsT=aT[:, f_, ti * 128:(ti + 1) * 128], rhs=wd[:, f_, hf * 512:(hf + 1) * 512],
                                                   start=(f_ == 0), stop=(f_ == NF - 1))
                                return ins
                            sc.op("pe", [raT, rwd], [rpd[j]], fd)
                            sc.op("dve", [rpd[j], self.xs_r[i]], [self.xs_r[i]], lambda e, i=i, hf=hf, j=j: e.tensor_tensor(
                                out=xs[:, i, hf * 512:(hf + 1) * 512], in0=xs[:, i, hf * 512:(hf + 1) * 512], in1=pd[j][:], op=ALU.add))
            sc.barrier()


def _bf(a):
    return np.ascontiguousarray(a.astype(NPBF))


def _consts(core):
    r = core % 4
    c = {}
    c["c_ident"] = _bf(np.eye(128, dtype=np.float32))
    c["c_ones"] = _bf(np.ones((128, 128), np.float32))
    blk = np.zeros((128, 128), np.float32)
    blk[:64, :64] = 1; blk[64:, 64:] = 1
    c["c_blk"] = _bf(blk)
    s = np.arange(128)[:, None]; t = np.arange(128)[None, :]
    dm = np.zeros((128, 4, 128), np.float32)
    dm[:, 0] = np.maximum(t - s, 0); dm[:, 1] = (t >= s); dm[:, 2] = np.maximum(s - t, 0); dm[:, 3] = (s > t)
    c["c_dm"] = dm
    pos = np.zeros((128, 2, 128), np.float32)
    pos[:, 0] = np.arange(128) + 1; pos[:, 1] = 128 - np.arange(128)
    c["c_pos"] = pos
    pp = np.zeros((128, 2), np.float32)
    pp[:, 0] = 127 - np.arange(128); pp[:, 1] = np.arange(128)
    c["c_ppos"] = pp
    ex = np.zeros((128, 8), np.float32)
    for p in range(4):
        if p < r:
            ex[:64, p] = 128.0 * 16 * (r - 1 - p); ex[:64, 4 + p] = 1
        if p > r:
            ex[64:, p] = 128.0 * 16 * (p - r - 1); ex[64:, 4 + p] = 1
    c["c_ex"] = ex
    tl = np.arange(TL); i = tl % 512; gq = tl // 512
    qaug = np.zeros((4, 32, TL), np.float32)
    kaugc = np.zeros((4, 32, S), np.float32)
    col = np.arange(S); cc_ = col // TL; tok = col % TL
    s_abs = ((r + cc_) % 4) * TL + tok
    S0 = 128 * (s_abs // 128)
    bt = np.zeros((128, 4, 4, 64), np.float32)
    mi = np.zeros((128, 4, 128), np.float32)
    kbr = np.arange(64)
    S0b = ((r + kbr // 16) % 4) * TL + 128 * (kbr % 16)
    for h in range(4):
        m = SLOPES[h]
        for g in range(4):
            sel = (gq == g)
            qaug[h, 4 * g + 0] = np.where(sel, -m * 256 * (i // 256), 0.0)
            qaug[h, 4 * g + 1] = np.where(sel, -m * (i % 256), 0.0)
            qaug[h, 4 * g + 2] = np.where(sel, 1.0, 0.0)
            qaug[h, 4 * g + 3] = np.where(sel, 1.0, 0.0)
            T0 = r * TL + 512 * g
            sig = np.where(S0 + 128 <= T0, 1.0, np.where(S0 >= T0 + 512, -1.0, 0.0))
            kaugc[h, 4 * g + 0] = sig; kaugc[h, 4 * g + 1] = sig; kaugc[h, 4 * g + 2] = sig * m * (tok % 128)
            kaugc[h, 4 * g + 3] = np.where(sig == 0.0, MASKV, 0.0)
            bt[:, h, g, :] = -m * np.abs(T0 - S0b)[None, :]
        mi[:, h, :] = m * np.eye(128)
    c["c_qaug"] = _bf(qaug); c["c_kaugc"] = _bf(kaugc)
    c["c_bt"] = bt; c["c_mi"] = _bf(mi)
    dt = np.zeros((128, 2, 4, 512), np.float32)
    jj = np.arange(128)[:, None]; ii = np.arange(512)[None, :]
    for jb in range(4):
        v = np.abs(ii - 128 * jb - jj)
        dt[:, 0, jb] = -(2 * (v // 2)); dt[:, 1, jb] = -(v % 2)
    c["c_dt"] = _bf(dt)
    return c


def _params(inp):
    f = lambda a: np.asarray(a, dtype=np.float32)
    p = {}
    g_attn = f(inp["attn_norm_g"]); g_ffn = f(inp["ffn_norm_g"])
    gcol = np.zeros((128, 2, 2, 8), np.float32)
    for l in range(2):
        gcol[:, l, 0, :] = g_attn[l].reshape(8, 128).T
        gcol[:, l, 1, :] = g_ffn[l].reshape(8, 128).T
    p["gcol"] = gcol
    pcol = np.zeros((128, 2, 4), np.float32)
    idx = np.arange(128) % 64
    for l in range(2):
        pcol[:, l, 0] = f(inp["ret_norm_g"])[l][idx]
        pcol[:, l, 1] = f(inp["dq_norm_g"])[l][idx]
        pcol[:, l, 2] = f(inp["dk_norm_g"])[l][idx]
        pcol[:, l, 3] = f(inp["diff_norm_g"])[l]
    p["pcol"] = pcol
    df = f(inp["ret_decay_fwd"]); db = f(inp["ret_decay_bwd"])
    dech = np.zeros((128, 2, 2, 8), np.float32)
    decp = np.zeros((128, 2, 2, 4), np.float32)
    for l in range(2):
        dech[:, l, 0, :] = df[l][None, :]; dech[:, l, 1, :] = db[l][None, :]
        for cc in range(4):
            decp[:64, l, 0, cc] = df[l][2 * cc]; decp[64:, l, 0, cc] = df[l][2 * cc + 1]
            decp[:64, l, 1, cc] = db[l][2 * cc]; decp[64:, l, 1, cc] = db[l][2 * cc + 1]
    p["dech"] = dech; p["decp"] = decp
    lamv = np.zeros((128, 2, 4, 64), np.float32)
    for l in range(2):
        for k, nm in enumerate(("lambda_q1", "lambda_k1", "lambda_q2", "lambda_k2")):
            lamv[:, l, k, :] = f(inp[nm])[l][None, :]
    p["lamv"] = lamv
    for nm in ("w_in", "w_out", "w_gate", "w_up", "w_down"):
        p[nm] = np.ascontiguousarray(f(inp[nm]))
    return p


_NC_CACHE = {}


def _get_nc(mode):
    if mode not in _NC_CACHE:
        _NC_CACHE[mode] = Builder(mode).build()
    return _NC_CACHE[mode]


def _gather(res, name, group):
    return np.concatenate([np.asarray(res[c][name]) for c in group], axis=0)


FUSED = True


def kernel(**inputs):
    x = np.asarray(inputs["x"], dtype=np.float32)
    p = _params(inputs)
    base = []
    for c in range(NCORE):
        d = dict(p)
        d.update(_consts(c))
        b, r = c // 4, c % 4
        d["x"] = np.ascontiguousarray(x[b, r * TL:(r + 1) * TL, :])
        base.append(d)
    cores = list(range(NCORE))
    if FUSED:
        res = run_bass_kernel_spmd(_get_nc("fused"), base, core_ids=cores).results
    else:
        xk = [f"{a}{{l}}_{hp}" for a in ("kt", "v") for hp in (0, 1)]
        def xfer(prev, l):
            outs = []
            for c in range(NCORE):
                grp = [4 * (c // 4) + k for k in range(4)]
                d = dict(base[c])
                for a in ("kt", "v"):
                    for hp in (0, 1):
                        d[f"{a}_all{l}_{hp}"] = _gather(prev, f"{a}_loc{l}_{hp}", grp)
                d[f"st_all{l}"] = _gather(prev, f"st_loc{l}", grp)
                outs.append(d)
            return outs
        r1 = run_bass_kernel_spmd(_get_nc("L1"), base, core_ids=cores).results
        in2 = xfer(r1, 0)
        r2 = run_bass_kernel_spmd(_get_nc("L2"), in2, core_ids=cores).results
        in3 = xfer(r2, 1)
        for c in range(NCORE):
            in3[c]["x"] = np.asarray(r2[c]["y"])
        res = run_bass_kernel_spmd(_get_nc("L3"), in3, core_ids=cores).results
    out = np.zeros((2, S, D), np.float32)
    for c in range(NCORE):
        b, r = c // 4, c % 4
        out[b, r * TL:(r + 1) * TL, :] = np.asarray(res[c]["y"])
    return out
```

```python
import math
from contextlib import ExitStack

import numpy as np
import ml_dtypes

import concourse.bass as bass
import concourse.mybir as mybir
from concourse.bass_utils import run_bass_kernel_spmd

F32 = mybir.dt.float32
BF16 = mybir.dt.bfloat16
AF = mybir.ActivationFunctionType
ALU = mybir.AluOpType
AX = mybir.AxisListType
NPBF = ml_dtypes.bfloat16

D = 1024
S = 8192
NCORE = 8
TL = 2048
NT = 16
DIN = 3584
DFF = 2816
NF = 22
EPS = 1e-6
SLOPES = [2.0 ** (-8.0 * (i + 1) / 4) for i in range(4)]
MASKV = -30000.0


class Res:
    __slots__ = ("w", "rd")

    def __init__(self):
        self.w = None
        self.rd = {}


class Sched:
    def __init__(self, nc, es, ndma=12):
        self.nc = nc
        self.eng = {"pe": nc.tensor, "act": nc.scalar, "dve": nc.vector, "pool": nc.gpsimd, "sp": nc.sync}
        self.sem = {k: es.enter_context(nc.semaphore("s_" + k)) for k in ("pe", "act", "dve", "pool")}
        self.cnt = {k: 0 for k in self.sem}
        self.seen = {e: {} for e in self.eng}
        self.dsem = {q: [es.enter_context(nc.semaphore(f"d_{q}{i}")) for i in range(ndma)] for q in ("sp", "pool")}
        self.dcnt = {q: [0] * ndma for q in self.dsem}
        self.drr = {q: 0 for q in self.dsem}
        self.ccsem = [es.enter_context(nc.semaphore(f"cc{i}")) for i in range(10)]
        self.ncc = 0

    def _need(self, eng, toks):
        best = {}
        for (s, v) in toks:
            if best.get(s, 0) < v:
                best[s] = v
        for s, v in best.items():
            if self.seen[eng].get(s, 0) >= v:
                continue
            self.eng[eng].wait_ge(s, v)
            self.seen[eng][s] = v

    def _deps(self, eng, reads, writes):
        toks = []
        for r in reads:
            if r.w is not None:
                toks.append(r.w)
        for w in writes:
            if w.w is not None:
                toks.append(w.w)
            toks.extend(w.rd.items())
        if eng == "pe":
            toks = [t for t in toks if t[0] is not self.sem["pe"]]
        return toks

    def _mark(self, tok, reads, writes):
        for r in reads:
            if r.rd.get(tok[0], 0) < tok[1]:
                r.rd[tok[0]] = tok[1]
        for w in writes:
            w.w = tok
            w.rd = {}

    def op(self, eng, reads, writes, fn):
        self._need(eng, self._deps(eng, reads, writes))
        ins = fn(self.eng[eng])
        self.cnt[eng] += 1
        ins.then_inc(self.sem[eng], 1)
        tok = (self.sem[eng], self.cnt[eng])
        self._mark(tok, reads, writes)
        return tok

    def dma(self, q, out, in_, reads, writes):
        toks = self._deps(q, reads, writes)
        k = self.drr[q]
        self.drr[q] = (k + 1) % len(self.dsem[q])
        if self.dcnt[q][k] > 0:
            toks.append((self.dsem[q][k], self.dcnt[q][k]))
        self._need(q, toks)
        ins = self.eng[q].dma_start(out=out, in_=in_)
        self.dcnt[q][k] += 16
        ins.then_inc(self.dsem[q][k], 16)
        tok = (self.dsem[q][k], self.dcnt[q][k])
        self._mark(tok, reads, writes)
        return tok

    def allgather(self, in_ap, out_ap, reads, writes):
        self._need("pool", self._deps("pool", reads, writes))
        s = self.ccsem[self.ncc]
        self.ncc += 1
        ins = self.nc.gpsimd.collective_compute(
            "AllGather", ALU.bypass, replica_groups=[[0, 1, 2, 3], [4, 5, 6, 7]], ins=[in_ap], outs=[out_ap])
        ins.then_inc(s, 1)
        tok = (s, 1)
        self._mark(tok, reads, writes)
        return tok

    def all_tokens(self):
        toks = [(self.sem[k], self.cnt[k]) for k in self.sem if self.cnt[k] > 0]
        for q in self.dsem:
            toks += [(s, c) for s, c in zip(self.dsem[q], self.dcnt[q]) if c > 0]
        return toks

    def barrier(self, engines=None):
        toks = self.all_tokens()
        for e in (engines or self.eng):
            self._need(e, toks)


def bc(ap, shape):
    return ap.unsqueeze(len(ap.shape)).to_broadcast(list(shape))


class _Stop(Exception):
    pass


class Builder:
    stopped = False

    def chk(self, tag):
        import os
        if os.environ.get("KSTOP", "") == tag:
            self.stopped = True
        return self.stopped

    def __init__(self, mode, dbg=False):
        self.mode = mode
        self.dbg = dbg
        self.nc = bass.Bass("TRN2", target_bir_lowering=False)
        self.T = {}

    def din(self, name, shape, dt=F32):
        t = self.nc.dram_tensor(name, list(shape), dt, kind="ExternalInput").ap()
        self.T[name] = t
        return t

    def dout(self, name, shape, dt=F32):
        t = self.nc.dram_tensor(name, list(shape), dt, kind="ExternalOutput").ap()
        self.T[name] = t
        return t

    def dscr(self, name, shape, dt=BF16):
        t = self.nc.dram_tensor(name, list(shape), dt).ap()
        self.T[name] = t
        return t

    def build(self):
        nc = self.nc
        mode = self.mode
        T = self.T
        self.din("x", [TL, D])
        self.din("w_in", [2, D, DIN]); self.din("w_out", [2, D, D])
        self.din("w_gate", [2, D, DFF]); self.din("w_up", [2, D, DFF]); self.din("w_down", [2, DFF, D])
        self.din("gcol", [128, 2, 2, 8])
        self.din("pcol", [128, 2, 4])
        self.din("decp", [128, 2, 2, 4])
        self.din("dech", [128, 2, 2, 8])
        self.din("lamv", [128, 2, 4, 64])
        self.din("c_ident", [128, 128], BF16)
        self.din("c_ones", [128, 128], BF16)
        self.din("c_blk", [128, 128], BF16)
        self.din("c_dm", [128, 4, 128])
        self.din("c_pos", [128, 2, 128])
        self.din("c_ppos", [128, 2])
        self.din("c_ex", [128, 8])
        self.din("c_qaug", [4, 32, TL], BF16)
        self.din("c_kaugc", [4, 32, S], BF16)
        self.din("c_bt", [128, 4, 4, 64])
        self.din("c_dt", [128, 2, 4, 512], BF16)
        self.din("c_mi", [128, 4, 128], BF16)
        layers = {"fused": [0, 1], "L1": [0], "L2": [0, 1], "L3": [1]}[mode]
        for l in layers:
            import os
            for nm in ("rqT", "rqfT", "rqbT", "rkT", "rgT", "dqT"):
                self.dscr(f"{nm}{l}", [512, TL])
            self.dscr(f"wgu{l}", [NF // 2, 2, 128, 8, 256])
            for nm in ("retT", "daT"):
                if os.environ.get("KDBG"):
                    self.dout(f"{nm}{l}", [512, TL], BF16)
                else:
                    self.dscr(f"{nm}{l}", [512, TL])
        def xdecl(l, loc_kind, all_kind):
            for hp in (0, 1):
                for (nm, shp, dt_) in ((f"kt_loc{l}_{hp}", [256, TL], BF16), (f"v_loc{l}_{hp}", [TL, 256], BF16)):
                    (self.dout if loc_kind == "out" else self.dscr)(nm, shp, dt_)
                if all_kind is not None:
                    for (nm, shp, dt_) in ((f"kt_all{l}_{hp}", [1024, TL], BF16), (f"v_all{l}_{hp}", [S, 256], BF16)):
                        (self.din if all_kind == "in" else self.dscr)(nm, shp, dt_)
            (self.dout if loc_kind == "out" else self.dscr)(f"st_loc{l}", [128, 512], F32)
            if all_kind is not None:
                (self.din if all_kind == "in" else self.dscr)(f"st_all{l}", [512, 512], F32)
        if mode == "fused":
            for l in (0, 1):
                for hp in (0, 1):
                    self.dscr(f"kt_loc{l}_{hp}", [256, TL], BF16); self.dscr(f"v_loc{l}_{hp}", [TL, 256], BF16)
                self.dscr(f"st_loc{l}", [128, 512], F32); self.dscr(f"st_all{l}", [512, 512], F32)
                for hp in (0, 1):
                    self.dscr(f"kt_all{l}_{hp}", [1024, TL], BF16); self.dscr(f"v_all{l}_{hp}", [S, 256], BF16)
                    self.dscr(f"ktbig{l}_{hp}", [2048, TL], BF16); self.dscr(f"vbig{l}_{hp}", [2 * S, 256], BF16)
                self.dscr(f"rel_kt{l}", [2, 768, TL], BF16); self.dscr(f"rel_v{l}", [2, 3 * TL, 256], BF16)
        elif mode == "L1":
            xdecl(0, "out", None)
        elif mode == "L2":
            xdecl(0, "scr", "in"); xdecl(1, "out", None)
        elif mode == "L3":
            xdecl(1, "scr", "in")
        if mode in ("fused", "L2", "L3"):
            self.dout("y", [TL, D])

        with ExitStack() as es:
            self.es = es
            sc = self.sc = Sched(nc, es)
            self.R = {}
            self.stA = ExitStack()

            def sb(name, shape, dt, stack=es):
                t = stack.enter_context(nc.sbuf_tensor(name, list(shape), dt))
                return t
            self.sb = sb
            xs = self.xs = sb("xs", [128, NT, D], F32)
            self.xs_r = [Res() for _ in range(NT)]
            cst = self.cst = {}
            for nm, shape, dt in (("gcol", [128, 2, 2, 8], F32), ("pcol", [128, 2, 4], F32), ("decp", [128, 2, 2, 4], F32),
                                  ("dech", [128, 2, 2, 8], F32), ("lamv", [128, 2, 4, 64], F32),
                                  ("c_ident", [128, 128], BF16), ("c_ones", [128, 128], BF16), ("c_blk", [128, 128], BF16),
                                  ("c_dm", [128, 4, 128], F32), ("c_pos", [128, 2, 128], F32), ("c_ppos", [128, 2], F32),
                                  ("c_ex", [128, 8], F32)):
                t = sb("k_" + nm, shape, dt)
                r = Res()
                sc.dma("sp", t[:], T[nm], [], [r])
                cst[nm] = (t, r)
            xr = T["x"].rearrange("(n p) d -> p n d", p=128)
            for i in range(NT):
                sc.dma("sp", xs[:, i, :], xr[:, i, :], [], [self.xs_r[i]])
            self.dmask = sb("dmask", [128, 8, 128], F32)
            self.wq2 = sb("wq2", [128, 2, 4, 128], F32)
            self.wk8 = sb("wk8", [128, 2, 8], F32)
            self.dccat = sb("dccat", [128, 8], F32)
            self.lgh = sb("lgh", [128, 2, 8], F32)
            self.lgp = sb("lgp", [128, 2, 4], F32)
            self.lgcat = sb("lgcat", [128, 8], F32)
            self.lam = sb("lam", [128, 4], F32)
            self.gk8 = sb("gk8", [128, 1], F32)
            self.gdf = sb("gdf", [128, 1], F32)
            self.r_der = Res()

            self.chk("consts")
            for l in layers:
                if self.stopped:
                    break
                do_attn = not (mode == "L1") and not (mode == "L2" and l == 1)
                self.derived(l)
                if self.chk("derived"):
                    break
                self.phase_A(l)
                if self.chk("A") or not do_attn:
                    break
                self.phase_R(l)
                if self.chk("R"):
                    break
                self.phase_B(l)
                if self.chk("B"):
                    break
                self.phase_C(l)
                if self.chk("C"):
                    break
            self.stA.close()
            if "y" in T:
                yr = T["y"].rearrange("(n p) d -> p n d", p=128)
                for i in range(NT):
                    sc.dma("sp", yr[:, i, :], xs[:, i, :], [self.xs_r[i]], [])
            sc.barrier()
        return nc

    def derived(self, l):
        sc, cst = self.sc, self.cst
        rd = self.r_der
        dech, r1 = cst["dech"]; decp, r2 = cst["decp"]; lamv, r3 = cst["lamv"]
        cdm, r4 = cst["c_dm"]; cpos, r5 = cst["c_pos"]; cpp, r6 = cst["c_ppos"]; pcol, r7 = cst["pcol"]
        lgh, lgp, lgcat = self.lgh, self.lgp, self.lgcat
        with ExitStack() as st:
            tmp = self.sb(f"dtmp{l}", [128, 2, 128], F32, st)
            tl = self.sb(f"dtl{l}", [128, 4, 64], F32, st)
            rt = Res()
            sc.op("act", [r1, r2], [rd], lambda e: (e.activation(out=lgh[:], in_=dech[:, l], func=AF.Exp),
                                                     e.activation(out=lgp[:], in_=decp[:, l], func=AF.Exp))[1])
            sc.op("dve", [rd], [rd], lambda e: (e.tensor_scalar(out=lgh[:], in0=lgh[:], scalar1=-1.0, scalar2=None, op0=ALU.mult),
                                                 e.tensor_scalar(out=lgp[:], in0=lgp[:], scalar1=-1.0, scalar2=None, op0=ALU.mult))[1])
            sc.op("dve", [rd], [rd], lambda e: (e.tensor_copy(out=lgcat[0:64, :], in_=lgh[0:64, 0, :]),
                                                 e.tensor_copy(out=lgcat[64:128, :], in_=lgh[64:128, 1, :]))[1])
            sc.op("act", [rd], [rd], lambda e: e.activation(out=self.dccat[:], in_=lgcat[:], func=AF.Exp, scale=128.0))
            def f_wk(e):
                e.tensor_scalar(out=self.wk8[:, 0, :], in0=lgh[:, 0, :], scalar1=cpp[:, 0:1], scalar2=None, op0=ALU.mult)
                return e.tensor_scalar(out=self.wk8[:, 1, :], in0=lgh[:, 1, :], scalar1=cpp[:, 1:2], scalar2=None, op0=ALU.mult)
            sc.op("dve", [rd, r6], [rd], f_wk)
            sc.op("act", [rd], [rd], lambda e: e.activation(out=self.wk8[:], in_=self.wk8[:], func=AF.Exp))
            sc.op("dve", [rd], [rd], lambda e: e.tensor_scalar(out=self.wk8[:], in0=self.wk8[:], scalar1=0.125, scalar2=None, op0=ALU.mult))
            def f_wq(e):
                ins = None
                for d in range(2):
                    for cc in range(4):
                        ins = e.activation(out=self.wq2[:, d, cc, :], in_=cpos[:, d, :], func=AF.Exp, scale=lgp[:, d, cc:cc + 1])
                return ins
            sc.op("act", [rd, r5], [rd], f_wq)
            for h in range(8):
                sc.op("act", [rd, r4], [rt], lambda e, h=h: (e.activation(out=tmp[:, 0, :], in_=cdm[:, 0, :], func=AF.Exp, scale=lgh[:, 0, h:h + 1]),
                                                              e.activation(out=tmp[:, 1, :], in_=cdm[:, 2, :], func=AF.Exp, scale=lgh[:, 1, h:h + 1]))[1])
                sc.op("dve", [rt, r4], [rt], lambda e: e.tensor_tensor(out=tmp[:, 0, :], in0=tmp[:, 0, :], in1=cdm[:, 1, :], op=ALU.mult))
                sc.op("dve", [rt, r4], [rt], lambda e: e.tensor_tensor(out=tmp[:, 1, :], in0=tmp[:, 1, :], in1=cdm[:, 3, :], op=ALU.mult))
                sc.op("dve", [rt], [rt, rd], lambda e, h=h: e.tensor_tensor(out=self.dmask[:, h, :], in0=tmp[:, 0, :], in1=tmp[:, 1, :], op=ALU.add))
            lam_init = 0.8 - 0.6 * math.exp(-0.3 * l)
            sc.op("dve", [r3], [rt], lambda e: e.tensor_tensor(out=tl[:, 0, :], in0=lamv[:, l, 0, :], in1=lamv[:, l, 1, :], op=ALU.mult))
            sc.op("dve", [r3, rt], [rt], lambda e: e.tensor_tensor(out=tl[:, 1, :], in0=lamv[:, l, 2, :], in1=lamv[:, l, 3, :], op=ALU.mult))
            sc.op("dve", [rt], [rd], lambda e: e.reduce_sum(out=self.lam[:, 2:4], in_=tl[:, 0:2, :], axis=AX.X))
            sc.op("act", [rd], [rd], lambda e: e.activation(out=self.lam[:, 2:4], in_=self.lam[:, 2:4], func=AF.Exp))
            sc.op("dve", [rd], [rd], lambda e: e.tensor_tensor(out=self.lam[:, 0:1], in0=self.lam[:, 2:3], in1=self.lam[:, 3:4], op=ALU.subtract))
            sc.op("dve", [rd], [rd], lambda e: e.tensor_scalar(out=self.lam[:, 0:1], in0=self.lam[:, 0:1], scalar1=lam_init, scalar2=None, op0=ALU.add))
            sc.op("dve", [rd], [rd], lambda e: e.tensor_scalar(out=self.lam[:, 1:2], in0=self.lam[:, 0:1], scalar1=-1.0, scalar2=None, op0=ALU.mult))
            sc.op("dve", [rd, r7], [rd], lambda e: e.tensor_scalar(out=self.gk8[:], in0=pcol[:, l, 2:3], scalar1=8.0, scalar2=None, op0=ALU.mult))
            sc.op("dve", [rd, r7], [rd], lambda e: e.tensor_scalar(out=self.gdf[:], in0=pcol[:, l, 3:4], scalar1=1.0 - lam_init, scalar2=None, op0=ALU.mult))
            sc.barrier(["act", "dve"])

    def norm_scratch(self, l, kind, tag, n, st):
        ss = self.sb(f"ss{l}{kind}{tag}", [128, n], F32, st)
        junk = self.sb(f"junk{l}{kind}{tag}", [128, D], BF16, st)
        xn = [self.sb(f"xn{l}{kind}{tag}_{k}", [128, D], BF16, st) for k in range(2)]
        return (ss, Res(), junk, Res(), xn, [Res(), Res()])

    def norm_T(self, l, kind, tiles, hT, hT_r, st, ps_t, ps_r, scratch=None):
        sc, cst, xs = self.sc, self.cst, self.xs
        gcol, rg = cst["gcol"]; ident, ri = cst["c_ident"]
        n = len(tiles)
        if scratch is None:
            scratch = self.norm_scratch(l, kind, tiles[0], n, st)
        ss, rs, junk, rj, xn, rxn = scratch
        for j, i in enumerate(tiles):
            sc.op("act", [self.xs_r[i]], [rj, rs], lambda e, i=i, j=j: e.activation(
                out=junk[:], in_=xs[:, i, :], func=AF.Square, accum_out=ss[:, j:j + 1]))
        sc.op("act", [rs], [rs], lambda e: e.activation(out=ss[:], in_=ss[:], func=AF.Ln, bias=EPS, scale=1.0 / D))
        sc.op("act", [rs], [rs], lambda e: e.activation(out=ss[:], in_=ss[:], func=AF.Exp, scale=-0.5))
        for j, i in enumerate(tiles):
            k = j % 2
            sc.op("dve", [self.xs_r[i], rs], [rxn[k]], lambda e, i=i, j=j, k=k: e.tensor_scalar(
                out=xn[k][:], in0=xs[:, i, :], scalar1=ss[:, j:j + 1], scalar2=None, op0=ALU.mult))
            def f_t(e, k=k):
                ins = None
                for c in range(8):
                    ins = e.transpose(out=ps_t[:, c, :], in_=xn[k][:, c * 128:(c + 1) * 128], identity=ident[:])
                return ins
            sc.op("pe", [rxn[k], ri], [ps_r], f_t)
            sc.op("dve", [ps_r, rg], [hT_r], lambda e, j=j: e.tensor_tensor(
                out=hT[:, :, j * 128:(j + 1) * 128], in0=ps_t[:], in1=bc(gcol[:, l, kind, :], [128, 8, 128]), op=ALU.mult))

    def load_w(self, wname, l, c0, ncols, dst, dst_r, krows=8):
        w = self.T[wname]
        src = w[l].rearrange("(kc p) n -> p kc n", p=128)[:, 0:krows, c0:c0 + ncols]
        self.sc.dma("pool", dst[:, 0:krows, 0:ncols], src, [], [dst_r])

    def phase_A(self, l):
        nc, sc, cst, T = self.nc, self.sc, self.cst, self.T
        sb = self.sb
        pcol, rpc = cst["pcol"]; blk, rblk = cst["c_blk"]
        sc.barrier()
        self.stA = ExitStack()
        stA = self.stA
        self.V_all = sb(f"V_all{l}", [128, NT, 512], BF16, stA)
        self.KV = sb(f"KV{l}", [128, NT, 512], F32, stA)
        self.rV = Res(); self.rKV = Res()
        V_all, KV = self.V_all, self.KV
        with ExitStack() as st:
            hT = sb(f"hT{l}", [128, 8, TL], BF16, st)
            rh = Res()
            wb = [sb(f"wb{l}_{k}", [128, 8, 512], BF16, st) for k in range(3)]
            rwb = [Res(), Res(), Res()]
            ps_t = st.enter_context(nc.psum_tensor(f"pst{l}", [128, 8, 128], BF16)); rpt = Res()
            pp = [st.enter_context(nc.psum_tensor(f"pp{l}_{k}", [128, 512], F32)) for k in range(3)]
            rpp = [Res() for _ in range(3)]
            pss = st.enter_context(nc.psum_tensor(f"pss{l}", [128, 512], F32)); rss = Res()
            pkv = st.enter_context(nc.psum_tensor(f"pkv{l}", [128, 512], F32)); rkv = Res()
            stg = [sb(f"stg{l}_{k}", [128, 512], BF16, st) for k in range(4)]
            rstg = [Res() for _ in range(4)]
            sqt = sb(f"sqt{l}", [128, 512], BF16, st); rsq = Res()
            rr = sb(f"rr{l}", [128, 512], F32, st); rrr = Res()
            kc_t = [sb(f"kcat{l}_{k}", [128, 8, 128], BF16, st) for k in range(2)]
            rkc = [Res(), Res()]
            self.norm_T(l, 0, list(range(NT)), hT, rh, st, ps_t, rpt)
            import os
            order = [2, 1, 5, 6, 0, 3, 4][:int(os.environ.get("KNB", "7"))]
            if self.chk("norm"):
                order = []
            ppi = [0]; sgi = [0]

            def next_pp():
                k = ppi[0]; ppi[0] = (k + 1) % 3
                return pp[k], rpp[k]

            def next_stg():
                k = sgi[0]; sgi[0] = (k + 1) % 4
                return stg[k], rstg[k]

            def feat_mm(w, rw, cc, tg, p, rp):
                def f(e):
                    ins = None
                    for kc in range(8):
                        ins = e.matmul(p[:], lhsT=w[:, kc, cc * 128:(cc + 1) * 128], rhs=hT[:, kc, tg * 512:(tg + 1) * 512],
                                       start=(kc == 0), stop=(kc == 7))
                    return ins
                sc.op("pe", [rw, rh], [rp], f)

            def tok_mm(w, rw, i, p, rp):
                def f(e):
                    ins = None
                    for kc in range(8):
                        ins = e.matmul(p[:], lhsT=hT[:, kc, i * 128:(i + 1) * 128], rhs=w[:, kc, :], start=(kc == 0), stop=(kc == 7))
                    return ins
                sc.op("pe", [rw, rh], [rp], f)

            def store(dname, rows, cols, s, rs_):
                r = Res()
                self.R.setdefault(dname, []).append(r)
                sc.dma("sp", T[dname][rows[0]:rows[1], cols[0]:cols[1]], s, [rs_], [r])

            for bi0 in range(min(2, len(order))):
                self.load_w("w_in", l, order[bi0] * 512, 512, wb[bi0], rwb[bi0])
            for bi, b in enumerate(order):
                w, rw = wb[bi % 3], rwb[bi % 3]
                if bi + 2 < len(order) and not (self.mode == "fused" and bi + 2 == 6):
                    self.load_w("w_in", l, order[bi + 2] * 512, 512, wb[(bi + 2) % 3], rwb[(bi + 2) % 3])
                if b == 2:
                    for i in range(NT):
                        p, rp = next_pp()
                        tok_mm(w, rw, i, p, rp)
                        sc.op("act", [rp], [self.rV], lambda e, i=i, p=p: e.copy(out=V_all[:, i, :], in_=p[:]))
                elif b == 6:
                    for i in range(NT):
                        p, rp = next_pp()
                        tok_mm(w, rw, i, p, rp)
                        s, rs_ = next_stg()
                        sc.op("act", [rp], [rs_], lambda e, p=p, s=s: e.copy(out=s[:], in_=p[:]))
                        store(f"v_loc{l}_0", (i * 128, (i + 1) * 128), (0, 256), s[:, 0:256], rs_)
                        store(f"v_loc{l}_1", (i * 128, (i + 1) * 128), (0, 256), s[:, 256:512], rs_)
                    if self.mode == "fused":
                        self.load_w("w_in", l, order[6] * 512, 512, wb[6 % 3], rwb[6 % 3])
                        self.gather_kv(l, 0)
                        self.dup_kv(l, 0)
                elif b == 1:
                    wk8 = self.wk8
                    for i in range(NT):
                        p, rp = next_pp()
                        tok_mm(w, rw, i, p, rp)
                        k = i % 2
                        def f_kc(e, p=p, k=k):
                            pv = p[:].rearrange("p (h d) -> p h d", h=8)
                            e.tensor_tensor(out=kc_t[k][:, :, 0:64], in0=pv, in1=bc(wk8[:, 0, :], [128, 8, 64]), op=ALU.mult)
                            return e.tensor_tensor(out=kc_t[k][:, :, 64:128], in0=pv, in1=bc(wk8[:, 1, :], [128, 8, 64]), op=ALU.mult)
                        sc.op("dve", [rp, self.r_der], [rkc[k]], f_kc)
                        def f_kv(e, i=i, k=k):
                            ins = None
                            for h in range(8):
                                ins = e.matmul(pkv[:, h * 64:(h + 1) * 64], lhsT=kc_t[k][:, h, :], rhs=V_all[:, i, h * 64:(h + 1) * 64],
                                               start=True, stop=True)
                            return ins
                        sc.op("pe", [rkc[k], self.rV], [rkv], f_kv)
                        sc.op("act", [rkv], [self.rKV], lambda e, i=i: e.copy(out=KV[:, i, :], in_=pkv[:]))
                    for cc in range(4):
                        for tg in range(4):
                            p, rp = next_pp()
                            feat_mm(w, rw, cc, tg, p, rp)
                            s, rs_ = next_stg()
                            sc.op("act", [rp], [rs_], lambda e, p=p, s=s: e.mul(out=s[:], in_=p[:], mul=0.125))
                            store(f"rkT{l}", (cc * 128, (cc + 1) * 128), (tg * 512, (tg + 1) * 512), s[:], rs_)
                    self.scan_states(l)
                elif b == 0:
                    for cc in range(4):
                        for tg in range(4):
                            p, rp = next_pp()
                            feat_mm(w, rw, cc, tg, p, rp)
                            s, rs_ = next_stg()
                            sc.op("act", [rp], [rs_], lambda e, p=p, s=s: e.copy(out=s[:], in_=p[:]))
                            store(f"rqT{l}", (cc * 128, (cc + 1) * 128), (tg * 512, (tg + 1) * 512), s[:], rs_)
                            rs_raw = rs_
                            for d, nm in ((0, "rqfT"), (1, "rqbT")):
                                s, rs_ = next_stg()
                                sc.op("dve", [rp, rs_raw, self.r_der], [rs_], lambda e, p=p, s=s, d=d, cc=cc: e.tensor_tensor(
                                    out=s[:].rearrange("p (n t) -> p n t", n=4), in0=p[:].rearrange("p (n t) -> p n t", n=4),
                                    in1=self.wq2[:, d, cc, :].unsqueeze(1).to_broadcast([128, 4, 128]), op=ALU.mult))
                                store(f"{nm}{l}", (cc * 128, (cc + 1) * 128), (tg * 512, (tg + 1) * 512), s[:], rs_)
                elif b == 3:
                    for cc in range(4):
                        for tg in range(4):
                            p, rp = next_pp()
                            feat_mm(w, rw, cc, tg, p, rp)
                            s, rs_ = next_stg()
                            sc.op("act", [rp], [rs_], lambda e, p=p, s=s: e.activation(out=s[:], in_=p[:], func=AF.Silu))
                            store(f"rgT{l}", (cc * 128, (cc + 1) * 128), (tg * 512, (tg + 1) * 512), s[:], rs_)
                else:
                    dname = f"dqT{l}" if b == 4 else None
                    gsc = pcol[:, l, 1:2] if b == 4 else self.gk8[:, 0:1]
                    for cc in range(4):
                        for tg in range(4):
                            p, rp = next_pp()
                            feat_mm(w, rw, cc, tg, p, rp)
                            sc.op("act", [rp], [rsq], lambda e, p=p: e.activation(out=sqt[:], in_=p[:], func=AF.Square))
                            sc.op("pe", [rsq, rblk], [rss], lambda e: e.matmul(pss[:], lhsT=blk[:], rhs=sqt[:], start=True, stop=True))
                            sc.op("act", [rss], [rrr], lambda e: e.activation(out=rr[:], in_=pss[:], func=AF.Ln, bias=64.0 * EPS, scale=1.0))
                            sc.op("act", [rrr], [rrr], lambda e: e.activation(out=rr[:], in_=rr[:], func=AF.Exp, scale=-0.5))
                            s, rs_ = next_stg()
                            sc.op("dve", [rp, rrr, rpc, self.r_der], [rs_], lambda e, p=p, s=s, gsc=gsc: e.scalar_tensor_tensor(
                                out=s[:], in0=p[:], scalar=gsc, in1=rr[:], op0=ALU.mult, op1=ALU.mult))
                            if b == 4:
                                store(dname, (cc * 128, (cc + 1) * 128), (tg * 512, (tg + 1) * 512), s[:], rs_)
                            else:
                                store(f"kt_loc{l}_{cc // 2}", ((cc % 2) * 128, (cc % 2) * 128 + 128), (tg * 512, (tg + 1) * 512), s[:], rs_)
            sc.barrier()
        if self.stopped:
            return
        if self.mode == "fused":
            pass
        else:
            for nm in ("kt_all", "v_all"):
                for hp in (0, 1):
                    self.R.setdefault(f"{nm}{l}_{hp}", [])
            self.R.setdefault(f"st_all{l}", [])

    def gather_kv(self, l, hp):
        T, sc = self.T, self.sc
        for nm in ("kt", "v"):
            ra = Res()
            self.R[f"{nm}_all{l}_{hp}"] = [ra]
            sc.allgather(T[f"{nm}_loc{l}_{hp}"], T[f"{nm}_all{l}_{hp}"], self.R[f"{nm}_loc{l}_{hp}"], [ra])

    def dup_kv(self, l, hp):
        T, sc = self.T, self.sc
        if not hasattr(self, "_dup"):
            self._dup = {}
        for nm, rows in (("kt", 256), ("v", TL)):
            big = T[f"{nm}big{l}_{hp}"]
            al = T[f"{nm}_all{l}_{hp}"]
            rd = [Res(), Res()]
            sc.dma("pool", big[0:4 * rows, :], al, self.R[f"{nm}_all{l}_{hp}"], [rd[0]])
            sc.dma("pool", big[4 * rows:8 * rows, :], al, self.R[f"{nm}_all{l}_{hp}"], [rd[1]])
            self._dup[(l, nm, hp)] = rd

    def relayout_kv(self, l, hp):
        T, sc, nc = self.T, self.sc, self.nc
        pid = nc.sync.partition_id()
        nxt = (pid % 4) + 1
        for nm, rows in (("kt", 256), ("v", TL)):
            big = T[f"{nm}big{l}_{hp}"]
            r1 = Res()
            sc.dma("sp", T[f"rel_{nm}{l}"][hp], big[bass.ds(nxt * rows, 3 * rows), :], self._dup[(l, nm, hp)], [r1])
            self.R[f"rel_{nm}{l}_{hp}"] = [r1]

    def scan_states(self, l):
        sc, T = self.sc, self.T
        KV, rKV = self.KV, self.rKV
        dcb0 = bc(self.dccat[0:64, :], [64, 8, 64])
        dcb1 = bc(self.dccat[64:128, :], [64, 8, 64])
        with ExitStack() as st:
            tmp = self.sb(f"sctmp{l}", [128, 512], F32, st)
            rt = Res()

            def v3(ap):
                return ap.rearrange("p (h e) -> p h e", h=8)

            def f(e):
                ins = None
                for n in range(1, NT):
                    e.tensor_tensor(out=v3(tmp[0:64, :]), in0=v3(KV[0:64, n - 1, :]), in1=dcb0, op=ALU.mult)
                    e.tensor_tensor(out=KV[0:64, n, :], in0=KV[0:64, n, :], in1=tmp[0:64, :], op=ALU.add)
                    m = NT - 1 - n
                    e.tensor_tensor(out=v3(tmp[64:128, :]), in0=v3(KV[64:128, m + 1, :]), in1=dcb1, op=ALU.mult)
                    ins = e.tensor_tensor(out=KV[64:128, m, :], in0=KV[64:128, m, :], in1=tmp[64:128, :], op=ALU.add)
                return ins
            sc.op("dve", [rKV, self.r_der], [rKV, rt], f)
            rs = [Res(), Res()]
            self.R[f"st_loc{l}"] = rs
            sc.dma("sp", T[f"st_loc{l}"][0:64, :], KV[0:64, NT - 1, :], [rKV], [rs[0]])
            sc.dma("sp", T[f"st_loc{l}"][64:128, :], KV[64:128, 0, :], [rKV], [rs[1]])
            if self.mode == "fused":
                ra = Res()
                self.R[f"st_all{l}"] = [ra]
                sc.allgather(T[f"st_loc{l}"], T[f"st_all{l}"], rs, [ra])
            sc.barrier(["dve"])

    def phase_R(self, l):
        nc, sc, cst, T, sb, R = self.nc, self.sc, self.cst, self.T, self.sb, self.R
        KV, V_all = self.KV, self.V_all
        cex, rex = cst["c_ex"]; pcol, rpc = cst["pcol"]; blk, rblk = cst["c_blk"]
        with ExitStack() as st:
            Rbf = sb(f"Rbf{l}", [128, NT, 512], BF16, st); rRb = Res()
            stin = sb(f"stin{l}", [128, 4, 512], F32, st); rsi = Res()
            cf = sb(f"cf{l}", [128, 4, 8], F32, st); rcf = Res()
            C = sb(f"Cc{l}", [128, 512], F32, st); rC = Res()
            tmp = sb(f"Rtmp{l}", [128, 512], F32, st)
            sc.dma("sp", stin[:], T[f"st_all{l}"].rearrange("(r p) e -> p r e", p=128), R[f"st_all{l}"], [rsi])
            def f_c(e):
                ins = None
                for p in range(4):
                    ins = e.activation(out=cf[:, p, :], in_=self.lgcat[:], func=AF.Exp, scale=cex[:, p:p + 1])
                return ins
            sc.op("act", [self.r_der, rex], [rcf], f_c)

            def v3(ap):
                return ap.rearrange("p (h e) -> p h e", h=8)

            def f_in(e):
                for p in range(4):
                    e.tensor_scalar(out=cf[:, p, :], in0=cf[:, p, :], scalar1=cex[:, 4 + p:5 + p], scalar2=None, op0=ALU.mult)
                e.tensor_tensor(out=v3(C[:]), in0=v3(stin[:, 0, :]), in1=bc(cf[:, 0, :], [128, 8, 64]), op=ALU.mult)
                ins = None
                for p in range(1, 4):
                    e.tensor_tensor(out=v3(tmp[:]), in0=v3(stin[:, p, :]), in1=bc(cf[:, p, :], [128, 8, 64]), op=ALU.mult)
                    ins = e.tensor_tensor(out=C[:], in0=C[:], in1=tmp[:], op=ALU.add)
                return ins
            sc.op("dve", [rcf, rsi, rex], [rC, rcf], f_in)
            dcb0 = bc(self.dccat[0:64, :], [64, 8, 64])
            dcb1 = bc(self.dccat[64:128, :], [64, 8, 64])

            def f_rb(e):
                e.tensor_copy(out=Rbf[0:64, 0, :], in_=C[0:64, :])
                for n in range(1, NT):
                    e.tensor_tensor(out=v3(C[0:64, :]), in0=v3(C[0:64, :]), in1=dcb0, op=ALU.mult)
                    e.tensor_tensor(out=Rbf[0:64, n, :], in0=KV[0:64, n - 1, :], in1=C[0:64, :], op=ALU.add)
                ins = e.tensor_copy(out=Rbf[64:128, NT - 1, :], in_=C[64:128, :])
                for n in range(NT - 2, -1, -1):
                    e.tensor_tensor(out=v3(C[64:128, :]), in0=v3(C[64:128, :]), in1=dcb1, op=ALU.mult)
                    ins = e.tensor_tensor(out=Rbf[64:128, n, :], in0=KV[64:128, n + 1, :], in1=C[64:128, :], op=ALU.add)
                return ins
            sc.op("dve", [rC, self.rKV, self.r_der], [rC, rRb], f_rb)
            NB = 2
            QR = [sb(f"QR{l}_{k}", [64, TL], BF16, st) for k in range(NB)]
            KT = [sb(f"KTr{l}_{k}", [64, TL], BF16, st) for k in range(NB)]
            QC = [sb(f"QC{l}_{k}", [128, TL], BF16, st) for k in range(NB)]
            GG = [sb(f"GG{l}_{k}", [64, TL], BF16, st) for k in range(NB)]
            rin = [Res() for _ in range(NB)]
            ps_s = [st.enter_context(nc.psum_tensor(f"pss_r{l}_{k}", [128, 4, 128], F32)) for k in range(2)]
            rps = [Res(), Res()]
            ps_o = [st.enter_context(nc.psum_tensor(f"pso_r{l}_{k}", [64, 512], F32)) for k in range(2)]
            rpo = [Res(), Res()]
            ps_n = [st.enter_context(nc.psum_tensor(f"psn_r{l}_{k}", [64, 512], F32)) for k in range(2)]
            rpn = [Res(), Res()]
            sm = [sb(f"sm{l}_{k}", [128, 4, 128], BF16, st) for k in range(2)]
            rsm = [Res(), Res()]
            osb = [sb(f"osb{l}_{k}", [64, 512], F32, st) for k in range(2)]
            ros = [Res(), Res()]
            sq = [sb(f"sqr{l}_{k}", [64, 512], BF16, st) for k in range(2)]
            rsq = [Res(), Res()]
            rr = [sb(f"rrr{l}_{k}", [64, 512], F32, st) for k in range(2)]
            rrr = [Res(), Res()]
            ofin = [sb(f"ofin{l}_{k}", [64, 512], BF16, st) for k in range(2)]
            rof = [Res(), Res()]
            items = [(h, tg) for h in range(8) for tg in range(4)]

            def stage1(it):
                h, tg = items[it]
                k = h % NB
                j = it % 2
                rows = (h * 64, (h + 1) * 64)
                def loads(hh):
                    kk = hh % NB
                    rw = (hh * 64, (hh + 1) * 64)
                    sc.dma("sp", QR[kk][:], T[f"rqT{l}"][rw[0]:rw[1], :], R[f"rqT{l}"], [rin[kk]])
                    sc.dma("sp", KT[kk][:], T[f"rkT{l}"][rw[0]:rw[1], :], R[f"rkT{l}"], [rin[kk]])
                    sc.dma("sp", QC[kk][0:64, :], T[f"rqfT{l}"][rw[0]:rw[1], :], R[f"rqfT{l}"], [rin[kk]])
                    sc.dma("sp", QC[kk][64:128, :], T[f"rqbT{l}"][rw[0]:rw[1], :], R[f"rqbT{l}"], [rin[kk]])
                    sc.dma("sp", GG[kk][:], T[f"rgT{l}"][rw[0]:rw[1], :], R[f"rgT{l}"], [rin[kk]])
                if it == 0:
                    loads(0)
                if tg == 1 and h + 1 < 8:
                    loads(h + 1)

                def f_s(e):
                    ins = None
                    for c in range(4):
                        n = tg * 4 + c
                        ins = e.matmul(ps_s[j][:, c, :], lhsT=KT[k][:, n * 128:(n + 1) * 128], rhs=QR[k][:, n * 128:(n + 1) * 128],
                                       start=True, stop=True)
                    return ins
                sc.op("pe", [rin[k]], [rps[j]], f_s)
                sc.op("dve", [rps[j], self.r_der], [rsm[j]], lambda e: e.tensor_tensor(
                    out=sm[j][:], in0=ps_s[j][:], in1=self.dmask[:, h, :].unsqueeze(1).to_broadcast([128, 4, 128]), op=ALU.mult))

                def f_o(e):
                    ins = None
                    for c in range(4):
                        n = tg * 4 + c
                        e.matmul(ps_o[j][:, c * 128:(c + 1) * 128], lhsT=V_all[:, n, h * 64:(h + 1) * 64], rhs=sm[j][:, c, :],
                                 start=True, stop=False)
                        ins = e.matmul(ps_o[j][:, c * 128:(c + 1) * 128], lhsT=Rbf[:, n, h * 64:(h + 1) * 64],
                                       rhs=QC[k][:, n * 128:(n + 1) * 128], start=False, stop=True)
                    return ins
                sc.op("pe", [rsm[j], self.rV, rRb, rin[k]], [rpo[j]], f_o)
                sc.op("act", [rpo[j]], [rsq[j]], lambda e: e.activation(out=sq[j][:], in_=ps_o[j][:], func=AF.Square))
                sc.op("pe", [rsq[j], rblk], [rpn[j]], lambda e: e.matmul(ps_n[j][:], lhsT=blk[0:64, 0:64], rhs=sq[j][:], start=True, stop=True))

            def stage2(it):
                h, tg = items[it]
                k = h % NB
                j = it % 2
                rows = (h * 64, (h + 1) * 64)
                sc.op("act", [rpn[j]], [rrr[j]], lambda e: e.activation(out=rr[j][:], in_=ps_n[j][:], func=AF.Ln, bias=EPS, scale=1.0 / 64))
                sc.op("act", [rrr[j]], [rrr[j]], lambda e: e.activation(out=rr[j][:], in_=rr[j][:], func=AF.Exp, scale=-0.5))
                sc.op("dve", [rpo[j], rrr[j], rpc], [ros[j]], lambda e: e.scalar_tensor_tensor(
                    out=osb[j][:], in0=ps_o[j][:], scalar=pcol[0:64, l, 0:1], in1=rr[j][:], op0=ALU.mult, op1=ALU.mult))
                sc.op("dve", [ros[j], rin[k]], [rof[j]], lambda e: e.tensor_tensor(
                    out=ofin[j][:], in0=osb[j][:], in1=GG[k][:, tg * 512:(tg + 1) * 512], op=ALU.mult))
                ro = Res()
                R.setdefault(f"retT{l}", []).append(ro)
                sc.dma("sp", T[f"retT{l}"][rows[0]:rows[1], tg * 512:(tg + 1) * 512], ofin[j][:], [rof[j]], [ro])

            stage1(0)
            for it in range(len(items)):
                if it + 1 < len(items):
                    stage1(it + 1)
                stage2(it)
            sc.barrier()
        self.stA.close()

    def phase_B(self, l):
        nc, sc, cst, T, sb, R = self.nc, self.sc, self.cst, self.T, self.sb, self.R
        ones, rones = cst["c_ones"]
        self.relayout_kv(l, 0)
        with ExitStack() as st:
            loc = {}
            for nm, shape, dt in (("c_bt", [128, 4, 4, 64], F32), ("c_dt", [128, 2, 4, 512], BF16), ("c_mi", [128, 4, 128], BF16)):
                t_ = sb(f"k_{nm}{l}", shape, dt, st)
                r_ = Res()
                sc.dma("sp", t_[:], T[nm], [], [r_])
                loc[nm] = (t_, r_)
            bt, rbt = loc["c_bt"]; cdt, rdt = loc["c_dt"]; cmi, rmi = loc["c_mi"]
            KA = sb(f"KA{l}", [96, 2, S], BF16, st); rKA = [Res() for _ in range(4)]
            VA = sb(f"VA{l}", [128, 64, 128], BF16, st); rVA = [Res() for _ in range(4)]
            QB = sb(f"QB{l}", [96, 2, TL], BF16, st); rQ = Res()
            ps_s = [st.enter_context(nc.psum_tensor(f"bs{l}_{k}", [128, 2, 512], F32)) for k in range(2)]
            rps = [Res(), Res()]
            ps_a = st.enter_context(nc.psum_tensor(f"ba{l}", [128, 2, 512], F32)); rpa = Res()
            ps_m = st.enter_context(nc.psum_tensor(f"bm{l}", [128, 512], F32)); rpm = Res()
            ps_e = st.enter_context(nc.psum_tensor(f"be{l}", [128, 512], F32)); rpe = Res()
            oraw = sb(f"oraw{l}", [128, 2, 512], F32, st); roraw = Res()
            pending = []
            ssb = sb(f"ssb{l}", [64, 512], F32, st); rssb = Res()
            shi = sb(f"shi{l}", [64, 512], BF16, st)
            slo = sb(f"slo{l}", [64, 512], BF16, st); rhl = Res()
            P = [sb(f"P{l}_{k}", [128, 2, 512], BF16, st) for k in range(3)]
            rP = [Res() for _ in range(3)]
            rsum = sb(f"rsum{l}", [128, 2, 512], F32, st); rrs = Res()
            o12 = sb(f"o12{l}", [128, 2, 512], F32, st); ro12 = Res()
            ocb = sb(f"ocb{l}", [128, 512], F32, st); rocb = Res()
            sq = sb(f"sqb{l}", [128, 512], BF16, st); rsq = Res()
            rr = sb(f"rrb{l}", [128, 512], F32, st); rrr = Res()
            ofin = [sb(f"ofb{l}_{k}", [128, 512], BF16, st) for k in range(2)]
            rof = [Res(), Res()]
            pcnt = [0]
            for h in range(4):
                hp = h // 2
                hc = slice((h % 2) * 128, (h % 2) * 128 + 128)
                if h == 2:
                    self.relayout_kv(l, 1)
                m_h = SLOPES[h]

                def need(g_, kbr_):
                    c_, kb_ = kbr_ // 16, kbr_ % 16
                    q0, q1 = 512 * g_, 512 * g_ + 511
                    if c_ == 0:
                        d_ = max(0, q0 - (128 * kb_ + 127), 128 * kb_ - q1)
                    elif c_ == 1:
                        d_ = (TL + 128 * kb_) - q1
                    elif c_ == 3:
                        d_ = q0 - (-TL + 128 * kb_ + 127)
                    else:
                        d_ = min((2 * TL + 128 * kb_) - q1, q0 - (-2 * TL + 128 * kb_ + 127))
                    return m_h * d_ < 130.0
                blocks_g = []
                for g in range(4):
                    bl = [("L", 4 * g + jb) for jb in range(4)]
                    for c_ in (0, 1, 3, 2):
                        for kb_ in range(16):
                            kbr_ = c_ * 16 + kb_
                            if c_ == 0 and 4 * g <= kb_ <= 4 * g + 3:
                                continue
                            if need(g, kbr_):
                                bl.append(("R", kbr_))
                    blocks_g.append(bl)
                used_chunks = sorted({kbr_ // 16 for bl in blocks_g for (t_, kbr_) in bl}, key=lambda c_: (0, 1, 3, 2).index(c_))
                for m in range(2):
                    r0 = h * 128 + m * 64
                    sc.dma("sp", QB[0:64, m, :], T[f"dqT{l}"][r0:r0 + 64, :], R[f"dqT{l}"], [rQ])
                    sc.dma("sp", QB[64:96, m, :], T["c_qaug"][h], [], [rQ])
                for c_ in used_chunks:
                    cs = slice(c_ * TL, (c_ + 1) * TL)
                    for m in range(2):
                        rh = (h % 2) * 128 + m * 64
                        if c_ == 0:
                            sc.dma("sp", KA[0:64, m, cs], T[f"kt_loc{l}_{hp}"][rh:rh + 64, :], R[f"kt_loc{l}_{hp}"], [rKA[c_]])
                        else:
                            rb = (c_ - 1) * 256 + rh
                            sc.dma("sp", KA[0:64, m, cs], T[f"rel_kt{l}"][hp, rb:rb + 64, :], R[f"rel_kt{l}_{hp}"], [rKA[c_]])
                        sc.dma("sp", KA[64:96, m, cs], T["c_kaugc"][h, :, cs], [], [rKA[c_]])
                    if c_ == 0:
                        sc.dma("sp", VA[:, 0:16, :], T[f"v_loc{l}_{hp}"][:, hc].rearrange("(kb p) e -> p kb e", p=128),
                               R[f"v_loc{l}_{hp}"], [rVA[c_]])
                    else:
                        sc.dma("sp", VA[:, c_ * 16:(c_ + 1) * 16, :],
                               T[f"rel_v{l}"][hp, (c_ - 1) * TL:c_ * TL, hc].rearrange("(kb p) e -> p kb e", p=128),
                               R[f"rel_v{l}_{hp}"], [rVA[c_]])
                if h == 0 and self.mode == "fused":
                    gate = [rQ] + [rKA[c_] for c_ in used_chunks] + [rVA[c_] for c_ in used_chunks]
                    self.R[f"kt_loc{l}_1"] = self.R[f"kt_loc{l}_1"] + gate
                    self.gather_kv(l, 1)
                    self.dup_kv(l, 1)
                for g in range(4):
                    gc = slice(g * 512, (g + 1) * 512)
                    blocks = blocks_g[g]
                    nb = len(blocks)

                    def emit_qk(bi):
                        typ, kb = blocks[bi]
                        j = bi % 2
                        kc = slice(kb * 128, (kb + 1) * 128)
                        if typ == "R":
                            def f_qk(e):
                                e.matmul(ps_s[j][:, 0, :], lhsT=KA[0:96, 0, kc], rhs=QB[0:96, 0, gc], start=True, stop=True)
                                return e.matmul(ps_s[j][:, 1, :], lhsT=KA[0:96, 1, kc], rhs=QB[0:96, 1, gc], start=True, stop=True)
                            sc.op("pe", [rKA[kb // 16], rQ], [rps[j]], f_qk)
                        else:
                            jb = kb - 4 * g
                            def f_qk(e):
                                ins = None
                                for m in range(2):
                                    e.matmul(ps_s[j][:, m, :], lhsT=KA[0:64, m, kc], rhs=QB[0:64, m, gc], start=True, stop=False)
                                    e.matmul(ps_s[j][:, m, :], lhsT=cmi[:, h, :], rhs=cdt[:, 0, jb, :], start=False, stop=False)
                                    ins = e.matmul(ps_s[j][:, m, :], lhsT=cmi[:, h, :], rhs=cdt[:, 1, jb, :], start=False, stop=True)
                                return ins
                            sc.op("pe", [rKA[0], rQ, rmi, rdt], [rps[j]], f_qk)

                    def emit_exp(bi):
                        typ, kb = blocks[bi]
                        j = bi % 2
                        pj = pcnt[0] % 3
                        bias = bt[:, h, g, kb:kb + 1] if typ == "R" else 0.0
                        sc.op("act", [rps[j], rbt], [rP[pj]], lambda e: e.activation(
                            out=P[pj][:], in_=ps_s[j][:], func=AF.Exp, bias=bias, scale=1.0))

                    def emit_av(bi):
                        typ, kb = blocks[bi]
                        pj = pcnt[0] % 3
                        pcnt[0] += 1
                        vt, rvt = VA, rVA[kb // 16]
                        def f_av(e):
                            for m in range(2):
                                e.matmul(ps_a[:, m, :], lhsT=vt[:, kb, :], rhs=P[pj][:, m, :], start=(bi == 0), stop=(bi == nb - 1))
                            e.matmul(ps_m[0:32, :], lhsT=ones[:, 0:32], rhs=P[pj][:, 0, :], start=(bi == 0), stop=(bi == nb - 1),
                                     tile_position=(0, 0))
                            return e.matmul(ps_m[32:64, :], lhsT=ones[:, 32:64], rhs=P[pj][:, 1, :], start=(bi == 0), stop=(bi == nb - 1),
                                            tile_position=(0, 32))
                        sc.op("pe", [rP[pj], rvt, rones], [rpa, rpm], f_av)

                    emit_qk(0)
                    for bi in range(nb):
                        if bi + 1 < nb:
                            emit_qk(bi + 1)
                        emit_exp(bi)
                        emit_av(bi)
                        if pending and bi >= 1:
                            pending.pop(0)()
                    sc.op("dve", [rpa], [roraw], lambda e: e.tensor_copy(out=oraw[:], in_=ps_a[:]))
                    sc.op("dve", [rpm], [rssb], lambda e: e.tensor_copy(out=ssb[:], in_=ps_m[0:64, :]))
                    sc.op("dve", [rssb], [rhl], lambda e: e.tensor_copy(out=shi[:], in_=ssb[:]))
                    sc.op("dve", [rssb, rhl], [rhl], lambda e: e.tensor_tensor(out=slo[:], in0=ssb[:], in1=shi[:], op=ALU.subtract))

                    def mk_rep(m):
                        def stage():
                            def f_rep(e):
                                e.matmul(ps_e[:], lhsT=ones[32 * m:32 * m + 32, :], rhs=shi[32 * m:32 * m + 32, :], start=True, stop=False)
                                return e.matmul(ps_e[:], lhsT=ones[32 * m:32 * m + 32, :], rhs=slo[32 * m:32 * m + 32, :], start=False, stop=True)
                            sc.op("pe", [rhl, rones], [rpe], f_rep)
                            sc.op("dve", [rpe], [rrs], lambda e: e.reciprocal(out=rsum[:, m, :], in_=ps_e[:]))
                        return stage

                    def stage3():
                        sc.op("dve", [roraw, rrs], [ro12], lambda e: e.scalar_tensor_tensor(
                            out=o12[:], in0=oraw[:], scalar=32.0, in1=rsum[:], op0=ALU.mult, op1=ALU.mult))
                        sc.op("dve", [ro12, self.r_der], [rocb], lambda e: e.scalar_tensor_tensor(
                            out=ocb[:], in0=o12[:, 1, :], scalar=self.lam[:, 1:2], in1=o12[:, 0, :], op0=ALU.mult, op1=ALU.add))
                        sc.op("act", [rocb], [rsq], lambda e: e.activation(out=sq[:], in_=ocb[:], func=AF.Square))

                    def stage4():
                        sc.op("pe", [rsq, rones], [rpe], lambda e: e.matmul(ps_e[:], lhsT=ones[:], rhs=sq[:], start=True, stop=True))
                        sc.op("act", [rpe], [rrr], lambda e: e.activation(out=rr[:], in_=ps_e[:], func=AF.Ln, bias=EPS, scale=1.0 / 128))
                        sc.op("act", [rrr], [rrr], lambda e: e.activation(out=rr[:], in_=rr[:], func=AF.Exp, scale=-0.5))

                    def mk_fin(h=h, g=g, gc=gc):
                        def stage():
                            fo = (h * 4 + g) % 2
                            sc.op("dve", [rocb, rrr, self.r_der], [rof[fo]], lambda e: e.scalar_tensor_tensor(
                                out=ofin[fo][:], in0=ocb[:], scalar=self.gdf[:, 0:1], in1=rr[:], op0=ALU.mult, op1=ALU.mult))
                            ro = Res()
                            R.setdefault(f"daT{l}", []).append(ro)
                            sc.dma("sp", T[f"daT{l}"][h * 128:(h + 1) * 128, gc], ofin[fo][:], [rof[fo]], [ro])
                        return stage
                    while pending:
                        pending.pop(0)()
                    pending.extend([mk_rep(0), mk_rep(1), stage3, stage4, mk_fin()])
            while pending:
                pending.pop(0)()
            sc.barrier()

    def phase_C(self, l):
        nc, sc, cst, T, sb, R = self.nc, self.sc, self.cst, self.T, self.sb, self.R
        xs = self.xs
        stF = ExitStack()
        wd = sb(f"wd{l}", [128, NF, D], BF16, stF); rwd = Res()
        wdr = T["w_down"][l].rearrange("(f p) n -> p f n", p=128)
        with ExitStack() as st:
            OT = sb(f"OT{l}", [128, 8, TL], BF16, st); rOT = Res()
            wo = sb(f"wo{l}", [128, 8, D], BF16, st); rwo = Res()
            pp = [st.enter_context(nc.psum_tensor(f"cp{l}_{k}", [128, 2, 512], F32)) for k in range(2)]
            rpp = [Res(), Res()]
            for c in range(4):
                sc.dma("sp", OT[:, c, :], T[f"retT{l}"][c * 128:(c + 1) * 128, :], R[f"retT{l}"], [rOT])
                sc.dma("sp", OT[:, 4 + c, :], T[f"daT{l}"][c * 128:(c + 1) * 128, :], R[f"daT{l}"], [rOT])
            self.load_w("w_out", l, 0, 512, wo, rwo)
            src = T["w_out"][l].rearrange("(kc p) n -> p kc n", p=128)[:, :, 512:1024]
            sc.dma("pool", wo[:, :, 512:1024], src, [], [rwo])
            for f0 in range(0, NF, 2):
                sc.dma("pool", wd[:, f0:f0 + 2, :], wdr[:, f0:f0 + 2, :], [], [rwd])
            for i in range(NT):
                j = i % 2
                def f(e, i=i, j=j):
                    ins = None
                    for nb_ in range(2):
                        for c in range(8):
                            ins = e.matmul(pp[j][:, nb_, :], lhsT=OT[:, c, i * 128:(i + 1) * 128], rhs=wo[:, c, nb_ * 512:(nb_ + 1) * 512],
                                           start=(c == 0), stop=(c == 7))
                    return ins
                sc.op("pe", [rOT, rwo], [rpp[j]], f)
                sc.op("dve", [rpp[j], self.xs_r[i]], [self.xs_r[i]], lambda e, i=i, j=j: e.tensor_tensor(
                    out=xs[:, i, :], in0=xs[:, i, :], in1=pp[j][:].rearrange("p a b -> p (a b)"), op=ALU.add))
            sc.barrier()
        with stF as st:
            wg = [sb(f"wg{l}_{k}", [128, 8, 256], BF16, st) for k in range(2)]
            wu = [sb(f"wu{l}_{k}", [128, 8, 256], BF16, st) for k in range(2)]
            rwg = [Res(), Res()]
            h2 = sb(f"h2{l}", [128, 8, 512], BF16, st); rh2 = Res()
            aT = sb(f"aT{l}", [128, NF, 512], BF16, st); raT = Res()
            ps_t = st.enter_context(nc.psum_tensor(f"fpt{l}", [128, 8, 128], BF16)); rpt = Res()
            pg = [st.enter_context(nc.psum_tensor(f"fpg{l}_{k}", [128, 2, 512], F32)) for k in range(2)]
            rpg = [Res(), Res()]
            pd = [st.enter_context(nc.psum_tensor(f"fpd{l}_{k}", [128, 512], F32)) for k in range(2)]
            rpd = [Res(), Res()]
            sg = [sb(f"fsg{l}_{k}", [128, 512], F32, st) for k in range(2)]
            rsg = [Res(), Res()]
            NP = NF // 2
            wsc = self.T.get(f"wgu{l}")
            rws = [[Res(), Res()] for _ in range(NP)]
            seq = [(tg, fp) for tg in range(4) for fp in range(NP)]

            def issue_load(idx):
                tg_, fp_ = seq[idx]
                k_ = idx % 2
                if tg_ == 0:
                    self.load_w("w_gate", l, fp_ * 256, 256, wg[k_], rwg[k_])
                    self.load_w("w_up", l, fp_ * 256, 256, wu[k_], rwg[k_])
                    sc.dma("sp", wsc[fp_, 0], wg[k_][:], [rwg[k_]], [rws[fp_][0]])
                    sc.dma("sp", wsc[fp_, 1], wu[k_][:], [rwg[k_]], [rws[fp_][1]])
                else:
                    sc.dma("sp", wg[k_][:], wsc[fp_, 0], [rws[fp_][0]], [rwg[k_]])
                    sc.dma("sp", wu[k_][:], wsc[fp_, 1], [rws[fp_][1]], [rwg[k_]])
            issue_load(0)
            nscr = self.norm_scratch(l, 1, "f", 4, st)
            for tg in range(4):
                with ExitStack() as st2:
                    self.norm_T(l, 1, list(range(tg * 4, tg * 4 + 4)), h2, rh2, st2, ps_t, rpt, scratch=nscr)
                    for fp in range(NP):
                        idx = tg * NP + fp
                        k = idx % 2
                        if idx + 1 < len(seq):
                            issue_load(idx + 1)
                        for s_ in range(2):
                            f_ = fp * 2 + s_
                            j = f_ % 2
                            def fm(e, k=k, s_=s_, j=j):
                                ins = None
                                for wi_, wt in enumerate((wg[k], wu[k])):
                                    for kc in range(8):
                                        ins = e.matmul(pg[j][:, wi_, :], lhsT=wt[:, kc, s_ * 128:(s_ + 1) * 128], rhs=h2[:, kc, :],
                                                       start=(kc == 0), stop=(kc == 7))
                                return ins
                            sc.op("pe", [rwg[k], rh2], [rpg[j]], fm)
                            sc.op("act", [rpg[j]], [rsg[j]], lambda e, j=j: e.activation(out=sg[j][:], in_=pg[j][:, 0, :], func=AF.Silu))
                            sc.op("dve", [rsg[j], rpg[j]], [raT], lambda e, j=j, f_=f_: e.tensor_tensor(
                                out=aT[:, f_, :], in0=sg[j][:], in1=pg[j][:, 1, :], op=ALU.mult))
                    for ti in range(4):
                        i = tg * 4 + ti
                        for hf in range(2):
                            j = (ti * 2 + hf) % 2
                            def fd(e, ti=ti, hf=hf, j=j):
                                ins = None
                                for f_ in range(NF):
                                    ins = e.matmul(pd[j][:], lhsT=aT[:, f_, ti * 128:(ti + 1) * 128], rhs=wd[:, f_, hf * 512:(hf + 1) * 512],
                                                   start=(f_ == 0), stop=(f_ == NF - 1))
                                return ins
                            sc.op("pe", [raT, rwd], [rpd[j]], fd)
                            sc.op("dve", [rpd[j], self.xs_r[i]], [self.xs_r[i]], lambda e, i=i, hf=hf, j=j: e.tensor_tensor(
                                out=xs[:, i, hf * 512:(hf + 1) * 512], in0=xs[:, i, hf * 512:(hf + 1) * 512], in1=pd[j][:], op=ALU.add))
            sc.barrier()


def _bf(a):
    return np.ascontiguousarray(a.astype(NPBF))


def _consts(core):
    r = core % 4
    c = {}
    c["c_ident"] = _bf(np.eye(128, dtype=np.float32))
    c["c_ones"] = _bf(np.ones((128, 128), np.float32))
    blk = np.zeros((128, 128), np.float32)
    blk[:64, :64] = 1; blk[64:, 64:] = 1
    c["c_blk"] = _bf(blk)
    s = np.arange(128)[:, None]; t = np.arange(128)[None, :]
    dm = np.zeros((128, 4, 128), np.float32)
    dm[:, 0] = np.maximum(t - s, 0); dm[:, 1] = (t >= s); dm[:, 2] = np.maximum(s - t, 0); dm[:, 3] = (s > t)
    c["c_dm"] = dm
    pos = np.zeros((128, 2, 128), np.float32)
    pos[:, 0] = np.arange(128) + 1; pos[:, 1] = 128 - np.arange(128)
    c["c_pos"] = pos
    pp = np.zeros((128, 2), np.float32)
    pp[:, 0] = 127 - np.arange(128); pp[:, 1] = np.arange(128)
    c["c_ppos"] = pp
    ex = np.zeros((128, 8), np.float32)
    for p in range(4):
        if p < r:
            ex[:64, p] = 128.0 * 16 * (r - 1 - p); ex[:64, 4 + p] = 1
        if p > r:
            ex[64:, p] = 128.0 * 16 * (p - r - 1); ex[64:, 4 + p] = 1
    c["c_ex"] = ex
    tl = np.arange(TL); i = tl % 512; gq = tl // 512
    qaug = np.zeros((4, 32, TL), np.float32)
    kaugc = np.zeros((4, 32, S), np.float32)
    col = np.arange(S); cc_ = col // TL; tok = col % TL
    s_abs = ((r + cc_) % 4) * TL + tok
    S0 = 128 * (s_abs // 128)
    bt = np.zeros((128, 4, 4, 64), np.float32)
    mi = np.zeros((128, 4, 128), np.float32)
    kbr = np.arange(64)
    S0b = ((r + kbr // 16) % 4) * TL + 128 * (kbr % 16)
    for h in range(4):
        m = SLOPES[h]
        for g in range(4):
            sel = (gq == g)
            qaug[h, 4 * g + 0] = np.where(sel, -m * 256 * (i // 256), 0.0)
            qaug[h, 4 * g + 1] = np.where(sel, -m * (i % 256), 0.0)
            qaug[h, 4 * g + 2] = np.where(sel, 1.0, 0.0)
            qaug[h, 4 * g + 3] = np.where(sel, 1.0, 0.0)
            T0 = r * TL + 512 * g
            sig = np.where(S0 + 128 <= T0, 1.0, np.where(S0 >= T0 + 512, -1.0, 0.0))
            kaugc[h, 4 * g + 0] = sig; kaugc[h, 4 * g + 1] = sig; kaugc[h, 4 * g + 2] = sig * m * (tok % 128)
            kaugc[h, 4 * g + 3] = np.where(sig == 0.0, MASKV, 0.0)
            bt[:, h, g, :] = -m * np.abs(T0 - S0b)[None, :]
        mi[:, h, :] = m * np.eye(128)
    c["c_qaug"] = _bf(qaug); c["c_kaugc"] = _bf(kaugc)
    c["c_bt"] = bt; c["c_mi"] = _bf(mi)
    dt = np.zeros((128, 2, 4, 512), np.float32)
    jj = np.arange(128)[:, None]; ii = np.arange(512)[None, :]
    for jb in range(4):
        v = np.abs(ii - 128 * jb - jj)
        dt[:, 0, jb] = -(2 * (v // 2)); dt[:, 1, jb] = -(v % 2)
    c["c_dt"] = _bf(dt)
    return c


def _params(inp):
    f = lambda a: np.asarray(a, dtype=np.float32)
    p = {}
    g_attn = f(inp["attn_norm_g"]); g_ffn = f(inp["ffn_norm_g"])
    gcol = np.zeros((128, 2, 2, 8), np.float32)
    for l in range(2):
        gcol[:, l, 0, :] = g_attn[l].reshape(8, 128).T
        gcol[:, l, 1, :] = g_ffn[l].reshape(8, 128).T
    p["gcol"] = gcol
    pcol = np.zeros((128, 2, 4), np.float32)
    idx = np.arange(128) % 64
    for l in range(2):
        pcol[:, l, 0] = f(inp["ret_norm_g"])[l][idx]
        pcol[:, l, 1] = f(inp["dq_norm_g"])[l][idx]
        pcol[:, l, 2] = f(inp["dk_norm_g"])[l][idx]
        pcol[:, l, 3] = f(inp["diff_norm_g"])[l]
    p["pcol"] = pcol
    df = f(inp["ret_decay_fwd"]); db = f(inp["ret_decay_bwd"])
    dech = np.zeros((128, 2, 2, 8), np.float32)
    decp = np.zeros((128, 2, 2, 4), np.float32)
    for l in range(2):
        dech[:, l, 0, :] = df[l][None, :]; dech[:, l, 1, :] = db[l][None, :]
        for cc in range(4):
            decp[:64, l, 0, cc] = df[l][2 * cc]; decp[64:, l, 0, cc] = df[l][2 * cc + 1]
            decp[:64, l, 1, cc] = db[l][2 * cc]; decp[64:, l, 1, cc] = db[l][2 * cc + 1]
    p["dech"] = dech; p["decp"] = decp
    lamv = np.zeros((128, 2, 4, 64), np.float32)
    for l in range(2):
        for k, nm in enumerate(("lambda_q1", "lambda_k1", "lambda_q2", "lambda_k2")):
            lamv[:, l, k, :] = f(inp[nm])[l][None, :]
    p["lamv"] = lamv
    for nm in ("w_in", "w_out", "w_gate", "w_up", "w_down"):
        p[nm] = np.ascontiguousarray(f(inp[nm]))
    return p


_NC_CACHE = {}


def _get_nc(mode):
    if mode not in _NC_CACHE:
        _NC_CACHE[mode] = Builder(mode).build()
    return _NC_CACHE[mode]


def _gather(res, name, group):
    return np.concatenate([np.asarray(res[c][name]) for c in group], axis=0)


FUSED = True


def kernel(**inputs):
    x = np.asarray(inputs["x"], dtype=np.float32)
    p = _params(inputs)
    base = []
    for c in range(NCORE):
        d = dict(p)
        d.update(_consts(c))
        b, r = c // 4, c % 4
        d["x"] = np.ascontiguousarray(x[b, r * TL:(r + 1) * TL, :])
        base.append(d)
    cores = list(range(NCORE))
    if FUSED:
        res = run_bass_kernel_spmd(_get_nc("fused"), base, core_ids=cores).results
    else:
        xk = [f"{a}{{l}}_{hp}" for a in ("kt", "v") for hp in (0, 1)]
        def xfer(prev, l):
            outs = []
            for c in range(NCORE):
                grp = [4 * (c // 4) + k for k in range(4)]
                d = dict(base[c])
                for a in ("kt", "v"):
                    for hp in (0, 1):
                        d[f"{a}_all{l}_{hp}"] = _gather(prev, f"{a}_loc{l}_{hp}", grp)
                d[f"st_all{l}"] = _gather(prev, f"st_loc{l}", grp)
                outs.append(d)
            return outs
        r1 = run_bass_kernel_spmd(_get_nc("L1"), base, core_ids=cores).results
        in2 = xfer(r1, 0)
        r2 = run_bass_kernel_spmd(_get_nc("L2"), in2, core_ids=cores).results
        in3 = xfer(r2, 1)
        for c in range(NCORE):
            in3[c]["x"] = np.asarray(r2[c]["y"])
        res = run_bass_kernel_spmd(_get_nc("L3"), in3, core_ids=cores).results
    out = np.zeros((2, S, D), np.float32)
    for c in range(NCORE):
        b, r = c // 4, c % 4
        out[b, r * TL:(r + 1) * TL, :] = np.asarray(res[c]["y"])
    return out
```

```python
import math
from contextlib import ExitStack

import numpy as np
import ml_dtypes

import concourse.bass as bass
import concourse.mybir as mybir
from concourse.bass_utils import run_bass_kernel_spmd

F32 = mybir.dt.float32
BF16 = mybir.dt.bfloat16
AF = mybir.ActivationFunctionType
ALU = mybir.AluOpType
AX = mybir.AxisListType
NPBF = ml_dtypes.bfloat16

D = 1024
S = 8192
NCORE = 8
TL = 2048
NT = 16
DIN = 3584
DFF = 2816
NF = 22
EPS = 1e-6
SLOPES = [2.0 ** (-8.0 * (i + 1) / 4) for i in range(4)]
MASKV = -30000.0


class Res:
    __slots__ = ("w", "rd")

    def __init__(self):
        self.w = None
        self.rd = {}


class Sched:
    def __init__(self, nc, es, ndma=12):
        self.nc = nc
        self.eng = {"pe": nc.tensor, "act": nc.scalar, "dve": nc.vector, "pool": nc.gpsimd, "sp": nc.sync}
        self.sem = {k: es.enter_context(nc.semaphore("s_" + k)) for k in ("pe", "act", "dve", "pool")}
        self.cnt = {k: 0 for k in self.sem}
        self.seen = {e: {} for e in self.eng}
        self.dsem = {q: [es.enter_context(nc.semaphore(f"d_{q}{i}")) for i in range(ndma)] for q in ("sp", "pool")}
        self.dcnt = {q: [0] * ndma for q in self.dsem}
        self.drr = {q: 0 for q in self.dsem}
        self.ccsem = [es.enter_context(nc.semaphore(f"cc{i}")) for i in range(10)]
        self.ncc = 0

    def _need(self, eng, toks):
        best = {}
        for (s, v) in toks:
            if best.get(s, 0) < v:
                best[s] = v
        for s, v in best.items():
            if self.seen[eng].get(s, 0) >= v:
                continue
            self.eng[eng].wait_ge(s, v)
            self.seen[eng][s] = v

    def _deps(self, eng, reads, writes):
        toks = []
        for r in reads:
            if r.w is not None:
                toks.append(r.w)
        for w in writes:
            if w.w is not None:
                toks.append(w.w)
            toks.extend(w.rd.items())
        if eng == "pe":
            toks = [t for t in toks if t[0] is not self.sem["pe"]]
        return toks

    def _mark(self, tok, reads, writes):
        for r in reads:
            if r.rd.get(tok[0], 0) < tok[1]:
                r.rd[tok[0]] = tok[1]
        for w in writes:
            w.w = tok
            w.rd = {}

    def op(self, eng, reads, writes, fn):
        self._need(eng, self._deps(eng, reads, writes))
        ins = fn(self.eng[eng])
        self.cnt[eng] += 1
        ins.then_inc(self.sem[eng], 1)
        tok = (self.sem[eng], self.cnt[eng])
        self._mark(tok, reads, writes)
        return tok

    def dma(self, q, out, in_, reads, writes):
        toks = self._deps(q, reads, writes)
        k = self.drr[q]
        self.drr[q] = (k + 1) % len(self.dsem[q])
        if self.dcnt[q][k] > 0:
            toks.append((self.dsem[q][k], self.dcnt[q][k]))
        self._need(q, toks)
        ins = self.eng[q].dma_start(out=out, in_=in_)
        self.dcnt[q][k] += 16
        ins.then_inc(self.dsem[q][k], 16)
        tok = (self.dsem[q][k], self.dcnt[q][k])
        self._mark(tok, reads, writes)
        return tok

    def allgather(self, in_ap, out_ap, reads, writes):
        self._need("pool", self._deps("pool", reads, writes))
        s = self.ccsem[self.ncc]
        self.ncc += 1
        ins = self.nc.gpsimd.collective_compute(
            "AllGather", ALU.bypass, replica_groups=[[0, 1, 2, 3], [4, 5, 6, 7]], ins=[in_ap], outs=[out_ap])
        ins.then_inc(s, 1)
        tok = (s, 1)
        self._mark(tok, reads, writes)
        return tok

    def all_tokens(self):
        toks = [(self.sem[k], self.cnt[k]) for k in self.sem if self.cnt[k] > 0]
        for q in self.dsem:
            toks += [(s, c) for s, c in zip(self.dsem[q], self.dcnt[q]) if c > 0]
        return toks

    def barrier(self, engines=None):
        toks = self.all_tokens()
        for e in (engines or self.eng):
            self._need(e, toks)


def bc(ap, shape):
    return ap.unsqueeze(len(ap.shape)).to_broadcast(list(shape))


class _Stop(Exception):
    pass


class Builder:
    stopped = False

    def chk(self, tag):
        import os
        if os.environ.get("KSTOP", "") == tag:
            self.stopped = True
        return self.stopped

    def __init__(self, mode, dbg=False):
        self.mode = mode
        self.dbg = dbg
        self.nc = bass.Bass("TRN2", target_bir_lowering=False)
        self.T = {}

    def din(self, name, shape, dt=F32):
        t = self.nc.dram_tensor(name, list(shape), dt, kind="ExternalInput").ap()
        self.T[name] = t
        return t

    def dout(self, name, shape, dt=F32):
        t = self.nc.dram_tensor(name, list(shape), dt, kind="ExternalOutput").ap()
        self.T[name] = t
        return t

    def dscr(self, name, shape, dt=BF16):
        t = self.nc.dram_tensor(name, list(shape), dt).ap()
        self.T[name] = t
        return t

    def build(self):
        nc = self.nc
        mode = self.mode
        T = self.T
        self.din("x", [TL, D])
        self.din("w_in", [2, D, DIN]); self.din("w_out", [2, D, D])
        self.din("w_gate", [2, D, DFF]); self.din("w_up", [2, D, DFF]); self.din("w_down", [2, DFF, D])
        self.din("gcol", [128, 2, 2, 8])
        self.din("pcol", [128, 2, 4])
        self.din("decp", [128, 2, 2, 4])
        self.din("dech", [128, 2, 2, 8])
        self.din("lamv", [128, 2, 4, 64])
        self.din("c_ident", [128, 128], BF16)
        self.din("c_ones", [128, 128], BF16)
        self.din("c_blk", [128, 128], BF16)
        self.din("c_dm", [128, 4, 128])
        self.din("c_pos", [128, 2, 128])
        self.din("c_ppos", [128, 2])
        self.din("c_ex", [128, 8])
        self.din("c_qaug", [4, 32, TL], BF16)
        self.din("c_kaugc", [4, 32, S], BF16)
        self.din("c_bt", [128, 4, 4, 64])
        self.din("c_dt", [128, 2, 4, 512], BF16)
        self.din("c_mi", [128, 4, 128], BF16)
        layers = {"fused": [0, 1], "L1": [0], "L2": [0, 1], "L3": [1]}[mode]
        for l in layers:
            import os
            for nm in ("rqT", "rqfT", "rqbT", "rkT", "rgT", "dqT"):
                self.dscr(f"{nm}{l}", [512, TL])
            self.dscr(f"wgu{l}", [NF // 2, 2, 128, 8, 256])
            for nm in ("retT", "daT"):
                if os.environ.get("KDBG"):
                    self.dout(f"{nm}{l}", [512, TL], BF16)
                else:
                    self.dscr(f"{nm}{l}", [512, TL])
        def xdecl(l, loc_kind, all_kind):
            for hp in (0, 1):
                for (nm, shp, dt_) in ((f"kt_loc{l}_{hp}", [256, TL], BF16), (f"v_loc{l}_{hp}", [TL, 256], BF16)):
                    (self.dout if loc_kind == "out" else self.dscr)(nm, shp, dt_)
                if all_kind is not None:
                    for (nm, shp, dt_) in ((f"kt_all{l}_{hp}", [1024, TL], BF16), (f"v_all{l}_{hp}", [S, 256], BF16)):
                        (self.din if all_kind == "in" else self.dscr)(nm, shp, dt_)
            (self.dout if loc_kind == "out" else self.dscr)(f"st_loc{l}", [128, 512], F32)
            if all_kind is not None:
                (self.din if all_kind == "in" else self.dscr)(f"st_all{l}", [512, 512], F32)
        if mode == "fused":
            for l in (0, 1):
                for hp in (0, 1):
                    self.dscr(f"kt_loc{l}_{hp}", [256, TL], BF16); self.dscr(f"v_loc{l}_{hp}", [TL, 256], BF16)
                self.dscr(f"st_loc{l}", [128, 512], F32); self.dscr(f"st_all{l}", [512, 512], F32)
                for hp in (0, 1):
                    self.dscr(f"kt_all{l}_{hp}", [1024, TL], BF16); self.dscr(f"v_all{l}_{hp}", [S, 256], BF16)
                    self.dscr(f"ktbig{l}_{hp}", [2048, TL], BF16); self.dscr(f"vbig{l}_{hp}", [2 * S, 256], BF16)
                self.dscr(f"rel_kt{l}", [2, 768, TL], BF16); self.dscr(f"rel_v{l}", [2, 3 * TL, 256], BF16)
        elif mode == "L1":
            xdecl(0, "out", None)
        elif mode == "L2":
            xdecl(0, "scr", "in"); xdecl(1, "out", None)
        elif mode == "L3":
            xdecl(1, "scr", "in")
        if mode in ("fused", "L2", "L3"):
            self.dout("y", [TL, D])

        with ExitStack() as es:
            self.es = es
            sc = self.sc = Sched(nc, es)
            self.R = {}
            self.stA = ExitStack()

            def sb(name, shape, dt, stack=es):
                t = stack.enter_context(nc.sbuf_tensor(name, list(shape), dt))
                return t
            self.sb = sb
            xs = self.xs = sb("xs", [128, NT, D], F32)
            self.xs_r = [Res() for _ in range(NT)]
            cst = self.cst = {}
            for nm, shape, dt in (("gcol", [128, 2, 2, 8], F32), ("pcol", [128, 2, 4], F32), ("decp", [128, 2, 2, 4], F32),
                                  ("dech", [128, 2, 2, 8], F32), ("lamv", [128, 2, 4, 64], F32),
                                  ("c_ident", [128, 128], BF16), ("c_ones", [128, 128], BF16), ("c_blk", [128, 128], BF16),
                                  ("c_dm", [128, 4, 128], F32), ("c_pos", [128, 2, 128], F32), ("c_ppos", [128, 2], F32),
                                  ("c_ex", [128, 8], F32)):
                t = sb("k_" + nm, shape, dt)
                r = Res()
                sc.dma("sp", t[:], T[nm], [], [r])
                cst[nm] = (t, r)
            xr = T["x"].rearrange("(n p) d -> p n d", p=128)
            for i in range(NT):
                sc.dma("sp", xs[:, i, :], xr[:, i, :], [], [self.xs_r[i]])
            self.dmask = sb("dmask", [128, 8, 128], F32)
            self.wq2 = sb("wq2", [128, 2, 4, 128], F32)
            self.wk8 = sb("wk8", [128, 2, 8], F32)
            self.dccat = sb("dccat", [128, 8], F32)
            self.lgh = sb("lgh", [128, 2, 8], F32)
            self.lgp = sb("lgp", [128, 2, 4], F32)
            self.lgcat = sb("lgcat", [128, 8], F32)
            self.lam = sb("lam", [128, 4], F32)
            self.gk8 = sb("gk8", [128, 1], F32)
            self.gdf = sb("gdf", [128, 1], F32)
            self.r_der = Res()

            self.chk("consts")
            for l in layers:
                if self.stopped:
                    break
                do_attn = not (mode == "L1") and not (mode == "L2" and l == 1)
                self.derived(l)
                if self.chk("derived"):
                    break
                self.phase_A(l)
                if self.chk("A") or not do_attn:
                    break
                self.phase_R(l)
                if self.chk("R"):
                    break
                self.phase_B(l)
                if self.chk("B"):
                    break
                self.phase_C(l)
                if self.chk("C"):
                    break
            self.stA.close()
            if "y" in T:
                yr = T["y"].rearrange("(n p) d -> p n d", p=128)
                for i in range(NT):
                    sc.dma("sp", yr[:, i, :], xs[:, i, :], [self.xs_r[i]], [])
            sc.barrier()
        return nc

    def derived(self, l):
        sc, cst = self.sc, self.cst
        rd = self.r_der
        dech, r1 = cst["dech"]; decp, r2 = cst["decp"]; lamv, r3 = cst["lamv"]
        cdm, r4 = cst["c_dm"]; cpos, r5 = cst["c_pos"]; cpp, r6 = cst["c_ppos"]; pcol, r7 = cst["pcol"]
        lgh, lgp, lgcat = self.lgh, self.lgp, self.lgcat
        with ExitStack() as st:
            tmp = self.sb(f"dtmp{l}", [128, 2, 128], F32, st)
            tl = self.sb(f"dtl{l}", [128, 4, 64], F32, st)
            rt = Res()
            sc.op("act", [r1, r2], [rd], lambda e: (e.activation(out=lgh[:], in_=dech[:, l], func=AF.Exp),
                                                     e.activation(out=lgp[:], in_=decp[:, l], func=AF.Exp))[1])
            sc.op("dve", [rd], [rd], lambda e: (e.tensor_scalar(out=lgh[:], in0=lgh[:], scalar1=-1.0, scalar2=None, op0=ALU.mult),
                                                 e.tensor_scalar(out=lgp[:], in0=lgp[:], scalar1=-1.0, scalar2=None, op0=ALU.mult))[1])
            sc.op("dve", [rd], [rd], lambda e: (e.tensor_copy(out=lgcat[0:64, :], in_=lgh[0:64, 0, :]),
                                                 e.tensor_copy(out=lgcat[64:128, :], in_=lgh[64:128, 1, :]))[1])
            sc.op("act", [rd], [rd], lambda e: e.activation(out=self.dccat[:], in_=lgcat[:], func=AF.Exp, scale=128.0))
            def f_wk(e):
                e.tensor_scalar(out=self.wk8[:, 0, :], in0=lgh[:, 0, :], scalar1=cpp[:, 0:1], scalar2=None, op0=ALU.mult)
                return e.tensor_scalar(out=self.wk8[:, 1, :], in0=lgh[:, 1, :], scalar1=cpp[:, 1:2], scalar2=None, op0=ALU.mult)
            sc.op("dve", [rd, r6], [rd], f_wk)
            sc.op("act", [rd], [rd], lambda e: e.activation(out=self.wk8[:], in_=self.wk8[:], func=AF.Exp))
            sc.op("dve", [rd], [rd], lambda e: e.tensor_scalar(out=self.wk8[:], in0=self.wk8[:], scalar1=0.125, scalar2=None, op0=ALU.mult))
            def f_wq(e):
                ins = None
                for d in range(2):
                    for cc in range(4):
                        ins = e.activation(out=self.wq2[:, d, cc, :], in_=cpos[:, d, :], func=AF.Exp, scale=lgp[:, d, cc:cc + 1])
                return ins
            sc.op("act", [rd, r5], [rd], f_wq)
            for h in range(8):
                sc.op("act", [rd, r4], [rt], lambda e, h=h: (e.activation(out=tmp[:, 0, :], in_=cdm[:, 0, :], func=AF.Exp, scale=lgh[:, 0, h:h + 1]),
                                                              e.activation(out=tmp[:, 1, :], in_=cdm[:, 2, :], func=AF.Exp, scale=lgh[:, 1, h:h + 1]))[1])
                sc.op("dve", [rt, r4], [rt], lambda e: e.tensor_tensor(out=tmp[:, 0, :], in0=tmp[:, 0, :], in1=cdm[:, 1, :], op=ALU.mult))
                sc.op("dve", [rt, r4], [rt], lambda e: e.tensor_tensor(out=tmp[:, 1, :], in0=tmp[:, 1, :], in1=cdm[:, 3, :], op=ALU.mult))
                sc.op("dve", [rt], [rt, rd], lambda e, h=h: e.tensor_tensor(out=self.dmask[:, h, :], in0=tmp[:, 0, :], in1=tmp[:, 1, :], op=ALU.add))
            lam_init = 0.8 - 0.6 * math.exp(-0.3 * l)
            sc.op("dve", [r3], [rt], lambda e: e.tensor_tensor(out=tl[:, 0, :], in0=lamv[:, l, 0, :], in1=lamv[:, l, 1, :], op=ALU.mult))
            sc.op("dve", [r3, rt], [rt], lambda e: e.tensor_tensor(out=tl[:, 1, :], in0=lamv[:, l, 2, :], in1=lamv[:, l, 3, :], op=ALU.mult))
            sc.op("dve", [rt], [rd], lambda e: e.reduce_sum(out=self.lam[:, 2:4], in_=tl[:, 0:2, :], axis=AX.X))
            sc.op("act", [rd], [rd], lambda e: e.activation(out=self.lam[:, 2:4], in_=self.lam[:, 2:4], func=AF.Exp))
            sc.op("dve", [rd], [rd], lambda e: e.tensor_tensor(out=self.lam[:, 0:1], in0=self.lam[:, 2:3], in1=self.lam[:, 3:4], op=ALU.subtract))
            sc.op("dve", [rd], [rd], lambda e: e.tensor_scalar(out=self.lam[:, 0:1], in0=self.lam[:, 0:1], scalar1=lam_init, scalar2=None, op0=ALU.add))
            sc.op("dve", [rd], [rd], lambda e: e.tensor_scalar(out=self.lam[:, 1:2], in0=self.lam[:, 0:1], scalar1=-1.0, scalar2=None, op0=ALU.mult))
            sc.op("dve", [rd, r7], [rd], lambda e: e.tensor_scalar(out=self.gk8[:], in0=pcol[:, l, 2:3], scalar1=8.0, scalar2=None, op0=ALU.mult))
            sc.op("dve", [rd, r7], [rd], lambda e: e.tensor_scalar(out=self.gdf[:], in0=pcol[:, l, 3:4], scalar1=1.0 - lam_init, scalar2=None, op0=ALU.mult))
            sc.barrier(["act", "dve"])

    def norm_scratch(self, l, kind, tag, n, st):
        ss = self.sb(f"ss{l}{kind}{tag}", [128, n], F32, st)
        junk = self.sb(f"junk{l}{kind}{tag}", [128, D], BF16, st)
        xn = [self.sb(f"xn{l}{kind}{tag}_{k}", [128, D], BF16, st) for k in range(2)]
        return (ss, Res(), junk, Res(), xn, [Res(), Res()])

    def norm_T(self, l, kind, tiles, hT, hT_r, st, ps_t, ps_r, scratch=None):
        sc, cst, xs = self.sc, self.cst, self.xs
        gcol, rg = cst["gcol"]; ident, ri = cst["c_ident"]
        n = len(tiles)
        if scratch is None:
            scratch = self.norm_scratch(l, kind, tiles[0], n, st)
        ss, rs, junk, rj, xn, rxn = scratch
        for j, i in enumerate(tiles):
            sc.op("act", [self.xs_r[i]], [rj, rs], lambda e, i=i, j=j: e.activation(
                out=junk[:], in_=xs[:, i, :], func=AF.Square, accum_out=ss[:, j:j + 1]))
        sc.op("act", [rs], [rs], lambda e: e.activation(out=ss[:], in_=ss[:], func=AF.Ln, bias=EPS, scale=1.0 / D))
        sc.op("act", [rs], [rs], lambda e: e.activation(out=ss[:], in_=ss[:], func=AF.Exp, scale=-0.5))
        for j, i in enumerate(tiles):
            k = j % 2
            sc.op("dve", [self.xs_r[i], rs], [rxn[k]], lambda e, i=i, j=j, k=k: e.tensor_scalar(
                out=xn[k][:], in0=xs[:, i, :], scalar1=ss[:, j:j + 1], scalar2=None, op0=ALU.mult))
            def f_t(e, k=k):
                ins = None
                for c in range(8):
                    ins = e.transpose(out=ps_t[:, c, :], in_=xn[k][:, c * 128:(c + 1) * 128], identity=ident[:])
                return ins
            sc.op("pe", [rxn[k], ri], [ps_r], f_t)
            sc.op("dve", [ps_r, rg], [hT_r], lambda e, j=j: e.tensor_tensor(
                out=hT[:, :, j * 128:(j + 1) * 128], in0=ps_t[:], in1=bc(gcol[:, l, kind, :], [128, 8, 128]), op=ALU.mult))

    def load_w(self, wname, l, c0, ncols, dst, dst_r, krows=8):
        w = self.T[wname]
        src = w[l].rearrange("(kc p) n -> p kc n", p=128)[:, 0:krows, c0:c0 + ncols]
        self.sc.dma("pool", dst[:, 0:krows, 0:ncols], src, [], [dst_r])

    def phase_A(self, l):
        nc, sc, cst, T = self.nc, self.sc, self.cst, self.T
        sb = self.sb
        pcol, rpc = cst["pcol"]; blk, rblk = cst["c_blk"]
        sc.barrier()
        self.stA = ExitStack()
        stA = self.stA
        self.V_all = sb(f"V_all{l}", [128, NT, 512], BF16, stA)
        self.KV = sb(f"KV{l}", [128, NT, 512], F32, stA)
        self.rV = Res(); self.rKV = Res()
        V_all, KV = self.V_all, self.KV
        with ExitStack() as st:
            hT = sb(f"hT{l}", [128, 8, TL], BF16, st)
            rh = Res()
            wb = [sb(f"wb{l}_{k}", [128, 8, 512], BF16, st) for k in range(3)]
            rwb = [Res(), Res(), Res()]
            ps_t = st.enter_context(nc.psum_tensor(f"pst{l}", [128, 8, 128], BF16)); rpt = Res()
            pp = [st.enter_context(nc.psum_tensor(f"pp{l}_{k}", [128, 512], F32)) for k in range(3)]
            rpp = [Res() for _ in range(3)]
            pss = st.enter_context(nc.psum_tensor(f"pss{l}", [128, 512], F32)); rss = Res()
            pkv = st.enter_context(nc.psum_tensor(f"pkv{l}", [128, 512], F32)); rkv = Res()
            stg = [sb(f"stg{l}_{k}", [128, 512], BF16, st) for k in range(4)]
            rstg = [Res() for _ in range(4)]
            sqt = sb(f"sqt{l}", [128, 512], BF16, st); rsq = Res()
            rr = sb(f"rr{l}", [128, 512], F32, st); rrr = Res()
            kc_t = [sb(f"kcat{l}_{k}", [128, 8, 128], BF16, st) for k in range(2)]
            rkc = [Res(), Res()]
            self.norm_T(l, 0, list(range(NT)), hT, rh, st, ps_t, rpt)
            import os
            order = [2, 1, 5, 6, 0, 3, 4][:int(os.environ.get("KNB", "7"))]
            if self.chk("norm"):
                order = []
            ppi = [0]; sgi = [0]

            def next_pp():
                k = ppi[0]; ppi[0] = (k + 1) % 3
                return pp[k], rpp[k]

            def next_stg():
                k = sgi[0]; sgi[0] = (k + 1) % 4
                return stg[k], rstg[k]

            def feat_mm(w, rw, cc, tg, p, rp):
                def f(e):
                    ins = None
                    for kc in range(8):
                        ins = e.matmul(p[:], lhsT=w[:, kc, cc * 128:(cc + 1) * 128], rhs=hT[:, kc, tg * 512:(tg + 1) * 512],
                                       start=(kc == 0), stop=(kc == 7))
                    return ins
                sc.op("pe", [rw, rh], [rp], f)

            def tok_mm(w, rw, i, p, rp):
                def f(e):
                    ins = None
                    for kc in range(8):
                        ins = e.matmul(p[:], lhsT=hT[:, kc, i * 128:(i + 1) * 128], rhs=w[:, kc, :], start=(kc == 0), stop=(kc == 7))
                    return ins
                sc.op("pe", [rw, rh], [rp], f)

            def store(dname, rows, cols, s, rs_):
                r = Res()
                self.R.setdefault(dname, []).append(r)
                sc.dma("sp", T[dname][rows[0]:rows[1], cols[0]:cols[1]], s, [rs_], [r])

            for bi0 in range(min(2, len(order))):
                self.load_w("w_in", l, order[bi0] * 512, 512, wb[bi0], rwb[bi0])
            for bi, b in enumerate(order):
                w, rw = wb[bi % 3], rwb[bi % 3]
                if bi + 2 < len(order) and not (self.mode == "fused" and bi + 2 == 6):
                    self.load_w("w_in", l, order[bi + 2] * 512, 512, wb[(bi + 2) % 3], rwb[(bi + 2) % 3])
                if b == 2:
                    for i in range(NT):
                        p, rp = next_pp()
                        tok_mm(w, rw, i, p, rp)
                        sc.op("act", [rp], [self.rV], lambda e, i=i, p=p: e.copy(out=V_all[:, i, :], in_=p[:]))
                elif b == 6:
                    for i in range(NT):
                        p, rp = next_pp()
                        tok_mm(w, rw, i, p, rp)
                        s, rs_ = next_stg()
                        sc.op("act", [rp], [rs_], lambda e, p=p, s=s: e.copy(out=s[:], in_=p[:]))
                        store(f"v_loc{l}_0", (i * 128, (i + 1) * 128), (0, 256), s[:, 0:256], rs_)
                        store(f"v_loc{l}_1", (i * 128, (i + 1) * 128), (0, 256), s[:, 256:512], rs_)
                    if self.mode == "fused":
                        self.load_w("w_in", l, order[6] * 512, 512, wb[6 % 3], rwb[6 % 3])
                        self.gather_kv(l, 1)
                        self.dup_kv(l, 1)
                elif b == 1:
                    wk8 = self.wk8
                    for i in range(NT):
                        p, rp = next_pp()
                        tok_mm(w, rw, i, p, rp)
                        k = i % 2
                        def f_kc(e, p=p, k=k):
                            pv = p[:].rearrange("p (h d) -> p h d", h=8)
                            e.tensor_tensor(out=kc_t[k][:, :, 0:64], in0=pv, in1=bc(wk8[:, 0, :], [128, 8, 64]), op=ALU.mult)
                            return e.tensor_tensor(out=kc_t[k][:, :, 64:128], in0=pv, in1=bc(wk8[:, 1, :], [128, 8, 64]), op=ALU.mult)
                        sc.op("dve", [rp, self.r_der], [rkc[k]], f_kc)
                        def f_kv(e, i=i, k=k):
                            ins = None
                            for h in range(8):
                                ins = e.matmul(pkv[:, h * 64:(h + 1) * 64], lhsT=kc_t[k][:, h, :], rhs=V_all[:, i, h * 64:(h + 1) * 64],
                                               start=True, stop=True)
                            return ins
                        sc.op("pe", [rkc[k], self.rV], [rkv], f_kv)
                        sc.op("act", [rkv], [self.rKV], lambda e, i=i: e.copy(out=KV[:, i, :], in_=pkv[:]))
                    for cc in range(4):
                        for tg in range(4):
                            p, rp = next_pp()
                            feat_mm(w, rw, cc, tg, p, rp)
                            s, rs_ = next_stg()
                            sc.op("act", [rp], [rs_], lambda e, p=p, s=s: e.mul(out=s[:], in_=p[:], mul=0.125))
                            store(f"rkT{l}", (cc * 128, (cc + 1) * 128), (tg * 512, (tg + 1) * 512), s[:], rs_)
                    self.scan_states(l)
                elif b == 0:
                    for cc in range(4):
                        for tg in range(4):
                            p, rp = next_pp()
                            feat_mm(w, rw, cc, tg, p, rp)
                            s, rs_ = next_stg()
                            sc.op("act", [rp], [rs_], lambda e, p=p, s=s: e.copy(out=s[:], in_=p[:]))
                            store(f"rqT{l}", (cc * 128, (cc + 1) * 128), (tg * 512, (tg + 1) * 512), s[:], rs_)
                            rs_raw = rs_
                            for d, nm in ((0, "rqfT"), (1, "rqbT")):
                                s, rs_ = next_stg()
                                sc.op("dve", [rp, rs_raw, self.r_der], [rs_], lambda e, p=p, s=s, d=d, cc=cc: e.tensor_tensor(
                                    out=s[:].rearrange("p (n t) -> p n t", n=4), in0=p[:].rearrange("p (n t) -> p n t", n=4),
                                    in1=self.wq2[:, d, cc, :].unsqueeze(1).to_broadcast([128, 4, 128]), op=ALU.mult))
                                store(f"{nm}{l}", (cc * 128, (cc + 1) * 128), (tg * 512, (tg + 1) * 512), s[:], rs_)
                elif b == 3:
                    for cc in range(4):
                        for tg in range(4):
                            p, rp = next_pp()
                            feat_mm(w, rw, cc, tg, p, rp)
                            s, rs_ = next_stg()
                            sc.op("act", [rp], [rs_], lambda e, p=p, s=s: e.activation(out=s[:], in_=p[:], func=AF.Silu))
                            store(f"rgT{l}", (cc * 128, (cc + 1) * 128), (tg * 512, (tg + 1) * 512), s[:], rs_)
                else:
                    dname = f"dqT{l}" if b == 4 else None
                    gsc = pcol[:, l, 1:2] if b == 4 else self.gk8[:, 0:1]
                    for cc in range(4):
                        for tg in range(4):
                            p, rp = next_pp()
                            feat_mm(w, rw, cc, tg, p, rp)
                            sc.op("act", [rp], [rsq], lambda e, p=p: e.activation(out=sqt[:], in_=p[:], func=AF.Square))
                            sc.op("pe", [rsq, rblk], [rss], lambda e: e.matmul(pss[:], lhsT=blk[:], rhs=sqt[:], start=True, stop=True))
                            sc.op("act", [rss], [rrr], lambda e: e.activation(out=rr[:], in_=pss[:], func=AF.Ln, bias=64.0 * EPS, scale=1.0))
                            sc.op("act", [rrr], [rrr], lambda e: e.activation(out=rr[:], in_=rr[:], func=AF.Exp, scale=-0.5))
                            s, rs_ = next_stg()
                            sc.op("dve", [rp, rrr, rpc, self.r_der], [rs_], lambda e, p=p, s=s, gsc=gsc: e.scalar_tensor_tensor(
                                out=s[:], in0=p[:], scalar=gsc, in1=rr[:], op0=ALU.mult, op1=ALU.mult))
                            if b == 4:
                                store(dname, (cc * 128, (cc + 1) * 128), (tg * 512, (tg + 1) * 512), s[:], rs_)
                            else:
                                store(f"kt_loc{l}_{cc // 2}", ((cc % 2) * 128, (cc % 2) * 128 + 128), (tg * 512, (tg + 1) * 512), s[:], rs_)
            sc.barrier()
        if self.stopped:
            return
        if self.mode == "fused":
            pass
        else:
            for nm in ("kt_all", "v_all"):
                for hp in (0, 1):
                    self.R.setdefault(f"{nm}{l}_{hp}", [])
            self.R.setdefault(f"st_all{l}", [])

    def gather_kv(self, l, hp):
        T, sc = self.T, self.sc
        for nm in ("kt", "v"):
            ra = Res()
            self.R[f"{nm}_all{l}_{hp}"] = [ra]
            sc.allgather(T[f"{nm}_loc{l}_{hp}"], T[f"{nm}_all{l}_{hp}"], self.R[f"{nm}_loc{l}_{hp}"], [ra])

    def dup_kv(self, l, hp):
        T, sc = self.T, self.sc
        if not hasattr(self, "_dup"):
            self._dup = {}
        for nm, rows in (("kt", 256), ("v", TL)):
            big = T[f"{nm}big{l}_{hp}"]
            al = T[f"{nm}_all{l}_{hp}"]
            rd = [Res(), Res()]
            sc.dma("pool", big[0:4 * rows, :], al, self.R[f"{nm}_all{l}_{hp}"], [rd[0]])
            sc.dma("pool", big[4 * rows:8 * rows, :], al, self.R[f"{nm}_all{l}_{hp}"], [rd[1]])
            self._dup[(l, nm, hp)] = rd

    def relayout_kv(self, l, hp):
        T, sc, nc = self.T, self.sc, self.nc
        pid = nc.sync.partition_id()
        nxt = (pid % 4) + 1
        for nm, rows in (("kt", 256), ("v", TL)):
            big = T[f"{nm}big{l}_{hp}"]
            r1 = Res()
            sc.dma("sp", T[f"rel_{nm}{l}"][hp], big[bass.ds(nxt * rows, 3 * rows), :], self._dup[(l, nm, hp)], [r1])
            self.R[f"rel_{nm}{l}_{hp}"] = [r1]

    def scan_states(self, l):
        sc, T = self.sc, self.T
        KV, rKV = self.KV, self.rKV
        dcb0 = bc(self.dccat[0:64, :], [64, 8, 64])
        dcb1 = bc(self.dccat[64:128, :], [64, 8, 64])
        with ExitStack() as st:
            tmp = self.sb(f"sctmp{l}", [128, 512], F32, st)
            rt = Res()

            def v3(ap):
                return ap.rearrange("p (h e) -> p h e", h=8)

            def f(e):
                ins = None
                for n in range(1, NT):
                    e.tensor_tensor(out=v3(tmp[0:64, :]), in0=v3(KV[0:64, n - 1, :]), in1=dcb0, op=ALU.mult)
                    e.tensor_tensor(out=KV[0:64, n, :], in0=KV[0:64, n, :], in1=tmp[0:64, :], op=ALU.add)
                    m = NT - 1 - n
                    e.tensor_tensor(out=v3(tmp[64:128, :]), in0=v3(KV[64:128, m + 1, :]), in1=dcb1, op=ALU.mult)
                    ins = e.tensor_tensor(out=KV[64:128, m, :], in0=KV[64:128, m, :], in1=tmp[64:128, :], op=ALU.add)
                return ins
            sc.op("dve", [rKV, self.r_der], [rKV, rt], f)
            rs = [Res(), Res()]
            self.R[f"st_loc{l}"] = rs
            sc.dma("sp", T[f"st_loc{l}"][0:64, :], KV[0:64, NT - 1, :], [rKV], [rs[0]])
            sc.dma("sp", T[f"st_loc{l}"][64:128, :], KV[64:128, 0, :], [rKV], [rs[1]])
            if self.mode == "fused":
                ra = Res()
                self.R[f"st_all{l}"] = [ra]
                sc.allgather(T[f"st_loc{l}"], T[f"st_all{l}"], rs, [ra])
            sc.barrier(["dve"])

    def phase_R(self, l):
        nc, sc, cst, T, sb, R = self.nc, self.sc, self.cst, self.T, self.sb, self.R
        KV, V_all = self.KV, self.V_all
        cex, rex = cst["c_ex"]; pcol, rpc = cst["pcol"]; blk, rblk = cst["c_blk"]
        with ExitStack() as st:
            Rbf = sb(f"Rbf{l}", [128, NT, 512], BF16, st); rRb = Res()
            stin = sb(f"stin{l}", [128, 4, 512], F32, st); rsi = Res()
            cf = sb(f"cf{l}", [128, 4, 8], F32, st); rcf = Res()
            C = sb(f"Cc{l}", [128, 512], F32, st); rC = Res()
            tmp = sb(f"Rtmp{l}", [128, 512], F32, st)
            sc.dma("sp", stin[:], T[f"st_all{l}"].rearrange("(r p) e -> p r e", p=128), R[f"st_all{l}"], [rsi])
            def f_c(e):
                ins = None
                for p in range(4):
                    ins = e.activation(out=cf[:, p, :], in_=self.lgcat[:], func=AF.Exp, scale=cex[:, p:p + 1])
                return ins
            sc.op("act", [self.r_der, rex], [rcf], f_c)

            def v3(ap):
                return ap.rearrange("p (h e) -> p h e", h=8)

            def f_in(e):
                for p in range(4):
                    e.tensor_scalar(out=cf[:, p, :], in0=cf[:, p, :], scalar1=cex[:, 4 + p:5 + p], scalar2=None, op0=ALU.mult)
                e.tensor_tensor(out=v3(C[:]), in0=v3(stin[:, 0, :]), in1=bc(cf[:, 0, :], [128, 8, 64]), op=ALU.mult)
                ins = None
                for p in range(1, 4):
                    e.tensor_tensor(out=v3(tmp[:]), in0=v3(stin[:, p, :]), in1=bc(cf[:, p, :], [128, 8, 64]), op=ALU.mult)
                    ins = e.tensor_tensor(out=C[:], in0=C[:], in1=tmp[:], op=ALU.add)
                return ins
            sc.op("dve", [rcf, rsi, rex], [rC, rcf], f_in)
            dcb0 = bc(self.dccat[0:64, :], [64, 8, 64])
            dcb1 = bc(self.dccat[64:128, :], [64, 8, 64])

            def f_rb(e):
                e.tensor_copy(out=Rbf[0:64, 0, :], in_=C[0:64, :])
                for n in range(1, NT):
                    e.tensor_tensor(out=v3(C[0:64, :]), in0=v3(C[0:64, :]), in1=dcb0, op=ALU.mult)
                    e.tensor_tensor(out=Rbf[0:64, n, :], in0=KV[0:64, n - 1, :], in1=C[0:64, :], op=ALU.add)
                ins = e.tensor_copy(out=Rbf[64:128, NT - 1, :], in_=C[64:128, :])
                for n in range(NT - 2, -1, -1):
                    e.tensor_tensor(out=v3(C[64:128, :]), in0=v3(C[64:128, :]), in1=dcb1, op=ALU.mult)
                    ins = e.tensor_tensor(out=Rbf[64:128, n, :], in0=KV[64:128, n + 1, :], in1=C[64:128, :], op=ALU.add)
                return ins
            sc.op("dve", [rC, self.rKV, self.r_der], [rC, rRb], f_rb)
            NB = 2
            QR = [sb(f"QR{l}_{k}", [64, TL], BF16, st) for k in range(NB)]
            KT = [sb(f"KTr{l}_{k}", [64, TL], BF16, st) for k in range(NB)]
            QC = [sb(f"QC{l}_{k}", [128, TL], BF16, st) for k in range(NB)]
            GG = [sb(f"GG{l}_{k}", [64, TL], BF16, st) for k in range(NB)]
            rin = [Res() for _ in range(NB)]
            ps_s = [st.enter_context(nc.psum_tensor(f"pss_r{l}_{k}", [128, 4, 128], F32)) for k in range(2)]
            rps = [Res(), Res()]
            ps_o = [st.enter_context(nc.psum_tensor(f"pso_r{l}_{k}", [64, 512], F32)) for k in range(2)]
            rpo = [Res(), Res()]
            ps_n = [st.enter_context(nc.psum_tensor(f"psn_r{l}_{k}", [64, 512], F32)) for k in range(2)]
            rpn = [Res(), Res()]
            sm = [sb(f"sm{l}_{k}", [128, 4, 128], BF16, st) for k in range(2)]
            rsm = [Res(), Res()]
            osb = [sb(f"osb{l}_{k}", [64, 512], F32, st) for k in range(2)]
            ros = [Res(), Res()]
            sq = [sb(f"sqr{l}_{k}", [64, 512], BF16, st) for k in range(2)]
            rsq = [Res(), Res()]
            rr = [sb(f"rrr{l}_{k}", [64, 512], F32, st) for k in range(2)]
            rrr = [Res(), Res()]
            ofin = [sb(f"ofin{l}_{k}", [64, 512], BF16, st) for k in range(2)]
            rof = [Res(), Res()]
            items = [(h, tg) for h in range(8) for tg in range(4)]

            def stage1(it):
                h, tg = items[it]
                k = h % NB
                j = it % 2
                rows = (h * 64, (h + 1) * 64)
                def loads(hh):
                    kk = hh % NB
                    rw = (hh * 64, (hh + 1) * 64)
                    sc.dma("sp", QR[kk][:], T[f"rqT{l}"][rw[0]:rw[1], :], R[f"rqT{l}"], [rin[kk]])
                    sc.dma("sp", KT[kk][:], T[f"rkT{l}"][rw[0]:rw[1], :], R[f"rkT{l}"], [rin[kk]])
                    sc.dma("sp", QC[kk][0:64, :], T[f"rqfT{l}"][rw[0]:rw[1], :], R[f"rqfT{l}"], [rin[kk]])
                    sc.dma("sp", QC[kk][64:128, :], T[f"rqbT{l}"][rw[0]:rw[1], :], R[f"rqbT{l}"], [rin[kk]])
                    sc.dma("sp", GG[kk][:], T[f"rgT{l}"][rw[0]:rw[1], :], R[f"rgT{l}"], [rin[kk]])
                if it == 0:
                    loads(0)
                if tg == 1 and h + 1 < 8:
                    loads(h + 1)

                def f_s(e):
                    ins = None
                    for c in range(4):
                        n = tg * 4 + c
                        ins = e.matmul(ps_s[j][:, c, :], lhsT=KT[k][:, n * 128:(n + 1) * 128], rhs=QR[k][:, n * 128:(n + 1) * 128],
                                       start=True, stop=True)
                    return ins
                sc.op("pe", [rin[k]], [rps[j]], f_s)
                sc.op("dve", [rps[j], self.r_der], [rsm[j]], lambda e: e.tensor_tensor(
                    out=sm[j][:], in0=ps_s[j][:], in1=self.dmask[:, h, :].unsqueeze(1).to_broadcast([128, 4, 128]), op=ALU.mult))

                def f_o(e):
                    ins = None
                    for c in range(4):
                        n = tg * 4 + c
                        e.matmul(ps_o[j][:, c * 128:(c + 1) * 128], lhsT=V_all[:, n, h * 64:(h + 1) * 64], rhs=sm[j][:, c, :],
                                 start=True, stop=False)
                        ins = e.matmul(ps_o[j][:, c * 128:(c + 1) * 128], lhsT=Rbf[:, n, h * 64:(h + 1) * 64],
                                       rhs=QC[k][:, n * 128:(n + 1) * 128], start=False, stop=True)
                    return ins
                sc.op("pe", [rsm[j], self.rV, rRb, rin[k]], [rpo[j]], f_o)
                sc.op("act", [rpo[j]], [rsq[j]], lambda e: e.activation(out=sq[j][:], in_=ps_o[j][:], func=AF.Square))
                sc.op("pe", [rsq[j], rblk], [rpn[j]], lambda e: e.matmul(ps_n[j][:], lhsT=blk[0:64, 0:64], rhs=sq[j][:], start=True, stop=True))

            def stage2(it):
                h, tg = items[it]
                k = h % NB
                j = it % 2
                rows = (h * 64, (h + 1) * 64)
                sc.op("act", [rpn[j]], [rrr[j]], lambda e: e.activation(out=rr[j][:], in_=ps_n[j][:], func=AF.Ln, bias=EPS, scale=1.0 / 64))
                sc.op("act", [rrr[j]], [rrr[j]], lambda e: e.activation(out=rr[j][:], in_=rr[j][:], func=AF.Exp, scale=-0.5))
                sc.op("dve", [rpo[j], rrr[j], rpc], [ros[j]], lambda e: e.scalar_tensor_tensor(
                    out=osb[j][:], in0=ps_o[j][:], scalar=pcol[0:64, l, 0:1], in1=rr[j][:], op0=ALU.mult, op1=ALU.mult))
                sc.op("dve", [ros[j], rin[k]], [rof[j]], lambda e: e.tensor_tensor(
                    out=ofin[j][:], in0=osb[j][:], in1=GG[k][:, tg * 512:(tg + 1) * 512], op=ALU.mult))
                ro = Res()
                R.setdefault(f"retT{l}", []).append(ro)
                sc.dma("sp", T[f"retT{l}"][rows[0]:rows[1], tg * 512:(tg + 1) * 512], ofin[j][:], [rof[j]], [ro])

            stage1(0)
            for it in range(len(items)):
                if it + 1 < len(items):
                    stage1(it + 1)
                stage2(it)
            sc.barrier()
        self.stA.close()

    def phase_B(self, l):
        nc, sc, cst, T, sb, R = self.nc, self.sc, self.cst, self.T, self.sb, self.R
        ones, rones = cst["c_ones"]
        self.relayout_kv(l, 1)
        with ExitStack() as st:
            loc = {}
            for nm, shape, dt in (("c_bt", [128, 4, 4, 64], F32), ("c_dt", [128, 2, 4, 512], BF16), ("c_mi", [128, 4, 128], BF16)):
                t_ = sb(f"k_{nm}{l}", shape, dt, st)
                r_ = Res()
                sc.dma("sp", t_[:], T[nm], [], [r_])
                loc[nm] = (t_, r_)
            bt, rbt = loc["c_bt"]; cdt, rdt = loc["c_dt"]; cmi, rmi = loc["c_mi"]
            KA = sb(f"KA{l}", [96, 2, S], BF16, st); rKA = [Res() for _ in range(4)]
            VA = sb(f"VA{l}", [128, 64, 128], BF16, st); rVA = [Res() for _ in range(4)]
            QB = sb(f"QB{l}", [96, 2, TL], BF16, st); rQ = Res()
            ps_s = [st.enter_context(nc.psum_tensor(f"bs{l}_{k}", [128, 2, 512], F32)) for k in range(2)]
            rps = [Res(), Res()]
            ps_a = st.enter_context(nc.psum_tensor(f"ba{l}", [128, 2, 512], F32)); rpa = Res()
            ps_m = st.enter_context(nc.psum_tensor(f"bm{l}", [128, 512], F32)); rpm = Res()
            ps_e = st.enter_context(nc.psum_tensor(f"be{l}", [128, 512], F32)); rpe = Res()
            oraw = sb(f"oraw{l}", [128, 2, 512], F32, st); roraw = Res()
            pending = []
            ssb = sb(f"ssb{l}", [64, 512], F32, st); rssb = Res()
            shi = sb(f"shi{l}", [64, 512], BF16, st)
            slo = sb(f"slo{l}", [64, 512], BF16, st); rhl = Res()
            P = [sb(f"P{l}_{k}", [128, 2, 512], BF16, st) for k in range(3)]
            rP = [Res() for _ in range(3)]
            rsum = sb(f"rsum{l}", [128, 2, 512], F32, st); rrs = Res()
            o12 = sb(f"o12{l}", [128, 2, 512], F32, st); ro12 = Res()
            ocb = sb(f"ocb{l}", [128, 512], F32, st); rocb = Res()
            sq = sb(f"sqb{l}", [128, 512], BF16, st); rsq = Res()
            rr = sb(f"rrb{l}", [128, 512], F32, st); rrr = Res()
            ofin = [sb(f"ofb{l}_{k}", [128, 512], BF16, st) for k in range(2)]
            rof = [Res(), Res()]
            pcnt = [0]
            for h in (2, 3, 0, 1):
                hp = h // 2
                hc = slice((h % 2) * 128, (h % 2) * 128 + 128)
                if h == 0:
                    self.relayout_kv(l, 0)
                m_h = SLOPES[h]

                def need(g_, kbr_):
                    c_, kb_ = kbr_ // 16, kbr_ % 16
                    q0, q1 = 512 * g_, 512 * g_ + 511
                    if c_ == 0:
                        d_ = max(0, q0 - (128 * kb_ + 127), 128 * kb_ - q1)
                    elif c_ == 1:
                        d_ = (TL + 128 * kb_) - q1
                    elif c_ == 3:
                        d_ = q0 - (-TL + 128 * kb_ + 127)
                    else:
                        d_ = min((2 * TL + 128 * kb_) - q1, q0 - (-2 * TL + 128 * kb_ + 127))
                    return m_h * d_ < 130.0
                blocks_g = []
                for g in range(4):
                    bl = [("L", 4 * g + jb) for jb in range(4)]
                    for c_ in (0, 1, 3, 2):
                        for kb_ in range(16):
                            kbr_ = c_ * 16 + kb_
                            if c_ == 0 and 4 * g <= kb_ <= 4 * g + 3:
                                continue
                            if need(g, kbr_):
                                bl.append(("R", kbr_))
                    blocks_g.append(bl)
                used_chunks = sorted({kbr_ // 16 for bl in blocks_g for (t_, kbr_) in bl}, key=lambda c_: (0, 1, 3, 2).index(c_))
                for m in range(2):
                    r0 = h * 128 + m * 64
                    sc.dma("sp", QB[0:64, m, :], T[f"dqT{l}"][r0:r0 + 64, :], R[f"dqT{l}"], [rQ])
                    sc.dma("sp", QB[64:96, m, :], T["c_qaug"][h], [], [rQ])
                for c_ in used_chunks:
                    cs = slice(c_ * TL, (c_ + 1) * TL)
                    for m in range(2):
                        rh = (h % 2) * 128 + m * 64
                        if c_ == 0:
                            sc.dma("sp", KA[0:64, m, cs], T[f"kt_loc{l}_{hp}"][rh:rh + 64, :], R[f"kt_loc{l}_{hp}"], [rKA[c_]])
                        else:
                            rb = (c_ - 1) * 256 + rh
                            sc.dma("sp", KA[0:64, m, cs], T[f"rel_kt{l}"][hp, rb:rb + 64, :], R[f"rel_kt{l}_{hp}"], [rKA[c_]])
                        sc.dma("sp", KA[64:96, m, cs], T["c_kaugc"][h, :, cs], [], [rKA[c_]])
                    if c_ == 0:
                        sc.dma("sp", VA[:, 0:16, :], T[f"v_loc{l}_{hp}"][:, hc].rearrange("(kb p) e -> p kb e", p=128),
                               R[f"v_loc{l}_{hp}"], [rVA[c_]])
                    else:
                        sc.dma("sp", VA[:, c_ * 16:(c_ + 1) * 16, :],
                               T[f"rel_v{l}"][hp, (c_ - 1) * TL:c_ * TL, hc].rearrange("(kb p) e -> p kb e", p=128),
                               R[f"rel_v{l}_{hp}"], [rVA[c_]])
                if h == 2 and self.mode == "fused":
                    gate = [rQ] + [rKA[c_] for c_ in used_chunks] + [rVA[c_] for c_ in used_chunks]
                    self.R[f"kt_loc{l}_0"] = self.R[f"kt_loc{l}_0"] + gate
                    self.gather_kv(l, 0)
                    self.dup_kv(l, 0)
                for g in range(4):
                    gc = slice(g * 512, (g + 1) * 512)
                    blocks = blocks_g[g]
                    nb = len(blocks)

                    def emit_qk(bi):
                        typ, kb = blocks[bi]
                        j = bi % 2
                        kc = slice(kb * 128, (kb + 1) * 128)
                        if typ == "R":
                            def f_qk(e):
                                e.matmul(ps_s[j][:, 0, :], lhsT=KA[0:96, 0, kc], rhs=QB[0:96, 0, gc], start=True, stop=True)
                                return e.matmul(ps_s[j][:, 1, :], lhsT=KA[0:96, 1, kc], rhs=QB[0:96, 1, gc], start=True, stop=True)
                            sc.op("pe", [rKA[kb // 16], rQ], [rps[j]], f_qk)
                        else:
                            jb = kb - 4 * g
                            def f_qk(e):
                                ins = None
                                for m in range(2):
                                    e.matmul(ps_s[j][:, m, :], lhsT=KA[0:64, m, kc], rhs=QB[0:64, m, gc], start=True, stop=False)
                                    e.matmul(ps_s[j][:, m, :], lhsT=cmi[:, h, :], rhs=cdt[:, 0, jb, :], start=False, stop=False)
                                    ins = e.matmul(ps_s[j][:, m, :], lhsT=cmi[:, h, :], rhs=cdt[:, 1, jb, :], start=False, stop=True)
                                return ins
                            sc.op("pe", [rKA[0], rQ, rmi, rdt], [rps[j]], f_qk)

                    def emit_exp(bi):
                        typ, kb = blocks[bi]
                        j = bi % 2
                        pj = pcnt[0] % 3
                        bias = bt[:, h, g, kb:kb + 1] if typ == "R" else 0.0
                        sc.op("act", [rps[j], rbt], [rP[pj]], lambda e: e.activation(
                            out=P[pj][:], in_=ps_s[j][:], func=AF.Exp, bias=bias, scale=1.0))

                    def emit_av(bi):
                        typ, kb = blocks[bi]
                        pj = pcnt[0] % 3
                        pcnt[0] += 1
                        vt, rvt = VA, rVA[kb // 16]
                        def f_av(e):
                            for m in range(2):
                                e.matmul(ps_a[:, m, :], lhsT=vt[:, kb, :], rhs=P[pj][:, m, :], start=(bi == 0), stop=(bi == nb - 1))
                            e.matmul(ps_m[0:32, :], lhsT=ones[:, 0:32], rhs=P[pj][:, 0, :], start=(bi == 0), stop=(bi == nb - 1),
                                     tile_position=(0, 0))
                            return e.matmul(ps_m[32:64, :], lhsT=ones[:, 32:64], rhs=P[pj][:, 1, :], start=(bi == 0), stop=(bi == nb - 1),
                                            tile_position=(0, 32))
                        sc.op("pe", [rP[pj], rvt, rones], [rpa, rpm], f_av)

                    emit_qk(0)
                    for bi in range(nb):
                        if bi + 1 < nb:
                            emit_qk(bi + 1)
                        emit_exp(bi)
                        emit_av(bi)
                        if pending and bi >= 1:
                            pending.pop(0)()
                    sc.op("dve", [rpa], [roraw], lambda e: e.tensor_copy(out=oraw[:], in_=ps_a[:]))
                    sc.op("dve", [rpm], [rssb], lambda e: e.tensor_copy(out=ssb[:], in_=ps_m[0:64, :]))
                    sc.op("dve", [rssb], [rhl], lambda e: e.tensor_copy(out=shi[:], in_=ssb[:]))
                    sc.op("dve", [rssb, rhl], [rhl], lambda e: e.tensor_tensor(out=slo[:], in0=ssb[:], in1=shi[:], op=ALU.subtract))

                    def mk_rep(m):
                        def stage():
                            def f_rep(e):
                                e.matmul(ps_e[:], lhsT=ones[32 * m:32 * m + 32, :], rhs=shi[32 * m:32 * m + 32, :], start=True, stop=False)
                                return e.matmul(ps_e[:], lhsT=ones[32 * m:32 * m + 32, :], rhs=slo[32 * m:32 * m + 32, :], start=False, stop=True)
                            sc.op("pe", [rhl, rones], [rpe], f_rep)
                            sc.op("dve", [rpe], [rrs], lambda e: e.reciprocal(out=rsum[:, m, :], in_=ps_e[:]))
                        return stage

                    def stage3():
                        sc.op("dve", [roraw, rrs], [ro12], lambda e: e.scalar_tensor_tensor(
                            out=o12[:], in0=oraw[:], scalar=32.0, in1=rsum[:], op0=ALU.mult, op1=ALU.mult))
                        sc.op("dve", [ro12, self.r_der], [rocb], lambda e: e.scalar_tensor_tensor(
                            out=ocb[:], in0=o12[:, 1, :], scalar=self.lam[:, 1:2], in1=o12[:, 0, :], op0=ALU.mult, op1=ALU.add))
                        sc.op("act", [rocb], [rsq], lambda e: e.activation(out=sq[:], in_=ocb[:], func=AF.Square))

                    def stage4():
                        sc.op("pe", [rsq, rones], [rpe], lambda e: e.matmul(ps_e[:], lhsT=ones[:], rhs=sq[:], start=True, stop=True))
                        sc.op("act", [rpe], [rrr], lambda e: e.activation(out=rr[:], in_=ps_e[:], func=AF.Ln, bias=EPS, scale=1.0 / 128))
                        sc.op("act", [rrr], [rrr], lambda e: e.activation(out=rr[:], in_=rr[:], func=AF.Exp, scale=-0.5))

                    def mk_fin(h=h, g=g, gc=gc):
                        def stage():
                            fo = (h * 4 + g) % 2
                            sc.op("dve", [rocb, rrr, self.r_der], [rof[fo]], lambda e: e.scalar_tensor_tensor(
                                out=ofin[fo][:], in0=ocb[:], scalar=self.gdf[:, 0:1], in1=rr[:], op0=ALU.mult, op1=ALU.mult))
                            ro = Res()
                            R.setdefault(f"daT{l}", []).append(ro)
                            sc.dma("sp", T[f"daT{l}"][h * 128:(h + 1) * 128, gc], ofin[fo][:], [rof[fo]], [ro])
                        return stage
                    while pending:
                        pending.pop(0)()
                    pending.extend([mk_rep(0), mk_rep(1), stage3, stage4, mk_fin()])
            while pending:
                pending.pop(0)()
            sc.barrier()

    def phase_C(self, l):
        nc, sc, cst, T, sb, R = self.nc, self.sc, self.cst, self.T, self.sb, self.R
        xs = self.xs
        stF = ExitStack()
        wd = sb(f"wd{l}", [128, NF, D], BF16, stF); rwd = Res()
        wdr = T["w_down"][l].rearrange("(f p) n -> p f n", p=128)
        with ExitStack() as st:
            OT = sb(f"OT{l}", [128, 8, TL], BF16, st); rOT = Res()
            wo = sb(f"wo{l}", [128, 8, D], BF16, st); rwo = Res()
            pp = [st.enter_context(nc.psum_tensor(f"cp{l}_{k}", [128, 2, 512], F32)) for k in range(2)]
            rpp = [Res(), Res()]
            for c in range(4):
                sc.dma("sp", OT[:, c, :], T[f"retT{l}"][c * 128:(c + 1) * 128, :], R[f"retT{l}"], [rOT])
                sc.dma("sp", OT[:, 4 + c, :], T[f"daT{l}"][c * 128:(c + 1) * 128, :], R[f"daT{l}"], [rOT])
            self.load_w("w_out", l, 0, 512, wo, rwo)
            src = T["w_out"][l].rearrange("(kc p) n -> p kc n", p=128)[:, :, 512:1024]
            sc.dma("pool", wo[:, :, 512:1024], src, [], [rwo])
            for f0 in range(0, NF, 2):
                sc.dma("pool", wd[:, f0:f0 + 2, :], wdr[:, f0:f0 + 2, :], [], [rwd])
            for i in range(NT):
                j = i % 2
                def f(e, i=i, j=j):
                    ins = None
                    for nb_ in range(2):
                        for c in range(8):
                            ins = e.matmul(pp[j][:, nb_, :], lhsT=OT[:, c, i * 128:(i + 1) * 128], rhs=wo[:, c, nb_ * 512:(nb_ + 1) * 512],
                                           start=(c == 0), stop=(c == 7))
                    return ins
                sc.op("pe", [rOT, rwo], [rpp[j]], f)
                sc.op("dve", [rpp[j], self.xs_r[i]], [self.xs_r[i]], lambda e, i=i, j=j: e.tensor_tensor(
                    out=xs[:, i, :], in0=xs[:, i, :], in1=pp[j][:].rearrange("p a b -> p (a b)"), op=ALU.add))
            sc.barrier()
        with stF as st:
            wg = [sb(f"wg{l}_{k}", [128, 8, 256], BF16, st) for k in range(2)]
            wu = [sb(f"wu{l}_{k}", [128, 8, 256], BF16, st) for k in range(2)]
            rwg = [Res(), Res()]
            h2 = sb(f"h2{l}", [128, 8, 512], BF16, st); rh2 = Res()
            aT = sb(f"aT{l}", [128, NF, 512], BF16, st); raT = Res()
            ps_t = st.enter_context(nc.psum_tensor(f"fpt{l}", [128, 8, 128], BF16)); rpt = Res()
            pg = [st.enter_context(nc.psum_tensor(f"fpg{l}_{k}", [128, 2, 512], F32)) for k in range(2)]
            rpg = [Res(), Res()]
            pd = [st.enter_context(nc.psum_tensor(f"fpd{l}_{k}", [128, 512], F32)) for k in range(2)]
            rpd = [Res(), Res()]
            sg = [sb(f"fsg{l}_{k}", [128, 512], F32, st) for k in range(2)]
            rsg = [Res(), Res()]
            NP = NF // 2
            wsc = self.T.get(f"wgu{l}")
            rws = [[Res(), Res()] for _ in range(NP)]
            seq = [(tg, fp) for tg in range(4) for fp in range(NP)]

            def issue_load(idx):
                tg_, fp_ = seq[idx]
                k_ = idx % 2
                if tg_ == 0:
                    self.load_w("w_gate", l, fp_ * 256, 256, wg[k_], rwg[k_])
                    self.load_w("w_up", l, fp_ * 256, 256, wu[k_], rwg[k_])
                    sc.dma("sp", wsc[fp_, 0], wg[k_][:], [rwg[k_]], [rws[fp_][0]])
                    sc.dma("sp", wsc[fp_, 1], wu[k_][:], [rwg[k_]], [rws[fp_][1]])
                else:
                    sc.dma("sp", wg[k_][:], wsc[fp_, 0], [rws[fp_][0]], [rwg[k_]])
                    sc.dma("sp", wu[k_][:], wsc[fp_, 1], [rws[fp_][1]], [rwg[k_]])
            issue_load(0)
            nscr = self.norm_scratch(l, 1, "f", 4, st)
            for tg in range(4):
                with ExitStack() as st2:
                    self.norm_T(l, 1, list(range(tg * 4, tg * 4 + 4)), h2, rh2, st2, ps_t, rpt, scratch=nscr)
                    for fp in range(NP):
                        idx = tg * NP + fp
                        k = idx % 2
                        if idx + 1 < len(seq):
                            issue_load(idx + 1)
                        for s_ in range(2):
                            f_ = fp * 2 + s_
                            j = f_ % 2
                            def fm(e, k=k, s_=s_, j=j):
                                ins = None
                                for wi_, wt in enumerate((wg[k], wu[k])):
                                    for kc in range(8):
                                        ins = e.matmul(pg[j][:, wi_, :], lhsT=wt[:, kc, s_ * 128:(s_ + 1) * 128], rhs=h2[:, kc, :],
                                                       start=(kc == 0), stop=(kc == 7))
                                return ins
                            sc.op("pe", [rwg[k], rh2], [rpg[j]], fm)
                            sc.op("act", [rpg[j]], [rsg[j]], lambda e, j=j: e.activation(out=sg[j][:], in_=pg[j][:, 0, :], func=AF.Silu))
                            sc.op("dve", [rsg[j], rpg[j]], [raT], lambda e, j=j, f_=f_: e.tensor_tensor(
                                out=aT[:, f_, :], in0=sg[j][:], in1=pg[j][:, 1, :], op=ALU.mult))
                    for ti in range(4):
                        i = tg * 4 + ti
                        for hf in range(2):
                            j = (ti * 2 + hf) % 2
                            def fd(e, ti=ti, hf=hf, j=j):
                                ins = None
                                for f_ in range(NF):
                                    ins = e.matmul(pd[j][:], lhsT=aT[:, f_, ti * 128:(ti + 1) * 128], rhs=wd[:, f_, hf * 512:(hf + 1) * 512],
                                                   start=(f_ == 0), stop=(f_ == NF - 1))
                                return ins
                            sc.op("pe", [raT, rwd], [rpd[j]], fd)
                            sc.op("dve", [rpd[j], self.xs_r[i]], [self.xs_r[i]], lambda e, i=i, hf=hf, j=j: e.tensor_tensor(
                                out=xs[:, i, hf * 512:(hf + 1) * 512], in0=xs[:, i, hf * 512:(hf + 1) * 512], in1=pd[j][:], op=ALU.add))
            sc.barrier()


def _bf(a):
    return np.ascontiguousarray(a.astype(NPBF))


def _consts(core):
    r = core % 4
    c = {}
    c["c_ident"] = _bf(np.eye(128, dtype=np.float32))
    c["c_ones"] = _bf(np.ones((128, 128), np.float32))
    blk = np.zeros((128, 128), np.float32)
    blk[:64, :64] = 1; blk[64:, 64:] = 1
    c["c_blk"] = _bf(blk)
    s = np.arange(128)[:, None]; t = np.arange(128)[None, :]
    dm = np.zeros((128, 4, 128), np.float32)
    dm[:, 0] = np.maximum(t - s, 0); dm[:, 1] = (t >= s); dm[:, 2] = np.maximum(s - t, 0); dm[:, 3] = (s > t)
    c["c_dm"] = dm
    pos = np.zeros((128, 2, 128), np.float32)
    pos[:, 0] = np.arange(128) + 1; pos[:, 1] = 128 - np.arange(128)
    c["c_pos"] = pos
    pp = np.zeros((128, 2), np.float32)
    pp[:, 0] = 127 - np.arange(128); pp[:, 1] = np.arange(128)
    c["c_ppos"] = pp
    ex = np.zeros((128, 8), np.float32)
    for p in range(4):
        if p < r:
            ex[:64, p] = 128.0 * 16 * (r - 1 - p); ex[:64, 4 + p] = 1
        if p > r:
            ex[64:, p] = 128.0 * 16 * (p - r - 1); ex[64:, 4 + p] = 1
    c["c_ex"] = ex
    tl = np.arange(TL); i = tl % 512; gq = tl // 512
    qaug = np.zeros((4, 32, TL), np.float32)
    kaugc = np.zeros((4, 32, S), np.float32)
    col = np.arange(S); cc_ = col // TL; tok = col % TL
    s_abs = ((r + cc_) % 4) * TL + tok
    S0 = 128 * (s_abs // 128)
    bt = np.zeros((128, 4, 4, 64), np.float32)
    mi = np.zeros((128, 4, 128), np.float32)
    kbr = np.arange(64)
    S0b = ((r + kbr // 16) % 4) * TL + 128 * (kbr % 16)
    for h in range(4):
        m = SLOPES[h]
        for g in range(4):
            sel = (gq == g)
            qaug[h, 4 * g + 0] = np.where(sel, -m * 256 * (i // 256), 0.0)
            qaug[h, 4 * g + 1] = np.where(sel, -m * (i % 256), 0.0)
            qaug[h, 4 * g + 2] = np.where(sel, 1.0, 0.0)
            qaug[h, 4 * g + 3] = np.where(sel, 1.0, 0.0)
            T0 = r * TL + 512 * g
            sig = np.where(S0 + 128 <= T0, 1.0, np.where(S0 >= T0 + 512, -1.0, 0.0))
            kaugc[h, 4 * g + 0] = sig; kaugc[h, 4 * g + 1] = sig; kaugc[h, 4 * g + 2] = sig * m * (tok % 128)
            kaugc[h, 4 * g + 3] = np.where(sig == 0.0, MASKV, 0.0)
            bt[:, h, g, :] = -m * np.abs(T0 - S0b)[None, :]
        mi[:, h, :] = m * np.eye(128)
    c["c_qaug"] = _bf(qaug); c["c_kaugc"] = _bf(kaugc)
    c["c_bt"] = bt; c["c_mi"] = _bf(mi)
    dt = np.zeros((128, 2, 4, 512), np.float32)
    jj = np.arange(128)[:, None]; ii = np.arange(512)[None, :]
    for jb in range(4):
        v = np.abs(ii - 128 * jb - jj)
        dt[:, 0, jb] = -(2 * (v // 2)); dt[:, 1, jb] = -(v % 2)
    c["c_dt"] = _bf(dt)
    return c


def _params(inp):
    f = lambda a: np.asarray(a, dtype=np.float32)
    p = {}
    g_attn = f(inp["attn_norm_g"]); g_ffn = f(inp["ffn_norm_g"])
    gcol = np.zeros((128, 2, 2, 8), np.float32)
    for l in range(2):
        gcol[:, l, 0, :] = g_attn[l].reshape(8, 128).T
        gcol[:, l, 1, :] = g_ffn[l].reshape(8, 128).T
    p["gcol"] = gcol
    pcol = np.zeros((128, 2, 4), np.float32)
    idx = np.arange(128) % 64
    for l in range(2):
        pcol[:, l, 0] = f(inp["ret_norm_g"])[l][idx]
        pcol[:, l, 1] = f(inp["dq_norm_g"])[l][idx]
        pcol[:, l, 2] = f(inp["dk_norm_g"])[l][idx]
        pcol[:, l, 3] = f(inp["diff_norm_g"])[l]
    p["pcol"] = pcol
    df = f(inp["ret_decay_fwd"]); db = f(inp["ret_decay_bwd"])
    dech = np.zeros((128, 2, 2, 8), np.float32)
    decp = np.zeros((128, 2, 2, 4), np.float32)
    for l in range(2):
        dech[:, l, 0, :] = df[l][None, :]; dech[:, l, 1, :] = db[l][None, :]
        for cc in range(4):
            decp[:64, l, 0, cc] = df[l][2 * cc]; decp[64:, l, 0, cc] = df[l][2 * cc + 1]
            decp[:64, l, 1, cc] = db[l][2 * cc]; decp[64:, l, 1, cc] = db[l][2 * cc + 1]
    p["dech"] = dech; p["decp"] = decp
    lamv = np.zeros((128, 2, 4, 64), np.float32)
    for l in range(2):
        for k, nm in enumerate(("lambda_q1", "lambda_k1", "lambda_q2", "lambda_k2")):
            lamv[:, l, k, :] = f(inp[nm])[l][None, :]
    p["lamv"] = lamv
    for nm in ("w_in", "w_out", "w_gate", "w_up", "w_down"):
        p[nm] = np.ascontiguousarray(f(inp[nm]))
    return p


_NC_CACHE = {}


def _get_nc(mode):
    if mode not in _NC_CACHE:
        _NC_CACHE[mode] = Builder(mode).build()
    return _NC_CACHE[mode]


def _gather(res, name, group):
    return np.concatenate([np.asarray(res[c][name]) for c in group], axis=0)


FUSED = True


def kernel(**inputs):
    x = np.asarray(inputs["x"], dtype=np.float32)
    p = _params(inputs)
    base = []
    for c in range(NCORE):
        d = dict(p)
        d.update(_consts(c))
        b, r = c // 4, c % 4
        d["x"] = np.ascontiguousarray(x[b, r * TL:(r + 1) * TL, :])
        base.append(d)
    cores = list(range(NCORE))
    if FUSED:
        res = run_bass_kernel_spmd(_get_nc("fused"), base, core_ids=cores).results
    else:
        xk = [f"{a}{{l}}_{hp}" for a in ("kt", "v") for hp in (0, 1)]
        def xfer(prev, l):
            outs = []
            for c in range(NCORE):
                grp = [4 * (c // 4) + k for k in range(4)]
                d = dict(base[c])
                for a in ("kt", "v"):
                    for hp in (0, 1):
                        d[f"{a}_all{l}_{hp}"] = _gather(prev, f"{a}_loc{l}_{hp}", grp)
                d[f"st_all{l}"] = _gather(prev, f"st_loc{l}", grp)
                outs.append(d)
            return outs
        r1 = run_bass_kernel_spmd(_get_nc("L1"), base, core_ids=cores).results
        in2 = xfer(r1, 0)
        r2 = run_bass_kernel_spmd(_get_nc("L2"), in2, core_ids=cores).results
        in3 = xfer(r2, 1)
        for c in range(NCORE):
            in3[c]["x"] = np.asarray(r2[c]["y"])
        res = run_bass_kernel_spmd(_get_nc("L3"), in3, core_ids=cores).results
    out = np.zeros((2, S, D), np.float32)
    for c in range(NCORE):
        b, r = c // 4, c % 4
        out[b, r * TL:(r + 1) * TL, :] = np.asarray(res[c]["y"])
    return out
```
